# Optimizing a Trainium2 kernel written in Bass

```python
import jax, jax.numpy as jnp
from jax import lax
import numpy as np

D_MODEL = 2048
BATCH = 8
SEQ = 2048
DEPTH = 4
DEC_BATCH = 16
DEC_SEQ = 64
PAST_LEN = 1024

CHUNK = 64
Q_BLOCK = 128
N_MIXERS = 2
N_MLA = (DEPTH + 1) // 2
N_SB = DEPTH // 2
MLA_HEADS = 16
Q_LORA = 512
KV_LORA = 512
NOPE_DIM = 128
ROPE_DIM = 64
V_DIM = 128
MLA_WIDTH = MLA_HEADS * V_DIM
MLA_IN = Q_LORA + KV_LORA + ROPE_DIM + MLA_WIDTH
ROPE_THETA = 10000.0
SB_HEADS = 16
SB_HEAD_DIM = 128
SB_WIDTH = SB_HEADS * SB_HEAD_DIM
SB_IN = 4 * SB_WIDTH
EPS = 1e-6
NEG = -1e30

kernel_name = "chunk_streaming_mla_stickbreaking_hybrid"


def rms_norm(x, g):
    xf = x.astype(jnp.float32)
    y = xf * lax.rsqrt(jnp.mean(xf * xf, axis=-1, keepdims=True) + EPS)
    return (y * g.astype(jnp.float32)).astype(x.dtype)


def rope(x, pos):
    half = ROPE_DIM // 2
    inv = 1.0 / (ROPE_THETA ** (jnp.arange(half, dtype=jnp.float32) * (2.0 / ROPE_DIM)))
    ang = pos.astype(jnp.float32)[:, None] * inv[None, :]
    shape = (1, pos.shape[0]) + (1,) * (x.ndim - 3) + (half,)
    cos = jnp.cos(ang).reshape(shape)
    sin = jnp.sin(ang).reshape(shape)
    xf = x.astype(jnp.float32)
    x1, x2 = xf[..., :half], xf[..., half:]
    return jnp.concatenate([x1 * cos - x2 * sin, x1 * sin + x2 * cos], axis=-1).astype(x.dtype)


def sweep_query_blocks(fn, q, q_pos):
    B, T = q.shape[0], q.shape[1]
    if T <= Q_BLOCK:
        return fn(q, q_pos)
    nb = T // Q_BLOCK
    qb = jnp.moveaxis(q.reshape((B, nb, Q_BLOCK) + q.shape[2:]), 1, 0)
    pb = q_pos.reshape(nb, Q_BLOCK)
    out = lax.map(lambda a: fn(a[0], a[1]), (qb, pb))
    return jnp.moveaxis(out, 0, 1).reshape((B, T) + out.shape[3:])


def mla_attend(q, k, v, q_pos, k_pos):
    s = jnp.einsum('bqhd,bkhd->bhqk', q, k).astype(jnp.float32) * ((NOPE_DIM + ROPE_DIM) ** -0.5)
    mask = (k_pos[None, :] // CHUNK) <= (q_pos[:, None] // CHUNK)
    s = jnp.where(mask[None, None], s, NEG)
    p = jax.nn.softmax(s, axis=-1).astype(v.dtype)
    return jnp.einsum('bhqk,bkhd->bqhd', p, v)


def sb_attend(q, k, v, q_pos, k_pos):
    z = jnp.einsum('bqhd,bkhd->bhqk', q, k).astype(jnp.float32) * (SB_HEAD_DIM ** -0.5)
    causal = (k_pos[None, :] < q_pos[:, None])[None, None]
    log_beta = jax.nn.log_sigmoid(z)
    log_fail = jnp.where(causal, jax.nn.log_sigmoid(-z), 0.0)
    later = lax.cumsum(log_fail, axis=3, reverse=True) - log_fail
    a = jnp.where(causal, jnp.exp(log_beta + later), 0.0).astype(v.dtype)
    return jnp.einsum('bhqk,bkhd->bqhd', a, v)


def mla_mixer(h, pos, ckv_past, kr_past, w_in, q_norm, kv_norm, w_q_up, w_kv_up, w_o):
    B, T, _ = h.shape
    proj = h @ w_in
    cq, ckv, kr, gate = jnp.split(proj, [Q_LORA, Q_LORA + KV_LORA, Q_LORA + KV_LORA + ROPE_DIM], axis=-1)
    cq = rms_norm(cq, q_norm)
    ckv = rms_norm(ckv, kv_norm)
    kr = rope(kr, pos)
    q = (cq @ w_q_up).reshape(B, T, MLA_HEADS, NOPE_DIM + ROPE_DIM)
    q = jnp.concatenate([q[..., :NOPE_DIM], rope(q[..., NOPE_DIM:], pos)], axis=-1)
    if ckv_past is None:
        ckv_all, kr_all, k_pos = ckv, kr, pos
    else:
        past_pos = jnp.arange(ckv_past.shape[1], dtype=jnp.int32)
        ckv_all = jnp.concatenate([ckv_past, ckv], axis=1)
        kr_all = jnp.concatenate([kr_past, kr], axis=1)
        k_pos = jnp.concatenate([past_pos, pos])
    Tk = ckv_all.shape[1]
    kv = (ckv_all @ w_kv_up).reshape(B, Tk, MLA_HEADS, NOPE_DIM + V_DIM)
    k = jnp.concatenate([kv[..., :NOPE_DIM],
                         jnp.broadcast_to(kr_all[:, :, None, :], (B, Tk, MLA_HEADS, ROPE_DIM))], axis=-1)
    v = kv[..., NOPE_DIM:]
    o = sweep_query_blocks(lambda qb, pb: mla_attend(qb, k, v, pb, k_pos), q, pos)
    o = o.reshape(B, T, MLA_WIDTH) * jax.nn.silu(gate)
    return o @ w_o, ckv, kr


def sb_mixer(h, pos, k_past, v_past, w_in, w_o):
    B, T, _ = h.shape
    q, k, v, gate = jnp.split(h @ w_in, 4, axis=-1)
    q = q.reshape(B, T, SB_HEADS, SB_HEAD_DIM)
    k = k.reshape(B, T, SB_HEADS, SB_HEAD_DIM)
    v = v.reshape(B, T, SB_HEADS, SB_HEAD_DIM)
    if k_past is None:
        k_all, v_all, k_pos = k, v, pos
    else:
        past_pos = jnp.arange(k_past.shape[1], dtype=jnp.int32)
        k_all = jnp.concatenate([k_past, k], axis=1)
        v_all = jnp.concatenate([v_past, v], axis=1)
        k_pos = jnp.concatenate([past_pos, pos])
    o = sweep_query_blocks(lambda qb, pb: sb_attend(qb, k_all, v_all, pb, k_pos), q, pos)
    o = o.reshape(B, T, SB_WIDTH) * jax.nn.silu(gate)
    return o @ w_o, k, v


def run_trunk(x, pos, past_ckv, past_kr, past_sk, past_sv, ln_gain, final_gain,
              mla_w_in, mla_q_norm, mla_kv_norm, mla_w_q_up, mla_w_kv_up, mla_w_o,
              sb_w_in, sb_w_o):
    new_ckv, new_kr, new_sk, new_sv = [], [], [], []
    for i in range(DEPTH):
        h = rms_norm(x, ln_gain[i])
        j = i // N_MIXERS
        if i % N_MIXERS == 0:
            out, ckv, kr = mla_mixer(h, pos,
                                     None if past_ckv is None else past_ckv[j],
                                     None if past_kr is None else past_kr[j],
                                     mla_w_in[j], mla_q_norm[j], mla_kv_norm[j],
                                     mla_w_q_up[j], mla_w_kv_up[j], mla_w_o[j])
            new_ckv.append(ckv)
            new_kr.append(kr)
        else:
            out, k, v = sb_mixer(h, pos,
                                 None if past_sk is None else past_sk[j],
                                 None if past_sv is None else past_sv[j],
                                 sb_w_in[j], sb_w_o[j])
            new_sk.append(k)
            new_sv.append(v)
        x = x + out
    y = rms_norm(x, final_gain)
    return y, jnp.stack(new_ckv), jnp.stack(new_kr), jnp.stack(new_sk), jnp.stack(new_sv)


def setup_inputs(seed: int = 0) -> dict:
    key = jax.random.key(seed)
    ks = jax.random.split(key, 18)
    f32 = jnp.float32

    def w(k, shape, fan_in):
        return jax.random.normal(k, shape, f32) * (fan_in ** -0.5)

    def gain(k, shape):
        return 1.0 + 0.02 * jax.random.normal(k, shape, f32)

    return {
        "x_prompt": jax.random.normal(ks[0], (BATCH, SEQ, D_MODEL), f32),
        "x_sample": jax.random.normal(ks[1], (DEC_BATCH, DEC_SEQ, D_MODEL), f32),
        "cache_mla_ckv": jax.random.normal(ks[2], (N_MLA, DEC_BATCH, PAST_LEN, KV_LORA), f32),
        "cache_mla_krope": jax.random.normal(ks[3], (N_MLA, DEC_BATCH, PAST_LEN, ROPE_DIM), f32),
        "cache_sb_k": jax.random.normal(ks[4], (N_SB, DEC_BATCH, PAST_LEN, SB_HEADS, SB_HEAD_DIM), f32),
        "cache_sb_v": jax.random.normal(ks[5], (N_SB, DEC_BATCH, PAST_LEN, SB_HEADS, SB_HEAD_DIM), f32),
        "ln_gain": gain(ks[6], (DEPTH, D_MODEL)),
        "final_gain": gain(ks[7], (D_MODEL,)),
        "mla_w_in": w(ks[8], (N_MLA, D_MODEL, MLA_IN), D_MODEL),
        "mla_q_norm": gain(ks[9], (N_MLA, Q_LORA)),
        "mla_kv_norm": gain(ks[10], (N_MLA, KV_LORA)),
        "mla_w_q_up": w(ks[11], (N_MLA, Q_LORA, MLA_HEADS * (NOPE_DIM + ROPE_DIM)), Q_LORA),
        "mla_w_kv_up": w(ks[12], (N_MLA, KV_LORA, MLA_HEADS * (NOPE_DIM + V_DIM)), KV_LORA),
        "mla_w_o": w(ks[13], (N_MLA, MLA_WIDTH, D_MODEL), MLA_WIDTH),
        "sb_w_in": w(ks[14], (N_SB, D_MODEL, SB_IN), D_MODEL),
        "sb_w_o": w(ks[15], (N_SB, SB_WIDTH, D_MODEL), SB_WIDTH),
    }


def reference(x_prompt, x_sample, cache_mla_ckv, cache_mla_krope, cache_sb_k, cache_sb_v,
              ln_gain, final_gain, mla_w_in, mla_q_norm, mla_kv_norm, mla_w_q_up,
              mla_w_kv_up, mla_w_o, sb_w_in, sb_w_o):
    pos_p = jnp.arange(x_prompt.shape[1], dtype=jnp.int32)
    pos_s = cache_mla_ckv.shape[2] + jnp.arange(x_sample.shape[1], dtype=jnp.int32)
    y_prompt, p_ckv, p_kr, p_sk, p_sv = run_trunk(
        x_prompt, pos_p, None, None, None, None, ln_gain, final_gain,
        mla_w_in, mla_q_norm, mla_kv_norm, mla_w_q_up, mla_w_kv_up, mla_w_o, sb_w_in, sb_w_o)
    y_sample, s_ckv, s_kr, s_sk, s_sv = run_trunk(
        x_sample, pos_s, cache_mla_ckv, cache_mla_krope, cache_sb_k, cache_sb_v, ln_gain, final_gain,
        mla_w_in, mla_q_norm, mla_kv_norm, mla_w_q_up, mla_w_kv_up, mla_w_o, sb_w_in, sb_w_o)
    return (y_prompt, y_sample, p_ckv, p_kr, p_sk, p_sv, s_ckv, s_kr, s_sk, s_sv)
```

```python
import numpy as np
from contextlib import ExitStack
import concourse.bass as bass
import concourse.mybir as mybir
from concourse.bass_utils import run_bass_kernel_spmd
import ml_dtypes

F32 = mybir.dt.float32
BF16 = mybir.dt.bfloat16
AF = mybir.ActivationFunctionType
ALU = mybir.AluOpType
AX = mybir.AxisListType
NEG = -1e30
EPS = 1e-6


class Cfg:
    def __init__(self, D=2048, T=2048, TS=64, PAST=1024, H=16, DEPTH=4):
        self.D, self.T, self.TS, self.PAST, self.H, self.DEPTH = D, T, TS, PAST, H, DEPTH
        self.KC = D // 128
        self.NT = T // 128
        self.NTT = self.NT + 1
        self.NTOK = T + 128
        self.HD = H * 128
        self.QL = 512
        self.KVL = 512
        self.MLA_IN = 512 + 512 + 64 + self.HD
        self.NKV = T + 256 + 2 * PAST
        self.NBLK = self.NKV // 128
        self.N_MLA = (DEPTH + 1) // 2
        self.N_SB = DEPTH // 2
        self.kvA, self.kvB = T, T + 128
        self.pastA, self.pastB = T + 256, T + 256 + PAST
        assert TS == 64 and T % 128 == 0 and PAST % 128 == 0


class Buf:
    __slots__ = ("name", "wc", "wd", "rc", "rd", "excl")

    def __init__(self, name="", excl=False):
        self.name = name
        self.excl = excl
        self.wc = {}
        self.wd = []
        self.rc = {}
        self.rd = []

    def clear(self):
        self.wc = {}
        self.wd = []
        self.rc = {}
        self.rd = []


class Op:
    __slots__ = ("eng", "meth", "args", "kw", "deps", "needed", "sem", "val", "dma")

    def __init__(self, eng, meth, args, kw, dma):
        self.eng, self.meth, self.args, self.kw, self.dma = eng, meth, args, kw, dma
        self.deps = []
        self.needed = dma
        self.sem = None
        self.val = 0


class Prog:
    ENGS = ["pe", "act", "dve", "pool", "sp"]

    def __init__(self, nc, es, n_dma_sems=20):
        self.nc = nc
        self.h = {"pe": nc.tensor, "act": nc.scalar, "dve": nc.vector, "pool": nc.gpsimd, "sp": nc.sync}
        self.esem = {e: es.enter_context(nc.semaphore("c_" + e)) for e in self.ENGS}
        self.ecnt = {e: 0 for e in self.ENGS}
        self.dsem = {}
        for q in ("sp", "pool", "act"):
            self.dsem[q] = [[es.enter_context(nc.semaphore(f"d_{q}{i}")), 0, None] for i in range(n_dma_sems)]
        self.dptr = {q: 0 for q in self.dsem}
        self.ops = []
        self.bufs = []
        self.waited = {e: {} for e in self.ENGS}
        self.nblock = 0
        self.max_blocks = None

    def buf(self, name="", excl=False):
        b = Buf(name, excl)
        self.bufs.append(b)
        return b

    def _deps(self, eng, dma, reads, writes):
        deps = []
        for b in reads:
            for d in b.wc.values():
                if not (d.eng == eng == "pe"):
                    deps.append(d)
            deps.extend(b.wd)
            if b.excl:
                for d in b.rc.values():
                    if d.eng != eng:
                        deps.append(d)
        for b in writes:
            for d in b.wc.values():
                if dma or d.eng != eng:
                    deps.append(d)
            deps.extend(b.wd)
            for d in b.rc.values():
                if dma or d.eng != eng:
                    deps.append(d)
            deps.extend(b.rd)
        out, seen = [], set()
        for d in deps:
            if id(d) not in seen:
                seen.add(id(d))
                d.needed = True
                out.append(d)
        return out

    def _register(self, ops, eng, dma, reads, writes):
        for b in writes:
            b.clear()
            for o in ops:
                if dma:
                    b.wd.append(o)
                else:
                    b.wc[eng] = o
        for b in reads:
            if b in writes:
                continue
            for o in ops:
                if dma:
                    b.rd.append(o)
                else:
                    b.rc[eng] = o

    def _emit(self, eng, meth, args, kw, reads, writes, dma):
        o = Op(eng, meth, args, kw, dma)
        o.deps = self._deps(eng, dma, reads, writes)
        self._register([o], eng, dma, reads, writes)
        self.ops.append(o)
        return o

    def dma_group(self, q, pairs, reads=(), writes=()):
        deps = self._deps(q, True, reads, writes)
        ops = []
        for (out, in_) in pairs:
            o = Op(q, "dma_start", (), dict(out=out, in_=in_), True)
            o.deps = list(deps)
            ops.append(o)
            self.ops.append(o)
        self._register(ops, q, True, reads, writes)
        return ops

    def op(self, eng, meth, *args, reads=(), writes=(), **kw):
        return self._emit(eng, meth, args, kw, reads, writes, False)

    def dma(self, q, out, in_, reads=(), writes=(), **kw):
        kw = dict(kw)
        kw["out"] = out
        kw["in_"] = in_
        return self._emit(q, "dma_start", (), kw, reads, writes, True)

    def flush(self):
        nc = self.nc
        if self.max_blocks is not None and self.nblock >= self.max_blocks:
            self.ops = []
            for b in self.bufs:
                b.clear()
            return
        for o in self.ops:
            if o.dma:
                slot = self.dsem[o.eng][self.dptr[o.eng] % len(self.dsem[o.eng])]
                self.dptr[o.eng] += 1
                if slot[2] is not None:
                    o.deps.append(slot[2])
                slot[1] += 16
                o.sem, o.val = slot[0], slot[1]
                slot[2] = o
            elif o.needed:
                self.ecnt[o.eng] += 1
                o.sem, o.val = self.esem[o.eng], self.ecnt[o.eng]
        per = {e: [o for o in self.ops if o.eng == e] for e in self.ENGS}
        dmas = [o for o in self.ops if o.dma]
        self.nblock += 1
        with nc.Block() as block:
            for e in self.ENGS:
                ops_e = per[e]
                if not ops_e and not (e == "sp" and dmas):
                    continue

                def body(h, e=e, ops_e=ops_e):
                    waited = self.waited[e]
                    for o in ops_e:
                        for d in o.deps:
                            k = d.sem.num
                            if waited.get(k, 0) >= d.val:
                                continue
                            h.wait_ge(d.sem, d.val)
                            waited[k] = d.val
                        ins = getattr(h, o.meth)(*o.args, **o.kw)
                        if o.sem is not None:
                            ins.then_inc(o.sem, 16 if o.dma else 1)
                    if e == "sp":
                        for d in dmas:
                            k = d.sem.num
                            if waited.get(k, 0) >= d.val:
                                continue
                            h.wait_ge(d.sem, d.val)
                            waited[k] = d.val

                getattr(block, {"pe": "tensor", "act": "scalar", "dve": "vector", "pool": "gpsimd", "sp": "sync"}[e])(body)
        self.ops = []
        for b in self.bufs:
            b.clear()
        for q in self.dsem:
            for slot in self.dsem[q]:
                slot[2] = None


class Ring:
    def __init__(self, items):
        self.items = items
        self.i = 0

    def next(self):
        it = self.items[self.i % len(self.items)]
        self.i += 1
        return it


def build(cfg, dbg_layers=None, max_blocks=None):
    c = cfg
    D, T, H, KC, NT, NTT, NTOK, HD, NKV, NBLK, PAST = c.D, c.T, c.H, c.KC, c.NT, c.NTT, c.NTOK, c.HD, c.NKV, c.NBLK, c.PAST
    DEPTH = c.DEPTH if dbg_layers is None else dbg_layers
    nc = bass.Bass("TRN2", target_bir_lowering=False)

    def din(name, shape, dt=F32):
        return nc.dram_tensor(name, list(shape), dt, kind="ExternalInput").ap()

    def dout(name, shape, dt=F32):
        return nc.dram_tensor(name, list(shape), dt, kind="ExternalOutput").ap()

    def dscr(name, shape, dt):
        return nc.dram_tensor(name, list(shape), dt).ap()

    x_in = din("x_in", [NTOK, D])
    c_ckv = din("c_ckv", [c.N_MLA, 2, PAST, 512])
    c_kr = din("c_kr", [c.N_MLA, 2, PAST, 64])
    c_sbk = din("c_sbk", [c.N_SB, 2, PAST, HD])
    c_sbv = din("c_sbv", [c.N_SB, 2, PAST, HD])
    ln_gain = din("ln_gain", [c.DEPTH, D])
    final_gain = din("final_gain", [1, D])
    mla_w_in = din("mla_w_in", [c.N_MLA, D, c.MLA_IN])
    mla_q_norm = din("mla_q_norm", [c.N_MLA, 512])
    mla_kv_norm = din("mla_kv_norm", [c.N_MLA, 512])
    mla_w_q_up = din("mla_w_q_up", [c.N_MLA, 512, H * 192])
    mla_w_kv_up = din("mla_w_kv_up", [c.N_MLA, 512, H * 256])
    mla_w_o = din("mla_w_o", [c.N_MLA, HD, D])
    sb_w_in = din("sb_w_in", [c.N_SB, D, 4 * HD])
    sb_w_o = din("sb_w_o", [c.N_SB, HD, D])
    k_ident = din("k_ident", [128, 128], BF16)
    k_tri = din("k_tri", [128, 128])
    k_cos_tm = din("k_cos_tm", [NTOK, 32])
    k_sin_tm = din("k_sin_tm", [NTOK, 32])
    k_cosT = din("k_cosT", [64, NTOK])
    k_sinT = din("k_sinT", [64, NTOK])

    y_o = dout("y", [NTOK, D])
    ckv_o = dout("ckv_o", [c.N_MLA, NTOK, 512])
    kr_o = dout("kr_o", [c.N_MLA, NTOK, 64])
    sbk_o = dout("sbk_o", [c.N_SB, NTOK, HD])
    sbv_o = dout("sbv_o", [c.N_SB, NTOK, HD])

    x_s = [None] + [dscr(f"x_s{l}", [NTOK, D], F32) for l in range(1, c.DEPTH)]
    qT_s = [dscr(f"qT_s{l}", [H, 192, NTOK], BF16) for l in range(c.DEPTH)]
    kT_s = [dscr(f"kT_s{l}", [H, 128, NKV], BF16) for l in range(c.DEPTH)]
    krT_s = [dscr(f"krT_s{l}", [64, NKV], BF16) for l in range(c.DEPTH)]
    v_s = [dscr(f"v_s{l}", [H, 128, NBLK, 128], BF16) for l in range(c.DEPTH)]
    sg_s = [dscr(f"sg_s{l}", [H, NTOK, 128], BF16) for l in range(c.DEPTH)]
    cqnT_s = [dscr(f"cqnT_s{l}", [128, 4, NTOK], BF16) for l in range(c.DEPTH)]
    ckvT_s = [dscr(f"ckvT_s{l}", [128, 4, NKV], BF16) for l in range(c.DEPTH)]

    es_top = ExitStack()
    with es_top:
        P = Prog(nc, es_top)
        P.max_blocks = max_blocks

        uid = [0]

        def sb(es, name, shape, dt):
            uid[0] += 1
            return es.enter_context(nc.sbuf_tensor(f"{name}_{uid[0]}", list(shape), dt))

        def ps(es, name, shape, dt=F32):
            uid[0] += 1
            return es.enter_context(nc.psum_tensor(f"{name}_{uid[0]}", list(shape), dt))

        ident = sb(es_top, "ident", [128, 128], BF16)
        tri = sb(es_top, "tri", [128, 128], F32)
        trib = sb(es_top, "trib", [128, 128], BF16)
        b_const = P.buf("const")
        P.dma("sp", ident[:], k_ident, writes=[b_const])
        P.dma("sp", tri[:], k_tri, writes=[b_const])
        P.op("dve", "tensor_copy", trib[:], tri[:], reads=[b_const], writes=[b_const])
        P.flush()

        def kv_cols(ti):
            if ti < NT:
                return [(0, 128, ti * 128)]
            return [(0, 64, c.kvA), (64, 64, c.kvB)]

        def rms_rstd(ss, rstd, n, bufs, rows=128):
            P.op("act", "activation", out=rstd[:rows], in_=ss[:rows], func=AF.Ln, bias=EPS, scale=1.0 / n,
                 reads=bufs, writes=bufs)
            P.op("act", "activation", out=rstd[:rows], in_=rstd[:rows], func=AF.Exp, scale=-0.5,
                 reads=bufs, writes=bufs)

        def phase_norm(es, l, hT, b_hT, pbank):
            x_src = x_in if l == 0 else x_s[l]
            gbc = sb(es, "gbc", [128, D], F32)
            b_g = P.buf("gbc")
            P.dma("sp", gbc[:], ln_gain[l:l + 1, :].partition_broadcast(128), writes=[b_g])
            xt = [sb(es, f"xt{i}", [128, D], F32) for i in range(2)]
            b_xt = [P.buf(f"xt{i}") for i in range(2)]
            xs = [sb(es, f"xs{i}", [128, D], BF16) for i in range(2)]
            b_xs = [P.buf(f"xs{i}") for i in range(2)]
            junk = sb(es, "junk", [128, D], BF16)
            b_junk = P.buf("junk")
            st = [sb(es, f"nst{i}", [128, 2], F32) for i in range(2)]
            b_st = [P.buf(f"nst{i}") for i in range(2)]
            for ti in range(NTT):
                i = ti % 2
                P.dma("sp", xt[i][:], x_src[ti * 128:(ti + 1) * 128, :], writes=[b_xt[i]])
                P.op("act", "activation", out=junk[:], in_=xt[i][:], func=AF.Square, accum_out=st[i][:, 0:1],
                     reads=[b_xt[i]], writes=[b_junk, b_st[i]])
                rms_rstd(st[i][:, 0:1], st[i][:, 1:2], D, [b_st[i]])
                P.op("dve", "scalar_tensor_tensor", out=xs[i][:], in0=xt[i][:], scalar=st[i][:, 1:2], in1=gbc[:],
                     op0=ALU.mult, op1=ALU.mult, reads=[b_xt[i], b_st[i], b_g], writes=[b_xs[i]])
                for g in range(0, KC, 8):
                    pt, b_pt = pbank.next()
                    ptb = pt[:].bitcast(BF16)
                    ng = min(8, KC - g)
                    for k in range(ng):
                        P.op("pe", "transpose", ptb[:, k * 128:(k + 1) * 128], xs[i][:, (g + k) * 128:(g + k + 1) * 128],
                             ident[:], reads=[b_xs[i]], writes=[b_pt])
                    eng = "act" if (g // 8) % 2 == 0 else "dve"
                    src = ptb[:, :ng * 128].rearrange("p (k t) -> p k t", t=128)
                    if eng == "act":
                        P.op("act", "copy", hT[:, g:g + ng, ti * 128:(ti + 1) * 128], src, reads=[b_pt], writes=[b_hT[ti]])
                    else:
                        P.op("dve", "tensor_copy", hT[:, g:g + ng, ti * 128:(ti + 1) * 128], src, reads=[b_pt], writes=[b_hT[ti]])

        def proj_tm(lhsT, b_l, nk, ti, wt, b_w, n, pbank):
            pt, b_pt = pbank.next()
            for k in range(nk):
                P.op("pe", "matmul", pt[:, :n], lhsT=lhsT[:, k, ti * 128:(ti + 1) * 128], rhs=wt[:, k, :n],
                     start=(k == 0), stop=(k == nk - 1), reads=[b_l] + list(b_w), writes=[b_pt])
            return pt, b_pt

        def load_w(wslots, src, nk, n):
            wt, b_w = wslots.next()
            kk = max(1, nk // 4)
            for k0 in range(0, nk, kk):
                P.dma("pool", wt[:, k0:k0 + kk, :n], src[k0 * 128:(k0 + kk) * 128, :].rearrange("(k p) n -> p k n", p=128),
                      writes=[b_w[k0 // kk]])
            return wt, b_w

        def phase_proj_sb(es, l, j, hT, b_hT, pbank):
            W = sb_w_in[j]
            wsl = []
            for i in range(2):
                wsl.append((sb(es, f"w{i}", [128, KC, 512], BF16), [P.buf(f"w{i}_{q}") for q in range(4)]))
            wslots = Ring(wsl)
            sf = Ring([(sb(es, f"sf{i}", [128, 512], F32), P.buf(f"sf{i}")) for i in range(3)])
            sh = Ring([(sb(es, f"sh{i}", [128, 512], BF16), P.buf(f"sh{i}")) for i in range(3)])
            skt = Ring([(sb(es, f"skt{i}", [128, 4, 128], BF16), P.buf(f"skt{i}")) for i in range(2)])
            sq = Ring([(sb(es, f"sq{i}", [128, 512], BF16), P.buf(f"sq{i}")) for i in range(3)])
            b_hall = b_hT
            for nb in range(HD // 512):
                wt, b_w = load_w(wslots, W[:, nb * 512:(nb + 1) * 512], KC, 512)
                for hh in range(4):
                    h = nb * 4 + hh
                    for t0 in range(0, NTOK, 512):
                        tn = min(512, NTOK - t0)
                        pt, b_pt = pbank.next()
                        for k in range(KC):
                            P.op("pe", "matmul", pt[:, :tn], lhsT=wt[:, k, hh * 128:(hh + 1) * 128], rhs=hT[:, k, t0:t0 + tn],
                                 start=(k == 0), stop=(k == KC - 1), reads=b_hall + b_w, writes=[b_pt])
                        s, b_s = sq.next()
                        P.op("act", "copy", s[:, :tn], pt[:, :tn], reads=[b_pt], writes=[b_s])
                        P.dma("sp", qT_s[l][h, 0:128, t0:t0 + tn], s[:, :tn], reads=[b_s])
            import os as _os
            _stop = int(_os.environ.get("SBSTOP", "9"))
            if _stop < 1:
                return
            for kind in ("k", "v", "g")[:max(0, _stop - 1)]:
                base = {"k": HD, "v": 2 * HD, "g": 3 * HD}[kind]
                for nb in range(HD // 512):
                    wt, b_w = load_w(wslots, W[:, base + nb * 512: base + (nb + 1) * 512], KC, 512)
                    h0 = nb * 4
                    for ti in range(NTT):
                        pt, b_pt = proj_tm(hT, b_hT[ti], KC, ti, wt, b_w, 512, pbank)
                        if kind == "g":
                            s, b_s = sh.next()
                            P.op("act", "activation", out=s[:], in_=pt[:], func=AF.Silu, reads=[b_pt], writes=[b_s])
                            P.dma("sp", sg_s[l][h0:h0 + 4, ti * 128:(ti + 1) * 128, :].rearrange("h t d -> t h d"),
                                  s[:].rearrange("t (h d) -> t h d", d=128), reads=[b_s])
                            continue
                        f, b_f = sf.next()
                        P.op("act", "copy", f[:], pt[:], reads=[b_pt], writes=[b_f])
                        dst = sbk_o if kind == "k" else sbv_o
                        P.dma("sp", dst[j, ti * 128:(ti + 1) * 128, nb * 512:(nb + 1) * 512], f[:], reads=[b_f])
                        s, b_s = sh.next()
                        P.op("dve", "tensor_copy", s[:], f[:], reads=[b_f], writes=[b_s])
                        if kind == "v":
                            for (r0, nr, kc0) in kv_cols(ti):
                                P.dma("sp", v_s[l][h0:h0 + 4, 0:nr, kc0 // 128, :].rearrange("h p d -> p h d"),
                                      s[r0:r0 + nr, :].rearrange("t (h d) -> t h d", d=128), reads=[b_s])
                        else:
                            tp, b_tp = pbank.next()
                            tpb = tp[:].bitcast(BF16)
                            for hh in range(4):
                                P.op("pe", "transpose", tpb[:, hh * 128:(hh + 1) * 128], s[:, hh * 128:(hh + 1) * 128], ident[:],
                                     reads=[b_s], writes=[b_tp])
                            kt, b_kt = skt.next()
                            import os as _os
                            _v = _os.environ.get("KVAR", "0")
                            if _v == "1":
                                P.op("act", "copy", kt[:], tpb[:, :512].rearrange("p (h t) -> p h t", t=128),
                                     reads=[b_tp], writes=[b_kt])
                            else:
                                P.op("dve", "tensor_copy", kt[:], tpb[:, :512].rearrange("p (h t) -> p h t", t=128),
                                     reads=[b_tp], writes=[b_kt])
                            for (r0, nr, kc0) in kv_cols(ti):
                                if _v == "2":
                                    for hh in range(4):
                                        P.dma("sp", kT_s[l][h0 + hh, :, kc0:kc0 + nr], kt[:, hh, r0:r0 + nr], reads=[b_kt])
                                elif _v == "3":
                                    pass
                                else:
                                    P.dma("sp", kT_s[l][h0:h0 + 4, :, kc0:kc0 + nr].rearrange("h p t -> p h t"),
                                          kt[:, :, r0:r0 + nr], reads=[b_kt])
            if _stop < 5:
                return
            pk = Ring([(sb(es, f"pk{i}", [128, HD], BF16), P.buf(f"pk{i}")) for i in range(2)])
            pkt = Ring([(sb(es, f"pkt{i}", [128, H, 128], BF16), P.buf(f"pkt{i}")) for i in range(2)])
            for sq_i, kv0 in ((0, c.pastA), (1, c.pastB)):
                for tb in range(PAST // 128):
                    kvc = kv0 + tb * 128
                    vt, b_vt = pk.next()
                    P.dma("pool", vt[:], c_sbv[j, sq_i, tb * 128:(tb + 1) * 128, :], writes=[b_vt])
                    P.dma("sp", v_s[l][:, :, kvc // 128, :].rearrange("h p d -> p h d"),
                          vt[:].rearrange("t (h d) -> t h d", d=128), reads=[b_vt])
                    ktm, b_ktm = pk.next()
                    P.dma("pool", ktm[:], c_sbk[j, sq_i, tb * 128:(tb + 1) * 128, :], writes=[b_ktm])
                    kt, b_kt = pkt.next()
                    for g in range(0, H, 8):
                        ng = min(8, H - g)
                        tp, b_tp = pbank.next()
                        tpb = tp[:].bitcast(BF16)
                        for hh in range(ng):
                            P.op("pe", "transpose", tpb[:, hh * 128:(hh + 1) * 128], ktm[:, (g + hh) * 128:(g + hh + 1) * 128],
                                 ident[:], reads=[b_ktm], writes=[b_tp])
                        P.op("act" if (g // 8) % 2 == 0 else "dve", "copy" if (g // 8) % 2 == 0 else "tensor_copy",
                             kt[:, g:g + ng, :], tpb[:, :ng * 128].rearrange("p (h t) -> p h t", t=128),
                             reads=[b_tp], writes=[b_kt])
                    P.dma("sp", kT_s[l][:, :, kvc:kvc + 128].rearrange("h p t -> p h t"), kt[:], reads=[b_kt])

        def phase_proj_mla_a(es, l, j, hT, b_hT, pbank):
            W = mla_w_in[j]
            wsl = []
            for i in range(2):
                wsl.append((sb(es, f"w{i}", [128, KC, 512], BF16), [P.buf(f"w{i}_{q}") for q in range(4)]))
            wslots = Ring(wsl)
            sf = Ring([(sb(es, f"sf{i}", [128, 512], F32), P.buf(f"sf{i}")) for i in range(3)])
            sh = Ring([(sb(es, f"sh{i}", [128, 512], BF16), P.buf(f"sh{i}")) for i in range(3)])
            skt = Ring([(sb(es, f"skt{i}", [128, 4, 128], BF16), P.buf(f"skt{i}")) for i in range(2)])
            stt = Ring([(sb(es, f"st{i}", [128, 2], F32), P.buf(f"st{i}")) for i in range(3)])
            junk = sb(es, "junk2", [128, 512], BF16)
            b_junk = P.buf("junk2")
            gq = sb(es, "gq", [128, 512], F32)
            gkv = sb(es, "gkv", [128, 512], F32)
            b_gq = P.buf("gq")
            P.dma("sp", gq[:], mla_q_norm[j:j + 1, :].partition_broadcast(128), writes=[b_gq])
            P.dma("sp", gkv[:], mla_kv_norm[j:j + 1, :].partition_broadcast(128), writes=[b_gq])
            cosm = sb(es, "cosm", [128, NTT, 32], F32)
            sinm = sb(es, "sinm", [128, NTT, 32], F32)
            b_cs = P.buf("cossin")
            P.dma("sp", cosm[:], k_cos_tm.rearrange("(b p) i -> p b i", p=128), writes=[b_cs])
            P.dma("sp", sinm[:], k_sin_tm.rearrange("(b p) i -> p b i", p=128), writes=[b_cs])
            for kind in ("cq", "ckv"):
                c0 = 0 if kind == "cq" else 512
                gain = gq if kind == "cq" else gkv
                wt, b_w = load_w(wslots, W[:, c0:c0 + 512], KC, 512)
                for ti in range(NTT):
                    pt, b_pt = proj_tm(hT, b_hT[ti], KC, ti, wt, b_w, 512, pbank)
                    st, b_st = stt.next()
                    P.op("act", "activation", out=junk[:], in_=pt[:], func=AF.Square, accum_out=st[:, 0:1],
                         reads=[b_pt], writes=[b_junk, b_st])
                    rms_rstd(st[:, 0:1], st[:, 1:2], 512, [b_st])
                    s, b_s = sh.next()
                    if kind == "ckv":
                        f, b_f = sf.next()
                        P.op("dve", "scalar_tensor_tensor", out=f[:], in0=pt[:], scalar=st[:, 1:2], in1=gain[:],
                             op0=ALU.mult, op1=ALU.mult, reads=[b_pt, b_st, b_gq], writes=[b_f])
                        P.dma("sp", ckv_o[j, ti * 128:(ti + 1) * 128, :], f[:], reads=[b_f])
                        P.op("dve", "tensor_copy", s[:], f[:], reads=[b_f], writes=[b_s])
                    else:
                        P.op("dve", "scalar_tensor_tensor", out=s[:], in0=pt[:], scalar=st[:, 1:2], in1=gain[:],
                             op0=ALU.mult, op1=ALU.mult, reads=[b_pt, b_st, b_gq], writes=[b_s])
                    tp, b_tp = pbank.next()
                    tpb = tp[:].bitcast(BF16)
                    for k in range(4):
                        P.op("pe", "transpose", tpb[:, k * 128:(k + 1) * 128], s[:, k * 128:(k + 1) * 128], ident[:],
                             reads=[b_s], writes=[b_tp])
                    kt, b_kt = skt.next()
                    P.op("act", "copy", kt[:], tpb[:, :512].rearrange("p (k t) -> p k t", t=128), reads=[b_tp], writes=[b_kt])
                    if kind == "cq":
                        P.dma("sp", cqnT_s[l][:, :, ti * 128:(ti + 1) * 128], kt[:], reads=[b_kt])
                    else:
                        for (r0, nr, kc0) in kv_cols(ti):
                            P.dma("sp", ckvT_s[l][:, :, kc0:kc0 + nr], kt[:, :, r0:r0 + nr], reads=[b_kt])
            wkr = sb(es, "wkr", [128, KC, 64], BF16)
            b_wkr = [P.buf("wkr")]
            P.dma("pool", wkr[:], W[:, 1024:1088].rearrange("(k p) n -> p k n", p=128), writes=b_wkr)
            krf = Ring([(sb(es, f"krf{i}", [128, 64], F32), P.buf(f"krf{i}")) for i in range(2)])
            krt = Ring([(sb(es, f"krt{i}", [128, 64], F32), P.buf(f"krt{i}")) for i in range(2)])
            krb = Ring([(sb(es, f"krb{i}", [128, 64], BF16), P.buf(f"krb{i}")) for i in range(2)])
            krT = Ring([(sb(es, f"krT{i}", [64, 128], BF16), P.buf(f"krT{i}")) for i in range(2)])
            for ti in range(NTT):
                pt, b_pt = proj_tm(hT, b_hT[ti], KC, ti, wkr, b_wkr, 64, pbank)
                f, b_f = krf.next()
                t_, b_t = krt.next()
                cs, sn = cosm[:, ti, :], sinm[:, ti, :]
                P.op("dve", "tensor_tensor", t_[:, 0:32], pt[:, 32:64], sn, ALU.mult, reads=[b_pt, b_cs], writes=[b_t])
                P.op("dve", "tensor_tensor", t_[:, 32:64], pt[:, 0:32], sn, ALU.mult, reads=[b_pt, b_cs], writes=[b_t])
                P.op("dve", "tensor_tensor", f[:, 0:32], pt[:, 0:32], cs, ALU.mult, reads=[b_pt, b_cs], writes=[b_f])
                P.op("dve", "tensor_tensor", f[:, 32:64], pt[:, 32:64], cs, ALU.mult, reads=[b_pt, b_cs], writes=[b_f])
                P.op("dve", "tensor_tensor", f[:, 0:32], f[:, 0:32], t_[:, 0:32], ALU.subtract, reads=[b_f, b_t], writes=[b_f])
                P.op("dve", "tensor_tensor", f[:, 32:64], f[:, 32:64], t_[:, 32:64], ALU.add, reads=[b_f, b_t], writes=[b_f])
                P.dma("sp", kr_o[j, ti * 128:(ti + 1) * 128, :], f[:], reads=[b_f])
                s, b_s = krb.next()
                P.op("dve", "tensor_copy", s[:], f[:], reads=[b_f], writes=[b_s])
                tp, b_tp = pbank.next()
                tpb = tp[:].bitcast(BF16)
                P.op("pe", "transpose", tpb[:64, 0:128], s[:, :], ident[:], reads=[b_s], writes=[b_tp])
                kt, b_kt = krT.next()
                P.op("act", "copy", kt[:], tpb[:64, 0:128], reads=[b_tp], writes=[b_kt])
                for (r0, nr, kc0) in kv_cols(ti):
                    P.dma("sp", krT_s[l][:, kc0:kc0 + nr], kt[:, r0:r0 + nr], reads=[b_kt])
            for nb in range(HD // 512):
                wt, b_w = load_w(wslots, W[:, 1088 + nb * 512:1088 + (nb + 1) * 512], KC, 512)
                h0 = nb * 4
                for ti in range(NTT):
                    pt, b_pt = proj_tm(hT, b_hT[ti], KC, ti, wt, b_w, 512, pbank)
                    s, b_s = sh.next()
                    P.op("act", "activation", out=s[:], in_=pt[:], func=AF.Silu, reads=[b_pt], writes=[b_s])
                    P.dma("sp", sg_s[l][h0:h0 + 4, ti * 128:(ti + 1) * 128, :].rearrange("h t d -> t h d"),
                          s[:].rearrange("t (h d) -> t h d", d=128), reads=[b_s])

        def phase_proj_mla_b(es, l, j, pbank):
            cqnT = sb(es, "cqnT", [128, 4, NTOK], BF16)
            ckvT = sb(es, "ckvT", [128, 4, NKV], BF16)
            b_cq = P.buf("cqnT")
            b_ck = P.buf("ckvT_new")
            P.op("pool", "memset", ckvT[:, :, T:T + 256], 0.0, writes=[b_ck])
            P.dma("sp", cqnT[:], cqnT_s[l], writes=[b_cq])
            P.dma_group("sp", [(ckvT[:, :, 0:T], ckvT_s[l][:, :, 0:T]),
                               (ckvT[:, :, c.kvA:c.kvA + 64], ckvT_s[l][:, :, c.kvA:c.kvA + 64]),
                               (ckvT[:, :, c.kvB:c.kvB + 64], ckvT_s[l][:, :, c.kvB:c.kvB + 64])], writes=[b_ck])
            cosT = sb(es, "cosT", [64, NTOK], F32)
            sinT = sb(es, "sinT", [64, NTOK], F32)
            b_cs = P.buf("cossinT")
            P.dma("sp", cosT[:], k_cosT, writes=[b_cs])
            P.dma("sp", sinT[:], k_sinT, writes=[b_cs])
            pk = Ring([(sb(es, f"pc{i}", [128, 512], BF16), P.buf(f"pc{i}")) for i in range(2)])
            pkr = Ring([(sb(es, f"pr{i}", [128, 64], BF16), P.buf(f"pr{i}")) for i in range(2)])
            pkrT = Ring([(sb(es, f"prT{i}", [64, 128], BF16), P.buf(f"prT{i}")) for i in range(2)])
            b_past = []
            for sq_i, kv0 in ((0, c.pastA), (1, c.pastB)):
                for tb in range(PAST // 128):
                    kvc = kv0 + tb * 128
                    ct, b_ct = pk.next()
                    P.dma("pool", ct[:], c_ckv[j, sq_i, tb * 128:(tb + 1) * 128, :], writes=[b_ct])
                    tp, b_tp = pbank.next()
                    tpb = tp[:].bitcast(BF16)
                    for k in range(4):
                        P.op("pe", "transpose", tpb[:, k * 128:(k + 1) * 128], ct[:, k * 128:(k + 1) * 128], ident[:],
                             reads=[b_ct], writes=[b_tp])
                    bb = P.buf("ckvT_past")
                    b_past.append(bb)
                    P.op("act", "copy", ckvT[:, :, kvc:kvc + 128], tpb[:, :512].rearrange("p (k t) -> p k t", t=128),
                         reads=[b_tp], writes=[bb])
                    rt, b_rt = pkr.next()
                    P.dma("pool", rt[:], c_kr[j, sq_i, tb * 128:(tb + 1) * 128, :], writes=[b_rt])
                    tp, b_tp = pbank.next()
                    tpb = tp[:].bitcast(BF16)
                    P.op("pe", "transpose", tpb[:64, 0:128], rt[:, :], ident[:], reads=[b_rt], writes=[b_tp])
                    rT, b_rT = pkrT.next()
                    P.op("dve", "tensor_copy", rT[:], tpb[:64, 0:128], reads=[b_tp], writes=[b_rT])
                    P.dma("sp", krT_s[l][:, kvc:kvc + 128], rT[:], reads=[b_rT])
            b_ckall = [b_ck] + b_past
            Wq = mla_w_q_up[j].rearrange("(k p) (h e) -> p k h e", p=128, e=192)
            Wkv = mla_w_kv_up[j].rearrange("(k p) (h e) -> p k h e", p=128, e=256)
            HG = 4
            wqn = Ring([(sb(es, f"wqn{i}", [128, 4, HG, 128], BF16), [P.buf(f"wqn{i}")]) for i in range(2)])
            wqr = Ring([(sb(es, f"wqr{i}", [128, 4, HG, 64], BF16), [P.buf(f"wqr{i}")]) for i in range(2)])
            wqs = Ring([(sb(es, f"wqs{i}", [128, 4, HG, 64], BF16), [P.buf(f"wqs{i}")]) for i in range(2)])
            wkn = Ring([(sb(es, f"wkn{i}", [128, 4, HG, 128], BF16), [P.buf(f"wkn{i}")]) for i in range(2)])
            wv = Ring([(sb(es, f"wv{i}", [128, 4, HG, 128], BF16), [P.buf(f"wv{i}")]) for i in range(2)])
            sq = Ring([(sb(es, f"sq{i}", [128, 512], BF16), P.buf(f"sq{i}")) for i in range(3)])
            r1 = Ring([(sb(es, f"r1{i}", [64, 512], F32), P.buf(f"r1{i}")) for i in range(2)])
            r2 = Ring([(sb(es, f"r2{i}", [64, 512], F32), P.buf(f"r2{i}")) for i in range(2)])
            sr = Ring([(sb(es, f"sr{i}", [64, 512], BF16), P.buf(f"sr{i}")) for i in range(2)])
            for hg in range(H // HG):
                hs = slice(hg * HG, (hg + 1) * HG)
                a_n, b_n = wqn.next()
                a_r, b_r = wqr.next()
                a_s, b_s_ = wqs.next()
                a_k, b_k = wkn.next()
                a_v, b_v = wv.next()
                P.dma_group("pool", [(a_n[:, k], Wq[:, k, hs, 0:128]) for k in range(4)], writes=b_n)
                P.dma_group("pool", [(a_r[:, k], Wq[:, k, hs, 128:192]) for k in range(4)], writes=b_r)
                P.dma_group("pool", [(a_s[:, k, :, 0:32], Wq[:, k, hs, 160:192]) for k in range(4)]
                            + [(a_s[:, k, :, 32:64], Wq[:, k, hs, 128:160]) for k in range(4)], writes=b_s_)
                P.dma_group("pool", [(a_k[:, k], Wkv[:, k, hs, 0:128]) for k in range(4)], writes=b_k)
                P.dma_group("pool", [(a_v[:, k], Wkv[:, k, hs, 128:256]) for k in range(4)], writes=b_v)
                for hh in range(HG):
                    h = hg * HG + hh
                    for t0 in range(0, NTOK, 512):
                        tn = min(512, NTOK - t0)
                        pt, b_pt = pbank.next()
                        for k in range(4):
                            P.op("pe", "matmul", pt[:, :tn], lhsT=a_n[:, k, hh, :], rhs=cqnT[:, k, t0:t0 + tn],
                                 start=(k == 0), stop=(k == 3), reads=[b_cq] + b_n, writes=[b_pt])
                        s, b_s = sq.next()
                        P.op("act", "copy", s[:, :tn], pt[:, :tn], reads=[b_pt], writes=[b_s])
                        P.dma("sp", qT_s[l][h, 0:128, t0:t0 + tn], s[:, :tn], reads=[b_s])
                        p1, b_p1 = pbank.next()
                        p2, b_p2 = pbank.next()
                        for k in range(4):
                            P.op("pe", "matmul", p1[:64, :tn], lhsT=a_r[:, k, hh, :], rhs=cqnT[:, k, t0:t0 + tn],
                                 start=(k == 0), stop=(k == 3), reads=[b_cq] + b_r, writes=[b_p1])
                        for k in range(4):
                            P.op("pe", "matmul", p2[:64, :tn], lhsT=a_s[:, k, hh, :], rhs=cqnT[:, k, t0:t0 + tn],
                                 start=(k == 0), stop=(k == 3), reads=[b_cq] + b_s_, writes=[b_p2])
                        t1, b_t1 = r1.next()
                        t2, b_t2 = r2.next()
                        P.op("dve", "tensor_tensor", t1[:, :tn], p1[:64, :tn], cosT[:, t0:t0 + tn], ALU.mult,
                             reads=[b_p1, b_cs], writes=[b_t1])
                        P.op("dve", "tensor_tensor", t2[:, :tn], p2[:64, :tn], sinT[:, t0:t0 + tn], ALU.mult,
                             reads=[b_p2, b_cs], writes=[b_t2])
                        o_, b_o = sr.next()
                        P.op("pool", "tensor_tensor", o_[:, :tn], t1[:, :tn], t2[:, :tn], ALU.add,
                             reads=[b_t1, b_t2], writes=[b_o])
                        P.dma("sp", qT_s[l][h, 128:192, t0:t0 + tn], o_[:, :tn], reads=[b_o])
                    for t0 in range(0, NKV, 512):
                        tn = min(512, NKV - t0)
                        pt, b_pt = pbank.next()
                        for k in range(4):
                            P.op("pe", "matmul", pt[:, :tn], lhsT=a_k[:, k, hh, :], rhs=ckvT[:, k, t0:t0 + tn],
                                 start=(k == 0), stop=(k == 3), reads=b_ckall + b_k, writes=[b_pt])
                        s, b_s = sq.next()
                        P.op("dve" if (t0 // 512) % 2 else "act", "tensor_copy" if (t0 // 512) % 2 else "copy",
                             s[:, :tn], pt[:, :tn], reads=[b_pt], writes=[b_s])
                        P.dma("sp", kT_s[l][h, :, t0:t0 + tn], s[:, :tn], reads=[b_s])
                for blk in range(NBLK):
                    pt, b_pt = pbank.next()
                    for k in range(4):
                        P.op("pe", "matmul", pt[:, :HG * 128], lhsT=ckvT[:, k, blk * 128:(blk + 1) * 128],
                             rhs=a_v[:, k].rearrange("p h d -> p (h d)"), start=(k == 0), stop=(k == 3),
                             reads=b_ckall + b_v, writes=[b_pt])
                    s, b_s = sq.next()
                    P.op("dve" if blk % 2 else "act", "tensor_copy" if blk % 2 else "copy", s[:, :HG * 128], pt[:, :HG * 128],
                         reads=[b_pt], writes=[b_s])
                    P.dma("sp", v_s[l][hs, :, blk, :].rearrange("h p d -> p h d"),
                          s[:, :HG * 128].rearrange("t (h d) -> t h d", d=128), reads=[b_s])

        def qtiles():
            qt = []
            for i in range(NT):
                qt.append(dict(tok0=i * 128, nq=128, segs=[(0, (i + 1) * 128)], diag=128, og=("p", i)))
            qt.append(dict(tok0=T, nq=64, segs=[(c.pastA, PAST), (c.kvA, 64)], diag=64, og=("s", 0)))
            qt.append(dict(tok0=T + 64, nq=64, segs=[(c.pastB, PAST), (c.kvB, 64)], diag=64, og=("s", 1)))
            return qt

        def chunks_of(q):
            ch = []
            nseg = len(q["segs"])
            for si, (c0, n) in enumerate(q["segs"]):
                for o in range(0, n, 1024):
                    nn = min(1024, n - o)
                    last = (si == nseg - 1) and (o + nn == n)
                    ch.append((c0 + o, nn, last))
            return ch

        def phase_attn(es, l, j, is_mla, og, ogs, b_og, b_ogs):
            kv_valid = [(0, T + 64), (T + 128, T + 192), (T + 256, NKV)]
            scale = (192.0 if is_mla else 128.0) ** -0.5
            Sps = Ring([(ps(es, f"S{i}", [128, 1024]), P.buf(f"S{i}", True)) for i in range(2)])
            Tps = Ring([(ps(es, f"Tp{i}", [128, 512]), P.buf(f"Tp{i}", True)) for i in range(2)])
            Ops = Ring([(ps(es, f"O{i}", [128, 512]), P.buf(f"O{i}", True)) for i in range(2)])
            kT = Ring([(sb(es, f"kT{i}", [128, NKV], BF16), P.buf(f"kT{i}")) for i in range(2)])
            vv = Ring([(sb(es, f"vv{i}", [128, NBLK, 128], BF16), P.buf(f"vv{i}")) for i in range(2)])
            qT = Ring([(sb(es, f"qT{i}", [128, NTOK], BF16), P.buf(f"qT{i}")) for i in range(2)])
            sgp = Ring([(sb(es, f"sgp{i}", [128, NT, 128], BF16), P.buf(f"sgp{i}")) for i in range(2)])
            sgs = Ring([(sb(es, f"sgs{i}", [64, 2, 128], BF16), P.buf(f"sgs{i}")) for i in range(2)])
            if is_mla:
                qr = Ring([(sb(es, f"qr{i}", [64, NTOK], BF16), P.buf(f"qr{i}")) for i in range(2)])
                krT = sb(es, "krTall", [64, NKV], BF16)
                b_krT = P.buf("krTall")
                P.dma_group("sp", [(krT[:, a:b], krT_s[l][:, a:b]) for (a, b) in kv_valid], writes=[b_krT])
                Ssb = Ring([(sb(es, f"Ssb{i}", [128, max(T, PAST + 64)], F32), P.buf(f"Ssb{i}")) for i in range(2)])
                stat = Ring([(sb(es, f"stat{i}", [128, 8], F32), P.buf(f"stat{i}")) for i in range(3)])
            else:
                Eb = Ring([(sb(es, f"E{i}", [128, 1024], F32), P.buf(f"E{i}")) for i in range(2)])
                Cb = Ring([(sb(es, f"C{i}", [128, 1024], F32), P.buf(f"C{i}")) for i in range(2)])
                ones = sb(es, "ones", [128, 1024], F32)
                b_ones = P.buf("ones")
                P.op("pool", "memset", ones[:], 1.0, writes=[b_ones])
            Ab = Ring([(sb(es, f"A{i}", [128, 1024], BF16), P.buf(f"A{i}")) for i in range(2)])
            ATb = Ring([(sb(es, f"AT{i}", [128, 8, 128], BF16), P.buf(f"AT{i}")) for i in range(2)])
            QT = qtiles()
            cnt = [0]

            def pv(A, b_A, nq, col_in_A, col0, n, o_ps, b_o, first, last):
                nb = (n + 127) // 128
                for g in range(0, nb, 8):
                    ng = min(8, nb - g)
                    tp, b_tp = Tps.next()
                    tpb = tp[:].bitcast(BF16)
                    nks = []
                    for bi in range(ng):
                        k0 = (g + bi) * 128
                        nk = min(128, n - k0)
                        nks.append(nk)
                        P.op("pe", "transpose", tpb[:nk, bi * 128:bi * 128 + nq], A[:nq, col_in_A + k0:col_in_A + k0 + nk],
                             ident[:nq, :nq], reads=[b_A], writes=[b_tp])
                    at, b_at = ATb.next()
                    nk0 = nks[0]
                    eng = "act" if cnt[0] % 2 == 0 else "dve"
                    cnt[0] += 1
                    P.op(eng, "copy" if eng == "act" else "tensor_copy", at[:nk0, :ng, :nq],
                         tpb[:nk0, :ng * 128].rearrange("p (b t) -> p b t", t=128)[:, :, :nq], reads=[b_tp], writes=[b_at])
                    for bi in range(ng):
                        blk = (col0 + (g + bi) * 128) // 128
                        is_first = first and g == 0 and bi == 0
                        is_last = last and (g + bi == nb - 1)
                        P.op("pe", "matmul", o_ps[:nq, :128], lhsT=at[:nks[bi], bi, :nq], rhs=vcur[:nks[bi], blk, :],
                             start=is_first, stop=is_last, reads=[b_at, b_vcur], writes=[b_o])

            for h in range(H):
                kcur, b_kcur = kT.next()
                vcur, b_vcur = vv.next()
                qcur, b_qcur = qT.next()
                sgpc, b_sgpc = sgp.next()
                sgsc, b_sgsc = sgs.next()
                P.dma_group("sp", [(kcur[:, a:b], kT_s[l][h, :, a:b]) for (a, b) in kv_valid], writes=[b_kcur])
                P.dma_group("sp", [(vcur[:, 0:NT, :], v_s[l][h, :, 0:NT, :]),
                                   (vcur[0:64, NT:NT + 2, :], v_s[l][h, 0:64, NT:NT + 2, :]),
                                   (vcur[:, NT + 2:NBLK, :], v_s[l][h, :, NT + 2:NBLK, :])], writes=[b_vcur])
                P.dma("sp", qcur[:], qT_s[l][h, 0:128, :], writes=[b_qcur])
                P.dma("sp", sgpc[:], sg_s[l][h, 0:T, :].rearrange("(b p) d -> p b d", p=128), writes=[b_sgpc])
                P.dma("sp", sgsc[:], sg_s[l][h, T:T + 128, :].rearrange("(s p) d -> p s d", p=64), writes=[b_sgsc])
                if is_mla:
                    qrc, b_qrc = qr.next()
                    P.dma("sp", qrc[:], qT_s[l][h, 128:192, :], writes=[b_qrc])
                for q in QT:
                    nq, tok0 = q["nq"], q["tok0"]
                    qc = slice(tok0, tok0 + nq)
                    chs = chunks_of(q)
                    o_ps, b_o = Ops.next()
                    if q["og"][0] == "p":
                        og_dst = og[:nq, q["og"][1], h * 128:(h + 1) * 128]
                        b_dst = b_og[q["og"][1]]
                        sg_src = sgpc[:nq, q["og"][1], :]
                        b_sg = b_sgpc
                    else:
                        og_dst = ogs[:nq, q["og"][1], h * 128:(h + 1) * 128]
                        b_dst = b_ogs[q["og"][1]]
                        sg_src = sgsc[:nq, q["og"][1], :]
                        b_sg = b_sgsc
                    if is_mla:
                        S, b_S = Ssb.next()
                        st, b_st = stat.next()
                        off = 0
                        offs = []
                        for ci, (col0, n, isd) in enumerate(chs):
                            sp_, b_sp = Sps.next()
                            for m in range(0, n, 512):
                                mm = min(512, n - m)
                                P.op("pe", "matmul", sp_[:nq, m:m + mm], lhsT=qcur[:, qc], rhs=kcur[:, col0 + m:col0 + m + mm],
                                     start=True, stop=False, reads=[b_qcur, b_kcur], writes=[b_sp])
                                P.op("pe", "matmul", sp_[:nq, m:m + mm], lhsT=qrc[:, qc], rhs=krT[:, col0 + m:col0 + m + mm],
                                     start=False, stop=True, reads=[b_qrc, b_krT], writes=[b_sp])
                            P.op("dve", "tensor_scalar", S[:nq, off:off + n], sp_[:nq, :n], scale, None, ALU.mult, ALU.max,
                                 st[:nq, ci:ci + 1], reads=[b_sp], writes=[b_S, b_st])
                            if isd and q["diag"] == 128:
                                P.op("dve", "memset", S[0:64, off + n - 64:off + n], NEG, reads=[b_S], writes=[b_S])
                            offs.append(off)
                            off += n
                        nch = len(chs)
                        P.op("dve", "tensor_reduce", st[:nq, 4:5], st[:nq, 0:nch], AX.X, ALU.max, negate=True,
                             reads=[b_st], writes=[b_st])
                        for ci, (col0, n, isd) in enumerate(chs):
                            A, b_A = Ab.next()
                            P.op("act", "activation", out=A[:nq, :n], in_=S[:nq, offs[ci]:offs[ci] + n], func=AF.Exp,
                                 bias=st[:nq, 4:5], scale=1.0, accum_out=st[:nq, 5 + ci:6 + ci],
                                 reads=[b_S, b_st], writes=[b_A, b_st])
                            pv(A, b_A, nq, 0, col0, n, o_ps, b_o, ci == 0, ci == nch - 1)
                        if nch > 1:
                            P.op("dve", "tensor_reduce", st[:nq, 7:8], st[:nq, 5:5 + nch], AX.X, ALU.add, reads=[b_st], writes=[b_st])
                            P.op("dve", "reciprocal", st[:nq, 7:8], st[:nq, 7:8], reads=[b_st], writes=[b_st])
                        else:
                            P.op("dve", "reciprocal", st[:nq, 7:8], st[:nq, 5:6], reads=[b_st], writes=[b_st])
                        P.op("dve", "scalar_tensor_tensor", out=og_dst, in0=o_ps[:nq, :128], scalar=st[:nq, 7:8], in1=sg_src,
                             op0=ALU.mult, op1=ALU.mult, reads=[b_o, b_st, b_sg], writes=[b_dst])
                    else:
                        carry = None
                        b_carry = None
                        nch = len(chs)
                        for ri, (col0, n, isd) in enumerate(reversed(chs)):
                            dn = q["diag"]
                            zp, b_zp = Sps.next()
                            for m in range(0, n, 512):
                                mm = min(512, n - m)
                                P.op("pe", "matmul", zp[:nq, m:m + mm], lhsT=qcur[:, qc], rhs=kcur[:, col0 + m:col0 + m + mm],
                                     start=True, stop=True, reads=[b_qcur, b_kcur], writes=[b_zp])
                            E, b_E = Eb.next()
                            P.op("act", "activation", out=E[:nq, :n], in_=zp[:nq, :n], func=AF.Exp, scale=scale,
                                 reads=[b_zp], writes=[b_E])
                            P.op("act", "activation", out=E[:nq, :n], in_=E[:nq, :n], func=AF.Ln, bias=1.0, scale=1.0,
                                 reads=[b_E], writes=[b_E])
                            if isd:
                                P.op("pool", "tensor_tensor", E[:nq, n - dn:n], E[:nq, n - dn:n], tri[:nq, :dn], ALU.mult,
                                     reads=[b_E, b_const], writes=[b_E])
                            Cs, b_C = Cb.next()
                            rd = [b_E, b_ones] + ([b_carry] if carry is not None else [])
                            P.op("dve", "tensor_tensor_scan", Cs[:nq, 0:n][:, ::-1], ones[:nq, 0:n][:, ::-1], E[:nq, 0:n][:, ::-1],
                                 carry if carry is not None else 0.0, ALU.mult, ALU.add, reads=rd, writes=[b_C])
                            carry, b_carry = Cs[:nq, 0:1], b_C
                            P.op("dve", "scalar_tensor_tensor", out=E[:nq, :n], in0=zp[:nq, :n], scalar=scale, in1=Cs[:nq, :n],
                                 op0=ALU.mult, op1=ALU.subtract, reads=[b_zp, b_C], writes=[b_E])
                            A, b_A = Ab.next()
                            P.op("act", "activation", out=A[:nq, :n], in_=E[:nq, :n], func=AF.Exp, reads=[b_E], writes=[b_A])
                            if isd:
                                P.op("pool", "tensor_tensor", A[:nq, n - dn:n], A[:nq, n - dn:n], trib[:nq, :dn], ALU.mult,
                                     reads=[b_A, b_const], writes=[b_A])
                            pv(A, b_A, nq, 0, col0, n, o_ps, b_o, ri == 0, ri == nch - 1)
                        P.op("dve", "tensor_tensor", og_dst, o_ps[:nq, :128], sg_src, ALU.mult, reads=[b_o, b_sg], writes=[b_dst])

        def phase_out(es, l, j, is_mla, og, ogs, b_og, b_ogs):
            HC = HD // 128
            Wo = (mla_w_o if is_mla else sb_w_o)[j]
            x_src = x_in if l == 0 else x_s[l]
            last = (l == c.DEPTH - 1)
            pbank = Ring([(ps(es, f"pb{i}", [128, 512]), P.buf(f"pb{i}", True)) for i in range(8)])
            wo = sb(es, "wo", [128, HC, D], BF16)
            b_wo = [P.buf(f"wo{i}") for i in range(4)]
            kk = max(1, HC // 4)
            for qi, k0 in enumerate(range(0, HC, kk)):
                P.dma("pool", wo[:, k0:k0 + kk, :], Wo[k0 * 128:(k0 + kk) * 128, :].rearrange("(k p) n -> p k n", p=128),
                      writes=[b_wo[qi]])
            xt = Ring([(sb(es, f"xo{i}", [128, D], F32), P.buf(f"xo{i}")) for i in range(2)])
            xn = Ring([(sb(es, f"xn{i}", [128, D], F32), P.buf(f"xn{i}")) for i in range(2)])
            oT = Ring([(sb(es, f"oT{i}", [128, HC, 128], BF16), P.buf(f"oT{i}")) for i in range(2)])
            if last:
                fg = sb(es, "fg", [128, D], F32)
                b_fg = P.buf("fg")
                P.dma("sp", fg[:], final_gain.partition_broadcast(128), writes=[b_fg])
                yt = Ring([(sb(es, f"yt{i}", [128, D], F32), P.buf(f"yt{i}")) for i in range(2)])
                stt = Ring([(sb(es, f"fst{i}", [128, 2], F32), P.buf(f"fst{i}")) for i in range(2)])
                junk = sb(es, "junk3", [128, D], BF16)
                b_junk = P.buf("junk3")
            for ti in range(NTT):
                x_, b_x = xt.next()
                P.dma("sp", x_[:], x_src[ti * 128:(ti + 1) * 128, :], writes=[b_x])
                ot, b_ot = oT.next()
                for g in range(0, HC, 8):
                    ng = min(8, HC - g)
                    tp, b_tp = pbank.next()
                    tpb = tp[:].bitcast(BF16)
                    for k in range(ng):
                        kc = slice((g + k) * 128, (g + k + 1) * 128)
                        if ti < NT:
                            P.op("pe", "transpose", tpb[:, k * 128:(k + 1) * 128], og[:, ti, kc], ident[:],
                                 reads=[b_og[ti]], writes=[b_tp])
                        else:
                            for s_ in range(2):
                                P.op("pe", "transpose", tpb[:, k * 128 + s_ * 64:k * 128 + s_ * 64 + 64], ogs[:, s_, kc],
                                     ident[:64, :64], reads=[b_ogs[s_]], writes=[b_tp])
                    eng = "act" if (g // 8) % 2 == 0 else "dve"
                    P.op(eng, "copy" if eng == "act" else "tensor_copy", ot[:, g:g + ng, :],
                         tpb[:, :ng * 128].rearrange("p (k t) -> p k t", t=128), reads=[b_tp], writes=[b_ot])
                xn_, b_xn = xn.next()
                for nb in range(D // 512):
                    pt, b_pt = pbank.next()
                    for k in range(HC):
                        P.op("pe", "matmul", pt[:, :512], lhsT=ot[:, k, :], rhs=wo[:, k, nb * 512:(nb + 1) * 512],
                             start=(k == 0), stop=(k == HC - 1), reads=[b_ot] + b_wo, writes=[b_pt])
                    P.op("dve", "tensor_tensor", xn_[:, nb * 512:(nb + 1) * 512], pt[:, :512], x_[:, nb * 512:(nb + 1) * 512], ALU.add,
                         reads=[b_pt, b_x], writes=[b_xn])
                if not last:
                    P.dma("sp", x_s[l + 1][ti * 128:(ti + 1) * 128, :], xn_[:], reads=[b_xn])
                else:
                    st, b_st = stt.next()
                    P.op("act", "activation", out=junk[:], in_=xn_[:], func=AF.Square, accum_out=st[:, 0:1],
                         reads=[b_xn], writes=[b_junk, b_st])
                    rms_rstd(st[:, 0:1], st[:, 1:2], D, [b_st])
                    y_, b_y = yt.next()
                    P.op("dve", "scalar_tensor_tensor", out=y_[:], in0=xn_[:], scalar=st[:, 1:2], in1=fg[:],
                         op0=ALU.mult, op1=ALU.mult, reads=[b_xn, b_st, b_fg], writes=[b_y])
                    P.dma("sp", y_o[ti * 128:(ti + 1) * 128, :], y_[:], reads=[b_y])

        for l in range(DEPTH):
            is_mla = (l % 2 == 0)
            j = l // 2
            with ExitStack() as es:
                pbank = Ring([(ps(es, f"pb{i}", [128, 512]), P.buf(f"pb{i}", True)) for i in range(8)])
                hT = sb(es, "hT", [128, KC, NTOK], BF16)
                b_hT = [P.buf(f"hT{ti}") for ti in range(NTT)]
                with ExitStack() as es1:
                    phase_norm(es1, l, hT, b_hT, pbank)
                    P.flush()
                with ExitStack() as es2:
                    if is_mla:
                        phase_proj_mla_a(es2, l, j, hT, b_hT, pbank)
                    else:
                        phase_proj_sb(es2, l, j, hT, b_hT, pbank)
                    P.flush()
            if is_mla:
                with ExitStack() as es:
                    pbank = Ring([(ps(es, f"pb{i}", [128, 512]), P.buf(f"pb{i}", True)) for i in range(8)])
                    phase_proj_mla_b(es, l, j, pbank)
                    P.flush()
            with ExitStack() as es:
                og = sb(es, "og", [128, NT, HD], BF16)
                ogs = sb(es, "ogs", [64, 2, HD], BF16)
                b_og = [P.buf(f"og{i}") for i in range(NT)]
                b_ogs = [P.buf(f"ogs{i}") for i in range(2)]
                with ExitStack() as es3:
                    phase_attn(es3, l, j, is_mla, og, ogs, b_og, b_ogs)
                    P.flush()
                with ExitStack() as es4:
                    phase_out(es4, l, j, is_mla, og, ogs, b_og, b_ogs)
                    P.flush()
    return nc


def _consts(cfg):
    c = cfg
    ident = np.eye(128, dtype=np.float32).astype(ml_dtypes.bfloat16)
    t = np.arange(128)
    tri = (t[None, :] < t[:, None]).astype(np.float32)
    half = 32
    inv = (1.0 / (np.float32(10000.0) ** (np.arange(half, dtype=np.float32) * np.float32(2.0 / 64)))).astype(np.float32)
    pos = np.concatenate([np.arange(c.T), c.PAST + np.arange(64), c.PAST + np.arange(64)]).astype(np.float32)
    ang = (pos[:, None] * inv[None, :]).astype(np.float32)
    cos, sin = np.cos(ang).astype(np.float32), np.sin(ang).astype(np.float32)
    cosT = np.ascontiguousarray(np.concatenate([cos, cos], axis=1).T)
    sinT = np.ascontiguousarray(np.concatenate([-sin, sin], axis=1).T)
    return dict(k_ident=ident, k_tri=tri, k_cos_tm=cos, k_sin_tm=sin, k_cosT=cosT, k_sinT=sinT)


def make_in_maps(cfg, n_cores, inp):
    c = cfg
    consts = _consts(c)
    f = lambda a: np.ascontiguousarray(np.asarray(a, dtype=np.float32))
    shared = dict(
        ln_gain=f(inp["ln_gain"]), final_gain=f(inp["final_gain"]).reshape(1, -1),
        mla_w_in=f(inp["mla_w_in"]), mla_q_norm=f(inp["mla_q_norm"]), mla_kv_norm=f(inp["mla_kv_norm"]),
        mla_w_q_up=f(inp["mla_w_q_up"]), mla_w_kv_up=f(inp["mla_w_kv_up"]), mla_w_o=f(inp["mla_w_o"]),
        sb_w_in=f(inp["sb_w_in"]), sb_w_o=f(inp["sb_w_o"]), **consts)
    xp, xs = np.asarray(inp["x_prompt"]), np.asarray(inp["x_sample"])
    ckv, kr = np.asarray(inp["cache_mla_ckv"]), np.asarray(inp["cache_mla_krope"])
    sk, sv = np.asarray(inp["cache_sb_k"]), np.asarray(inp["cache_sb_v"])
    maps = []
    for b in range(n_cores):
        m = dict(shared)
        m["x_in"] = f(np.concatenate([xp[b], xs[2 * b], xs[2 * b + 1]], axis=0))
        m["c_ckv"] = f(ckv[:, 2 * b:2 * b + 2])
        m["c_kr"] = f(kr[:, 2 * b:2 * b + 2])
        m["c_sbk"] = f(sk[:, 2 * b:2 * b + 2].reshape(c.N_SB, 2, c.PAST, c.HD))
        m["c_sbv"] = f(sv[:, 2 * b:2 * b + 2].reshape(c.N_SB, 2, c.PAST, c.HD))
        maps.append(m)
    return maps


def assemble(cfg, res):
    c = cfg
    T, H = c.T, c.H
    n = len(res)
    st = lambda k: np.stack([np.asarray(r[k]) for r in res], axis=0)
    y, ckv, kr, sbk, sbv = st("y"), st("ckv_o"), st("kr_o"), st("sbk_o"), st("sbv_o")

    def split_tok(a, tok_axis):
        p = np.take(a, np.arange(T), axis=tok_axis)
        s0 = np.take(a, np.arange(T, T + 64), axis=tok_axis)
        s1 = np.take(a, np.arange(T + 64, T + 128), axis=tok_axis)
        s = np.stack([s0, s1], axis=1)
        s = s.reshape((2 * n,) + s.shape[2:])
        return p, s

    yp, ys = split_tok(y, 1)
    cp, cs = split_tok(ckv, 2)
    kp, ks = split_tok(kr, 2)
    skp, sks = split_tok(sbk, 2)
    svp, svs = split_tok(sbv, 2)
    mv = lambda a: np.ascontiguousarray(np.moveaxis(a, 1, 0))
    hd = lambda a: a.reshape(a.shape[:-1] + (H, 128))
    return (np.ascontiguousarray(yp), np.ascontiguousarray(ys), mv(cp), mv(kp), hd(mv(skp)), hd(mv(svp)),
            mv(cs), mv(ks), hd(mv(sks)), hd(mv(svs)))


_NC_CACHE = {}


def kernel(**inputs):
    cfg = Cfg()
    n = 8
    if "nc" not in _NC_CACHE:
        _NC_CACHE["nc"] = build(cfg)
    nc = _NC_CACHE["nc"]
    in_maps = make_in_maps(cfg, n, inputs)
    res = run_bass_kernel_spmd(nc, in_maps, core_ids=list(range(n)))
    return assemble(cfg, res.results)
```

```python
import numpy as np
from contextlib import ExitStack
import concourse.bass as bass
import concourse.mybir as mybir
from concourse.bass_utils import run_bass_kernel_spmd
import ml_dtypes

F32 = mybir.dt.float32
BF16 = mybir.dt.bfloat16
AF = mybir.ActivationFunctionType
ALU = mybir.AluOpType
AX = mybir.AxisListType
NEG = -1e30
EPS = 1e-6


class Cfg:
    def __init__(self, D=2048, T=2048, TS=64, PAST=1024, H=16, DEPTH=4):
        self.D, self.T, self.TS, self.PAST, self.H, self.DEPTH = D, T, TS, PAST, H, DEPTH
        self.KC = D // 128
        self.NT = T // 128
        self.NTT = self.NT + 1
        self.NTOK = T + 128
        self.HD = H * 128
        self.QL = 512
        self.KVL = 512
        self.MLA_IN = 512 + 512 + 64 + self.HD
        self.NKV = T + 256 + 2 * PAST
        self.NBLK = self.NKV // 128
        self.N_MLA = (DEPTH + 1) // 2
        self.N_SB = DEPTH // 2
        self.kvA, self.kvB = T, T + 128
        self.pastA, self.pastB = T + 256, T + 256 + PAST
        assert TS == 64 and T % 128 == 0 and PAST % 128 == 0


class Buf:
    __slots__ = ("name", "wc", "wd", "rc", "rd", "excl")

    def __init__(self, name="", excl=False):
        self.name = name
        self.excl = excl
        self.wc = {}
        self.wd = []
        self.rc = {}
        self.rd = []

    def clear(self):
        self.wc = {}
        self.wd = []
        self.rc = {}
        self.rd = []


class Op:
    __slots__ = ("eng", "meth", "args", "kw", "deps", "needed", "sem", "val", "dma")

    def __init__(self, eng, meth, args, kw, dma):
        self.eng, self.meth, self.args, self.kw, self.dma = eng, meth, args, kw, dma
        self.deps = []
        self.needed = dma
        self.sem = None
        self.val = 0


class Prog:
    ENGS = ["pe", "act", "dve", "pool", "sp"]

    def __init__(self, nc, es, n_dma_sems=20):
        self.nc = nc
        self.h = {"pe": nc.tensor, "act": nc.scalar, "dve": nc.vector, "pool": nc.gpsimd, "sp": nc.sync}
        self.esem = {e: es.enter_context(nc.semaphore("c_" + e)) for e in self.ENGS}
        self.ecnt = {e: 0 for e in self.ENGS}
        self.dsem = {}
        for q in ("sp", "pool", "act"):
            self.dsem[q] = [[es.enter_context(nc.semaphore(f"d_{q}{i}")), 0, None] for i in range(n_dma_sems)]
        self.dptr = {q: 0 for q in self.dsem}
        self.ops = []
        self.bufs = []
        self.waited = {e: {} for e in self.ENGS}
        self.nblock = 0
        self.max_blocks = None

    def buf(self, name="", excl=False):
        b = Buf(name, excl)
        self.bufs.append(b)
        return b

    def _deps(self, eng, dma, reads, writes):
        deps = []
        for b in reads:
            for d in b.wc.values():
                if not (d.eng == eng == "pe"):
                    deps.append(d)
            deps.extend(b.wd)
            if b.excl:
                for d in b.rc.values():
                    if d.eng != eng:
                        deps.append(d)
        for b in writes:
            for d in b.wc.values():
                if dma or d.eng != eng or eng != "pe":
                    deps.append(d)
            deps.extend(b.wd)
            for d in b.rc.values():
                if dma or d.eng != eng or eng != "pe":
                    deps.append(d)
            deps.extend(b.rd)
        out, seen = [], set()
        for d in deps:
            if id(d) not in seen:
                seen.add(id(d))
                d.needed = True
                out.append(d)
        return out

    def _register(self, ops, eng, dma, reads, writes):
        for b in writes:
            b.clear()
            for o in ops:
                if dma:
                    b.wd.append(o)
                else:
                    b.wc[eng] = o
        for b in reads:
            if b in writes:
                continue
            for o in ops:
                if dma:
                    b.rd.append(o)
                else:
                    b.rc[eng] = o

    def _emit(self, eng, meth, args, kw, reads, writes, dma):
        o = Op(eng, meth, args, kw, dma)
        o.deps = self._deps(eng, dma, reads, writes)
        self._register([o], eng, dma, reads, writes)
        self.ops.append(o)
        return o

    def dma_group(self, q, pairs, reads=(), writes=()):
        deps = self._deps(q, True, reads, writes)
        ops = []
        for (out, in_) in pairs:
            o = Op(q, "dma_start", (), dict(out=out, in_=in_), True)
            o.deps = list(deps)
            ops.append(o)
            self.ops.append(o)
        self._register(ops, q, True, reads, writes)
        return ops

    def op(self, eng, meth, *args, reads=(), writes=(), **kw):
        return self._emit(eng, meth, args, kw, reads, writes, False)

    def dma(self, q, out, in_, reads=(), writes=(), **kw):
        kw = dict(kw)
        kw["out"] = out
        kw["in_"] = in_
        return self._emit(q, "dma_start", (), kw, reads, writes, True)

    def flush(self):
        nc = self.nc
        if self.max_blocks is not None and self.nblock >= self.max_blocks:
            self.ops = []
            for b in self.bufs:
                b.clear()
            return
        for o in self.ops:
            if o.dma:
                slot = self.dsem[o.eng][self.dptr[o.eng] % len(self.dsem[o.eng])]
                self.dptr[o.eng] += 1
                if slot[2] is not None:
                    o.deps.append(slot[2])
                slot[1] += 16
                o.sem, o.val = slot[0], slot[1]
                slot[2] = o
            elif o.needed:
                self.ecnt[o.eng] += 1
                o.sem, o.val = self.esem[o.eng], self.ecnt[o.eng]
        per = {e: [o for o in self.ops if o.eng == e] for e in self.ENGS}
        dmas = [o for o in self.ops if o.dma]
        self.nblock += 1
        with nc.Block() as block:
            for e in self.ENGS:
                ops_e = per[e]
                if not ops_e and not (e == "sp" and dmas):
                    continue

                def body(h, e=e, ops_e=ops_e):
                    waited = self.waited[e]
                    for o in ops_e:
                        for d in o.deps:
                            k = d.sem.num
                            if waited.get(k, 0) >= d.val:
                                continue
                            h.wait_ge(d.sem, d.val)
                            waited[k] = d.val
                        ins = getattr(h, o.meth)(*o.args, **o.kw)
                        if o.sem is not None:
                            ins.then_inc(o.sem, 16 if o.dma else 1)
                    if e == "sp":
                        for d in dmas:
                            k = d.sem.num
                            if waited.get(k, 0) >= d.val:
                                continue
                            h.wait_ge(d.sem, d.val)
                            waited[k] = d.val

                getattr(block, {"pe": "tensor", "act": "scalar", "dve": "vector", "pool": "gpsimd", "sp": "sync"}[e])(body)
        self.ops = []
        for b in self.bufs:
            b.clear()
        for q in self.dsem:
            for slot in self.dsem[q]:
                slot[2] = None


class Ring:
    def __init__(self, items):
        self.items = items
        self.i = 0

    def next(self):
        it = self.items[self.i % len(self.items)]
        self.i += 1
        return it


def build(cfg, dbg_layers=None, max_blocks=None):
    c = cfg
    D, T, H, KC, NT, NTT, NTOK, HD, NKV, NBLK, PAST = c.D, c.T, c.H, c.KC, c.NT, c.NTT, c.NTOK, c.HD, c.NKV, c.NBLK, c.PAST
    DEPTH = c.DEPTH if dbg_layers is None else dbg_layers
    nc = bass.Bass("TRN2", target_bir_lowering=False)

    def din(name, shape, dt=F32):
        return nc.dram_tensor(name, list(shape), dt, kind="ExternalInput").ap()

    def dout(name, shape, dt=F32):
        return nc.dram_tensor(name, list(shape), dt, kind="ExternalOutput").ap()

    def dscr(name, shape, dt):
        return nc.dram_tensor(name, list(shape), dt).ap()

    x_in = din("x_in", [NTOK, D])
    c_ckv = din("c_ckv", [c.N_MLA, 2, PAST, 512])
    c_kr = din("c_kr", [c.N_MLA, 2, PAST, 64])
    c_sbk = din("c_sbk", [c.N_SB, 2, PAST, HD])
    c_sbv = din("c_sbv", [c.N_SB, 2, PAST, HD])
    ln_gain = din("ln_gain", [c.DEPTH, D])
    final_gain = din("final_gain", [1, D])
    mla_w_in = din("mla_w_in", [c.N_MLA, D, c.MLA_IN])
    mla_q_norm = din("mla_q_norm", [c.N_MLA, 512])
    mla_kv_norm = din("mla_kv_norm", [c.N_MLA, 512])
    mla_w_q_up = din("mla_w_q_up", [c.N_MLA, 512, H * 192])
    mla_w_kv_up = din("mla_w_kv_up", [c.N_MLA, 512, H * 256])
    mla_w_o = din("mla_w_o", [c.N_MLA, HD, D])
    sb_w_in = din("sb_w_in", [c.N_SB, D, 4 * HD])
    sb_w_o = din("sb_w_o", [c.N_SB, HD, D])
    k_ident = din("k_ident", [128, 128], BF16)
    k_nm_sb = din("k_nm_sb", [128, 128], BF16)
    k_nm_mla = din("k_nm_mla", [128, 128], BF16)
    k_cos_tm = din("k_cos_tm", [NTOK, 32])
    k_sin_tm = din("k_sin_tm", [NTOK, 32])
    k_cosT = din("k_cosT", [64, NTOK])
    k_sinT = din("k_sinT", [64, NTOK])

    y_o = dout("y", [NTOK, D])
    ckv_o = dout("ckv_o", [c.N_MLA, NTOK, 512])
    kr_o = dout("kr_o", [c.N_MLA, NTOK, 64])
    sbk_o = dout("sbk_o", [c.N_SB, NTOK, HD])
    sbv_o = dout("sbv_o", [c.N_SB, NTOK, HD])

    x_s = [None] + [dscr(f"x_s{l}", [NTOK, D], F32) for l in range(1, c.DEPTH)]
    qT_s = [dscr(f"qT_s{l}", [H, 192, NTOK], BF16) for l in range(c.DEPTH)]
    kT_s = [dscr(f"kT_s{l}", [H, 128, NKV], BF16) for l in range(c.DEPTH)]
    krT_s = [dscr(f"krT_s{l}", [64, NKV], BF16) for l in range(c.DEPTH)]
    v_s = [dscr(f"v_s{l}", [H, 128, NBLK, 128], BF16) for l in range(c.DEPTH)]
    sg_s = [dscr(f"sg_s{l}", [H, NTOK, 128], BF16) for l in range(c.DEPTH)]
    cqnT_s = [dscr(f"cqnT_s{l}", [128, 4, NTOK], BF16) for l in range(c.DEPTH)]
    ckvT_s = [dscr(f"ckvT_s{l}", [128, 4, NKV], BF16) for l in range(c.DEPTH)]

    es_top = ExitStack()
    with es_top:
        P = Prog(nc, es_top)
        P.max_blocks = max_blocks

        uid = [0]

        def sb(es, name, shape, dt):
            uid[0] += 1
            return es.enter_context(nc.sbuf_tensor(f"{name}_{uid[0]}", list(shape), dt))

        def ps(es, name, shape, dt=F32):
            uid[0] += 1
            return es.enter_context(nc.psum_tensor(f"{name}_{uid[0]}", list(shape), dt))

        ident = sb(es_top, "ident", [128, 128], BF16)
        b_const = P.buf("const")
        P.dma("sp", ident[:], k_ident, writes=[b_const])
        P.flush()

        def kv_cols(ti):
            if ti < NT:
                return [(0, 128, ti * 128)]
            return [(0, 64, c.kvA), (64, 64, c.kvB)]

        def rms_rstd(ss, rstd, n, bufs, rows=128):
            P.op("act", "activation", out=rstd[:rows], in_=ss[:rows], func=AF.Ln, bias=EPS, scale=1.0 / n,
                 reads=bufs, writes=bufs)
            P.op("act", "activation", out=rstd[:rows], in_=rstd[:rows], func=AF.Exp, scale=-0.5,
                 reads=bufs, writes=bufs)

        def phase_norm(es, l, hT, b_hT, pbank):
            x_src = x_in if l == 0 else x_s[l]
            gbc = sb(es, "gbc", [128, D], F32)
            b_g = P.buf("gbc")
            P.dma("sp", gbc[:], ln_gain[l:l + 1, :].partition_broadcast(128), writes=[b_g])
            xt = [sb(es, f"xt{i}", [128, D], F32) for i in range(2)]
            b_xt = [P.buf(f"xt{i}") for i in range(2)]
            xs = [sb(es, f"xs{i}", [128, D], BF16) for i in range(2)]
            b_xs = [P.buf(f"xs{i}") for i in range(2)]
            junk = sb(es, "junk", [128, D], BF16)
            b_junk = P.buf("junk")
            st = [sb(es, f"nst{i}", [128, 2], F32) for i in range(2)]
            b_st = [P.buf(f"nst{i}") for i in range(2)]
            for ti in range(NTT):
                i = ti % 2
                P.dma("sp", xt[i][:], x_src[ti * 128:(ti + 1) * 128, :], writes=[b_xt[i]])
                P.op("act", "activation", out=junk[:], in_=xt[i][:], func=AF.Square, accum_out=st[i][:, 0:1],
                     reads=[b_xt[i]], writes=[b_junk, b_st[i]])
                rms_rstd(st[i][:, 0:1], st[i][:, 1:2], D, [b_st[i]])
                P.op("dve", "scalar_tensor_tensor", out=xs[i][:], in0=xt[i][:], scalar=st[i][:, 1:2], in1=gbc[:],
                     op0=ALU.mult, op1=ALU.mult, reads=[b_xt[i], b_st[i], b_g], writes=[b_xs[i]])
                for g in range(0, KC, 8):
                    pt, b_pt = pbank.next()
                    ptb = pt[:].bitcast(BF16)
                    ng = min(8, KC - g)
                    for k in range(ng):
                        P.op("pe", "transpose", ptb[:, k * 128:(k + 1) * 128], xs[i][:, (g + k) * 128:(g + k + 1) * 128],
                             ident[:], reads=[b_xs[i]], writes=[b_pt])
                    eng = "act" if (g // 8) % 2 == 0 else "dve"
                    src = ptb[:, :ng * 128].rearrange("p (k t) -> p k t", t=128)
                    if eng == "act":
                        P.op("act", "copy", hT[:, g:g + ng, ti * 128:(ti + 1) * 128], src, reads=[b_pt], writes=[b_hT[ti]])
                    else:
                        P.op("dve", "tensor_copy", hT[:, g:g + ng, ti * 128:(ti + 1) * 128], src, reads=[b_pt], writes=[b_hT[ti]])

        def proj_tm(lhsT, b_l, nk, ti, wt, b_w, n, pbank):
            pt, b_pt = pbank.next()
            for k in range(nk):
                P.op("pe", "matmul", pt[:, :n], lhsT=lhsT[:, k, ti * 128:(ti + 1) * 128], rhs=wt[:, k, :n],
                     start=(k == 0), stop=(k == nk - 1), reads=[b_l] + list(b_w), writes=[b_pt])
            return pt, b_pt

        def load_w(wslots, src, nk, n):
            wt, b_w = wslots.next()
            kk = max(1, nk // 4)
            for k0 in range(0, nk, kk):
                P.dma("pool", wt[:, k0:k0 + kk, :n], src[k0 * 128:(k0 + kk) * 128, :].rearrange("(k p) n -> p k n", p=128),
                      writes=[b_w[k0 // kk]])
            return wt, b_w

        def phase_proj_sb(es, l, j, hT, b_hT, pbank):
            W = sb_w_in[j]
            wsl = []
            for i in range(2):
                wsl.append((sb(es, f"w{i}", [128, KC, 512], BF16), [P.buf(f"w{i}_{q}") for q in range(4)]))
            wslots = Ring(wsl)
            sf = Ring([(sb(es, f"sf{i}", [128, 512], F32), P.buf(f"sf{i}")) for i in range(3)])
            sh = Ring([(sb(es, f"sh{i}", [128, 512], BF16), P.buf(f"sh{i}")) for i in range(3)])
            skt = Ring([(sb(es, f"skt{i}", [128, 4, 128], BF16), P.buf(f"skt{i}")) for i in range(2)])
            sq = Ring([(sb(es, f"sq{i}", [128, 512], BF16), P.buf(f"sq{i}")) for i in range(3)])
            b_hall = b_hT
            for nb in range(HD // 512):
                wt, b_w = load_w(wslots, W[:, nb * 512:(nb + 1) * 512], KC, 512)
                for hh in range(4):
                    h = nb * 4 + hh
                    for t0 in range(0, NTOK, 512):
                        tn = min(512, NTOK - t0)
                        pt, b_pt = pbank.next()
                        for k in range(KC):
                            P.op("pe", "matmul", pt[:, :tn], lhsT=wt[:, k, hh * 128:(hh + 1) * 128], rhs=hT[:, k, t0:t0 + tn],
                                 start=(k == 0), stop=(k == KC - 1), reads=b_hall + b_w, writes=[b_pt])
                        s, b_s = sq.next()
                        P.op("act", "copy", s[:, :tn], pt[:, :tn], reads=[b_pt], writes=[b_s])
                        P.dma("sp", qT_s[l][h, 0:128, t0:t0 + tn], s[:, :tn], reads=[b_s])
            import os as _os
            _stop = int(_os.environ.get("SBSTOP", "9"))
            if _stop < 1:
                return
            for kind in ("k", "v", "g")[:max(0, _stop - 1)]:
                base = {"k": HD, "v": 2 * HD, "g": 3 * HD}[kind]
                for nb in range(HD // 512):
                    wt, b_w = load_w(wslots, W[:, base + nb * 512: base + (nb + 1) * 512], KC, 512)
                    h0 = nb * 4
                    for ti in range(NTT):
                        pt, b_pt = proj_tm(hT, b_hT[ti], KC, ti, wt, b_w, 512, pbank)
                        if kind == "g":
                            s, b_s = sh.next()
                            P.op("act", "activation", out=s[:], in_=pt[:], func=AF.Silu, reads=[b_pt], writes=[b_s])
                            P.dma("sp", sg_s[l][h0:h0 + 4, ti * 128:(ti + 1) * 128, :].rearrange("h t d -> t h d"),
                                  s[:].rearrange("t (h d) -> t h d", d=128), reads=[b_s])
                            continue
                        f, b_f = sf.next()
                        P.op("act", "copy", f[:], pt[:], reads=[b_pt], writes=[b_f])
                        dst = sbk_o if kind == "k" else sbv_o
                        P.dma("sp", dst[j, ti * 128:(ti + 1) * 128, nb * 512:(nb + 1) * 512], f[:], reads=[b_f])
                        s, b_s = sh.next()
                        P.op("dve", "tensor_copy", s[:], f[:], reads=[b_f], writes=[b_s])
                        if kind == "v":
                            for (r0, nr, kc0) in kv_cols(ti):
                                P.dma("sp", v_s[l][h0:h0 + 4, 0:nr, kc0 // 128, :].rearrange("h p d -> p h d"),
                                      s[r0:r0 + nr, :].rearrange("t (h d) -> t h d", d=128), reads=[b_s])
                        else:
                            tp, b_tp = pbank.next()
                            tpb = tp[:].bitcast(BF16)
                            for hh in range(4):
                                P.op("pe", "transpose", tpb[:, hh * 128:(hh + 1) * 128], s[:, hh * 128:(hh + 1) * 128], ident[:],
                                     reads=[b_s], writes=[b_tp])
                            kt, b_kt = skt.next()
                            import os as _os
                            _v = _os.environ.get("KVAR", "0")
                            if _v == "1":
                                P.op("act", "copy", kt[:], tpb[:, :512].rearrange("p (h t) -> p h t", t=128),
                                     reads=[b_tp], writes=[b_kt])
                            else:
                                P.op("dve", "tensor_copy", kt[:], tpb[:, :512].rearrange("p (h t) -> p h t", t=128),
                                     reads=[b_tp], writes=[b_kt])
                            for (r0, nr, kc0) in kv_cols(ti):
                                if _v == "2":
                                    for hh in range(4):
                                        P.dma("sp", kT_s[l][h0 + hh, :, kc0:kc0 + nr], kt[:, hh, r0:r0 + nr], reads=[b_kt])
                                elif _v == "3":
                                    pass
                                else:
                                    P.dma("sp", kT_s[l][h0:h0 + 4, :, kc0:kc0 + nr].rearrange("h p t -> p h t"),
                                          kt[:, :, r0:r0 + nr], reads=[b_kt])
            if _stop < 5:
                return
            pk = Ring([(sb(es, f"pk{i}", [128, HD], BF16), P.buf(f"pk{i}")) for i in range(2)])
            pkt = Ring([(sb(es, f"pkt{i}", [128, H, 128], BF16), P.buf(f"pkt{i}")) for i in range(2)])
            for sq_i, kv0 in ((0, c.pastA), (1, c.pastB)):
                for tb in range(PAST // 128):
                    kvc = kv0 + tb * 128
                    vt, b_vt = pk.next()
                    P.dma("pool", vt[:], c_sbv[j, sq_i, tb * 128:(tb + 1) * 128, :], writes=[b_vt])
                    P.dma("sp", v_s[l][:, :, kvc // 128, :].rearrange("h p d -> p h d"),
                          vt[:].rearrange("t (h d) -> t h d", d=128), reads=[b_vt])
                    ktm, b_ktm = pk.next()
                    P.dma("pool", ktm[:], c_sbk[j, sq_i, tb * 128:(tb + 1) * 128, :], writes=[b_ktm])
                    kt, b_kt = pkt.next()
                    for g in range(0, H, 8):
                        ng = min(8, H - g)
                        tp, b_tp = pbank.next()
                        tpb = tp[:].bitcast(BF16)
                        for hh in range(ng):
                            P.op("pe", "transpose", tpb[:, hh * 128:(hh + 1) * 128], ktm[:, (g + hh) * 128:(g + hh + 1) * 128],
                                 ident[:], reads=[b_ktm], writes=[b_tp])
                        P.op("act" if (g // 8) % 2 == 0 else "dve", "copy" if (g // 8) % 2 == 0 else "tensor_copy",
                             kt[:, g:g + ng, :], tpb[:, :ng * 128].rearrange("p (h t) -> p h t", t=128),
                             reads=[b_tp], writes=[b_kt])
                    P.dma("sp", kT_s[l][:, :, kvc:kvc + 128].rearrange("h p t -> p h t"), kt[:], reads=[b_kt])

        def phase_proj_mla_a(es, l, j, hT, b_hT, pbank):
            W = mla_w_in[j]
            wsl = []
            for i in range(2):
                wsl.append((sb(es, f"w{i}", [128, KC, 512], BF16), [P.buf(f"w{i}_{q}") for q in range(4)]))
            wslots = Ring(wsl)
            sf = Ring([(sb(es, f"sf{i}", [128, 512], F32), P.buf(f"sf{i}")) for i in range(3)])
            sh = Ring([(sb(es, f"sh{i}", [128, 512], BF16), P.buf(f"sh{i}")) for i in range(3)])
            skt = Ring([(sb(es, f"skt{i}", [128, 4, 128], BF16), P.buf(f"skt{i}")) for i in range(2)])
            stt = Ring([(sb(es, f"st{i}", [128, 2], F32), P.buf(f"st{i}")) for i in range(3)])
            junk = sb(es, "junk2", [128, 512], BF16)
            b_junk = P.buf("junk2")
            gq = sb(es, "gq", [128, 512], F32)
            gkv = sb(es, "gkv", [128, 512], F32)
            b_gq = P.buf("gq")
            P.dma("sp", gq[:], mla_q_norm[j:j + 1, :].partition_broadcast(128), writes=[b_gq])
            P.dma("sp", gkv[:], mla_kv_norm[j:j + 1, :].partition_broadcast(128), writes=[b_gq])
            cosm = sb(es, "cosm", [128, NTT, 32], F32)
            sinm = sb(es, "sinm", [128, NTT, 32], F32)
            b_cs = P.buf("cossin")
            P.dma("sp", cosm[:], k_cos_tm.rearrange("(b p) i -> p b i", p=128), writes=[b_cs])
            P.dma("sp", sinm[:], k_sin_tm.rearrange("(b p) i -> p b i", p=128), writes=[b_cs])
            for kind in ("cq", "ckv"):
                c0 = 0 if kind == "cq" else 512
                gain = gq if kind == "cq" else gkv
                wt, b_w = load_w(wslots, W[:, c0:c0 + 512], KC, 512)
                for ti in range(NTT):
                    pt, b_pt = proj_tm(hT, b_hT[ti], KC, ti, wt, b_w, 512, pbank)
                    st, b_st = stt.next()
                    P.op("act", "activation", out=junk[:], in_=pt[:], func=AF.Square, accum_out=st[:, 0:1],
                         reads=[b_pt], writes=[b_junk, b_st])
                    rms_rstd(st[:, 0:1], st[:, 1:2], 512, [b_st])
                    s, b_s = sh.next()
                    if kind == "ckv":
                        f, b_f = sf.next()
                        P.op("dve", "scalar_tensor_tensor", out=f[:], in0=pt[:], scalar=st[:, 1:2], in1=gain[:],
                             op0=ALU.mult, op1=ALU.mult, reads=[b_pt, b_st, b_gq], writes=[b_f])
                        P.dma("sp", ckv_o[j, ti * 128:(ti + 1) * 128, :], f[:], reads=[b_f])
                        P.op("dve", "tensor_copy", s[:], f[:], reads=[b_f], writes=[b_s])
                    else:
                        P.op("dve", "scalar_tensor_tensor", out=s[:], in0=pt[:], scalar=st[:, 1:2], in1=gain[:],
                             op0=ALU.mult, op1=ALU.mult, reads=[b_pt, b_st, b_gq], writes=[b_s])
                    tp, b_tp = pbank.next()
                    tpb = tp[:].bitcast(BF16)
                    for k in range(4):
                        P.op("pe", "transpose", tpb[:, k * 128:(k + 1) * 128], s[:, k * 128:(k + 1) * 128], ident[:],
                             reads=[b_s], writes=[b_tp])
                    kt, b_kt = skt.next()
                    P.op("act", "copy", kt[:], tpb[:, :512].rearrange("p (k t) -> p k t", t=128), reads=[b_tp], writes=[b_kt])
                    if kind == "cq":
                        P.dma("sp", cqnT_s[l][:, :, ti * 128:(ti + 1) * 128], kt[:], reads=[b_kt])
                    else:
                        for (r0, nr, kc0) in kv_cols(ti):
                            P.dma("sp", ckvT_s[l][:, :, kc0:kc0 + nr], kt[:, :, r0:r0 + nr], reads=[b_kt])
            wkr = sb(es, "wkr", [128, KC, 64], BF16)
            b_wkr = [P.buf("wkr")]
            P.dma("pool", wkr[:], W[:, 1024:1088].rearrange("(k p) n -> p k n", p=128), writes=b_wkr)
            krf = Ring([(sb(es, f"krf{i}", [128, 64], F32), P.buf(f"krf{i}")) for i in range(2)])
            krt = Ring([(sb(es, f"krt{i}", [128, 64], F32), P.buf(f"krt{i}")) for i in range(2)])
            krb = Ring([(sb(es, f"krb{i}", [128, 64], BF16), P.buf(f"krb{i}")) for i in range(2)])
            krT = Ring([(sb(es, f"krT{i}", [64, 128], BF16), P.buf(f"krT{i}")) for i in range(2)])
            for ti in range(NTT):
                pt, b_pt = proj_tm(hT, b_hT[ti], KC, ti, wkr, b_wkr, 64, pbank)
                f, b_f = krf.next()
                t_, b_t = krt.next()
                cs, sn = cosm[:, ti, :], sinm[:, ti, :]
                P.op("dve", "tensor_tensor", t_[:, 0:32], pt[:, 32:64], sn, ALU.mult, reads=[b_pt, b_cs], writes=[b_t])
                P.op("dve", "tensor_tensor", t_[:, 32:64], pt[:, 0:32], sn, ALU.mult, reads=[b_pt, b_cs], writes=[b_t])
                P.op("dve", "tensor_tensor", f[:, 0:32], pt[:, 0:32], cs, ALU.mult, reads=[b_pt, b_cs], writes=[b_f])
                P.op("dve", "tensor_tensor", f[:, 32:64], pt[:, 32:64], cs, ALU.mult, reads=[b_pt, b_cs], writes=[b_f])
                P.op("dve", "tensor_tensor", f[:, 0:32], f[:, 0:32], t_[:, 0:32], ALU.subtract, reads=[b_f, b_t], writes=[b_f])
                P.op("dve", "tensor_tensor", f[:, 32:64], f[:, 32:64], t_[:, 32:64], ALU.add, reads=[b_f, b_t], writes=[b_f])
                P.dma("sp", kr_o[j, ti * 128:(ti + 1) * 128, :], f[:], reads=[b_f])
                s, b_s = krb.next()
                P.op("dve", "tensor_copy", s[:], f[:], reads=[b_f], writes=[b_s])
                tp, b_tp = pbank.next()
                tpb = tp[:].bitcast(BF16)
                P.op("pe", "transpose", tpb[:64, 0:128], s[:, :], ident[:], reads=[b_s], writes=[b_tp])
                kt, b_kt = krT.next()
                P.op("act", "copy", kt[:], tpb[:64, 0:128], reads=[b_tp], writes=[b_kt])
                for (r0, nr, kc0) in kv_cols(ti):
                    P.dma("sp", krT_s[l][:, kc0:kc0 + nr], kt[:, r0:r0 + nr], reads=[b_kt])
            for nb in range(HD // 512):
                wt, b_w = load_w(wslots, W[:, 1088 + nb * 512:1088 + (nb + 1) * 512], KC, 512)
                h0 = nb * 4
                for ti in range(NTT):
                    pt, b_pt = proj_tm(hT, b_hT[ti], KC, ti, wt, b_w, 512, pbank)
                    s, b_s = sh.next()
                    P.op("act", "activation", out=s[:], in_=pt[:], func=AF.Silu, reads=[b_pt], writes=[b_s])
                    P.dma("sp", sg_s[l][h0:h0 + 4, ti * 128:(ti + 1) * 128, :].rearrange("h t d -> t h d"),
                          s[:].rearrange("t (h d) -> t h d", d=128), reads=[b_s])

        def phase_proj_mla_b(es, l, j, pbank):
            cqnT = sb(es, "cqnT", [128, 4, NTOK], BF16)
            ckvT = sb(es, "ckvT", [128, 4, NKV], BF16)
            b_cq = P.buf("cqnT")
            b_ck = P.buf("ckvT_new")
            P.op("pool", "memset", ckvT[:, :, T:T + 256], 0.0, writes=[b_ck])
            P.dma("sp", cqnT[:], cqnT_s[l], writes=[b_cq])
            P.dma_group("sp", [(ckvT[:, :, 0:T], ckvT_s[l][:, :, 0:T]),
                               (ckvT[:, :, c.kvA:c.kvA + 64], ckvT_s[l][:, :, c.kvA:c.kvA + 64]),
                               (ckvT[:, :, c.kvB:c.kvB + 64], ckvT_s[l][:, :, c.kvB:c.kvB + 64])], writes=[b_ck])
            cosT = sb(es, "cosT", [64, NTOK], F32)
            sinT = sb(es, "sinT", [64, NTOK], F32)
            b_cs = P.buf("cossinT")
            P.dma("sp", cosT[:], k_cosT, writes=[b_cs])
            P.dma("sp", sinT[:], k_sinT, writes=[b_cs])
            pk = Ring([(sb(es, f"pc{i}", [128, 512], BF16), P.buf(f"pc{i}")) for i in range(2)])
            pkr = Ring([(sb(es, f"pr{i}", [128, 64], BF16), P.buf(f"pr{i}")) for i in range(2)])
            pkrT = Ring([(sb(es, f"prT{i}", [64, 128], BF16), P.buf(f"prT{i}")) for i in range(2)])
            b_past = []
            for sq_i, kv0 in ((0, c.pastA), (1, c.pastB)):
                for tb in range(PAST // 128):
                    kvc = kv0 + tb * 128
                    ct, b_ct = pk.next()
                    P.dma("pool", ct[:], c_ckv[j, sq_i, tb * 128:(tb + 1) * 128, :], writes=[b_ct])
                    tp, b_tp = pbank.next()
                    tpb = tp[:].bitcast(BF16)
                    for k in range(4):
                        P.op("pe", "transpose", tpb[:, k * 128:(k + 1) * 128], ct[:, k * 128:(k + 1) * 128], ident[:],
                             reads=[b_ct], writes=[b_tp])
                    bb = P.buf("ckvT_past")
                    b_past.append(bb)
                    P.op("act", "copy", ckvT[:, :, kvc:kvc + 128], tpb[:, :512].rearrange("p (k t) -> p k t", t=128),
                         reads=[b_tp], writes=[bb])
                    rt, b_rt = pkr.next()
                    P.dma("pool", rt[:], c_kr[j, sq_i, tb * 128:(tb + 1) * 128, :], writes=[b_rt])
                    tp, b_tp = pbank.next()
                    tpb = tp[:].bitcast(BF16)
                    P.op("pe", "transpose", tpb[:64, 0:128], rt[:, :], ident[:], reads=[b_rt], writes=[b_tp])
                    rT, b_rT = pkrT.next()
                    P.op("dve", "tensor_copy", rT[:], tpb[:64, 0:128], reads=[b_tp], writes=[b_rT])
                    P.dma("sp", krT_s[l][:, kvc:kvc + 128], rT[:], reads=[b_rT])
            b_ckall = [b_ck] + b_past
            Wq = mla_w_q_up[j].rearrange("(k p) (h e) -> p k h e", p=128, e=192)
            Wkv = mla_w_kv_up[j].rearrange("(k p) (h e) -> p k h e", p=128, e=256)
            HG = 4
            wqn = Ring([(sb(es, f"wqn{i}", [128, 4, HG, 128], BF16), [P.buf(f"wqn{i}")]) for i in range(2)])
            wqr = Ring([(sb(es, f"wqr{i}", [128, 4, HG, 64], BF16), [P.buf(f"wqr{i}")]) for i in range(2)])
            wqs = Ring([(sb(es, f"wqs{i}", [128, 4, HG, 64], BF16), [P.buf(f"wqs{i}")]) for i in range(2)])
            wkn = Ring([(sb(es, f"wkn{i}", [128, 4, HG, 128], BF16), [P.buf(f"wkn{i}")]) for i in range(2)])
            wv = Ring([(sb(es, f"wv{i}", [128, 4, HG, 128], BF16), [P.buf(f"wv{i}")]) for i in range(2)])
            sq = Ring([(sb(es, f"sq{i}", [128, 512], BF16), P.buf(f"sq{i}")) for i in range(3)])
            r1 = Ring([(sb(es, f"r1{i}", [64, 512], F32), P.buf(f"r1{i}")) for i in range(2)])
            r2 = Ring([(sb(es, f"r2{i}", [64, 512], F32), P.buf(f"r2{i}")) for i in range(2)])
            sr = Ring([(sb(es, f"sr{i}", [64, 512], BF16), P.buf(f"sr{i}")) for i in range(2)])
            for hg in range(H // HG):
                hs = slice(hg * HG, (hg + 1) * HG)
                a_n, b_n = wqn.next()
                a_r, b_r = wqr.next()
                a_s, b_s_ = wqs.next()
                a_k, b_k = wkn.next()
                a_v, b_v = wv.next()
                P.dma_group("pool", [(a_n[:, k], Wq[:, k, hs, 0:128]) for k in range(4)], writes=b_n)
                P.dma_group("pool", [(a_r[:, k], Wq[:, k, hs, 128:192]) for k in range(4)], writes=b_r)
                P.dma_group("pool", [(a_s[:, k, :, 0:32], Wq[:, k, hs, 160:192]) for k in range(4)]
                            + [(a_s[:, k, :, 32:64], Wq[:, k, hs, 128:160]) for k in range(4)], writes=b_s_)
                P.dma_group("pool", [(a_k[:, k], Wkv[:, k, hs, 0:128]) for k in range(4)], writes=b_k)
                P.dma_group("pool", [(a_v[:, k], Wkv[:, k, hs, 128:256]) for k in range(4)], writes=b_v)
                for hh in range(HG):
                    h = hg * HG + hh
                    for t0 in range(0, NTOK, 512):
                        tn = min(512, NTOK - t0)
                        pt, b_pt = pbank.next()
                        for k in range(4):
                            P.op("pe", "matmul", pt[:, :tn], lhsT=a_n[:, k, hh, :], rhs=cqnT[:, k, t0:t0 + tn],
                                 start=(k == 0), stop=(k == 3), reads=[b_cq] + b_n, writes=[b_pt])
                        s, b_s = sq.next()
                        P.op("act", "copy", s[:, :tn], pt[:, :tn], reads=[b_pt], writes=[b_s])
                        P.dma("sp", qT_s[l][h, 0:128, t0:t0 + tn], s[:, :tn], reads=[b_s])
                        p1, b_p1 = pbank.next()
                        p2, b_p2 = pbank.next()
                        for k in range(4):
                            P.op("pe", "matmul", p1[:64, :tn], lhsT=a_r[:, k, hh, :], rhs=cqnT[:, k, t0:t0 + tn],
                                 start=(k == 0), stop=(k == 3), reads=[b_cq] + b_r, writes=[b_p1])
                        for k in range(4):
                            P.op("pe", "matmul", p2[:64, :tn], lhsT=a_s[:, k, hh, :], rhs=cqnT[:, k, t0:t0 + tn],
                                 start=(k == 0), stop=(k == 3), reads=[b_cq] + b_s_, writes=[b_p2])
                        t1, b_t1 = r1.next()
                        t2, b_t2 = r2.next()
                        P.op("dve", "tensor_tensor", t1[:, :tn], p1[:64, :tn], cosT[:, t0:t0 + tn], ALU.mult,
                             reads=[b_p1, b_cs], writes=[b_t1])
                        P.op("dve", "tensor_tensor", t2[:, :tn], p2[:64, :tn], sinT[:, t0:t0 + tn], ALU.mult,
                             reads=[b_p2, b_cs], writes=[b_t2])
                        o_, b_o = sr.next()
                        P.op("pool", "tensor_tensor", o_[:, :tn], t1[:, :tn], t2[:, :tn], ALU.add,
                             reads=[b_t1, b_t2], writes=[b_o])
                        P.dma("sp", qT_s[l][h, 128:192, t0:t0 + tn], o_[:, :tn], reads=[b_o])
                    for t0 in range(0, NKV, 512):
                        tn = min(512, NKV - t0)
                        pt, b_pt = pbank.next()
                        for k in range(4):
                            P.op("pe", "matmul", pt[:, :tn], lhsT=a_k[:, k, hh, :], rhs=ckvT[:, k, t0:t0 + tn],
                                 start=(k == 0), stop=(k == 3), reads=b_ckall + b_k, writes=[b_pt])
                        s, b_s = sq.next()
                        P.op("dve" if (t0 // 512) % 2 else "act", "tensor_copy" if (t0 // 512) % 2 else "copy",
                             s[:, :tn], pt[:, :tn], reads=[b_pt], writes=[b_s])
                        P.dma("sp", kT_s[l][h, :, t0:t0 + tn], s[:, :tn], reads=[b_s])
                for blk in range(NBLK):
                    pt, b_pt = pbank.next()
                    for k in range(4):
                        P.op("pe", "matmul", pt[:, :HG * 128], lhsT=ckvT[:, k, blk * 128:(blk + 1) * 128],
                             rhs=a_v[:, k].rearrange("p h d -> p (h d)"), start=(k == 0), stop=(k == 3),
                             reads=b_ckall + b_v, writes=[b_pt])
                    s, b_s = sq.next()
                    P.op("dve" if blk % 2 else "act", "tensor_copy" if blk % 2 else "copy", s[:, :HG * 128], pt[:, :HG * 128],
                         reads=[b_pt], writes=[b_s])
                    P.dma("sp", v_s[l][hs, :, blk, :].rearrange("h p d -> p h d"),
                          s[:, :HG * 128].rearrange("t (h d) -> t h d", d=128), reads=[b_s])

        def qtiles():
            qt = []
            for i in range(NT):
                qt.append(dict(tok0=i * 128, nq=128, segs=[(0, (i + 1) * 128)], diag=128, og=("p", i)))
            qt.append(dict(tok0=T, nq=64, segs=[(c.pastA, PAST), (c.kvA, 64)], diag=64, og=("s", 0)))
            qt.append(dict(tok0=T + 64, nq=64, segs=[(c.pastB, PAST), (c.kvB, 64)], diag=64, og=("s", 1)))
            return qt

        def chunks_of(q):
            ch = []
            nseg = len(q["segs"])
            for si, (c0, n) in enumerate(q["segs"]):
                for o in range(0, n, 1024):
                    nn = min(1024, n - o)
                    last = (si == nseg - 1) and (o + nn == n)
                    ch.append((c0 + o, nn, last))
            return ch

        def phase_attn(es, l, j, is_mla, og, ogs, b_og, b_ogs):
            kv_valid = [(0, T + 64), (T + 128, T + 192), (T + 256, NKV)]
            scale = (192.0 if is_mla else 128.0) ** -0.5
            Sps = Ring([(ps(es, f"S{i}", [128, 1024]), P.buf(f"S{i}", True)) for i in range(2)])
            Tps = Ring([(ps(es, f"Tp{i}", [128, 512]), P.buf(f"Tp{i}", True)) for i in range(2)])
            Ops = Ring([(ps(es, f"O{i}", [128, 512]), P.buf(f"O{i}", True)) for i in range(2)])
            QT = qtiles()
            n_stages = 7 if is_mla else 9
            per_head = sum(len(chunks_of(q)) for q in QT)
            RH = 2 if per_head >= n_stages + 2 else 3
            kT = Ring([(sb(es, f"kT{i}", [128, NKV], BF16), P.buf(f"kT{i}")) for i in range(RH)])
            vv = Ring([(sb(es, f"vv{i}", [128, NBLK, 128], BF16), P.buf(f"vv{i}")) for i in range(RH)])
            qT = Ring([(sb(es, f"qT{i}", [128, NTOK], BF16), P.buf(f"qT{i}")) for i in range(RH)])
            sgp = Ring([(sb(es, f"sgp{i}", [128, NT, 128], BF16), P.buf(f"sgp{i}")) for i in range(RH)])
            sgs = Ring([(sb(es, f"sgs{i}", [64, 2, 128], BF16), P.buf(f"sgs{i}")) for i in range(RH)])
            nmask = sb(es, "nmask", [128, 128], BF16)
            b_nm = P.buf("nmask")
            P.dma("sp", nmask[:], k_nm_mla if is_mla else k_nm_sb, writes=[b_nm])
            if is_mla:
                qr = Ring([(sb(es, f"qr{i}", [64, NTOK], BF16), P.buf(f"qr{i}")) for i in range(RH)])
                krT = sb(es, "krTall", [64, NKV], BF16)
                b_krT = P.buf("krTall")
                P.dma_group("sp", [(krT[:, a_:b_], krT_s[l][:, a_:b_]) for (a_, b_) in kv_valid], writes=[b_krT])
                Ssb = Ring([(sb(es, f"Ssb{i}", [128, 1024], F32), P.buf(f"Ssb{i}")) for i in range(4)])
                stat = Ring([(sb(es, f"stat{i}", [128, 16], F32), P.buf(f"stat{i}")) for i in range(6)])
                cmb = Ring([(sb(es, f"cmb{i}", [128, 128], F32), P.buf(f"cmb{i}")) for i in range(2)])
            else:
                Eb = Ring([(sb(es, f"E{i}", [128, 1024], F32), P.buf(f"E{i}")) for i in range(5)])
                Fb = Ring([(sb(es, f"F{i}", [128, 1024], F32), P.buf(f"F{i}")) for i in range(5)])
                Cb = Ring([(sb(es, f"C{i}", [128, 1024], F32), P.buf(f"C{i}")) for i in range(3)])
                ones = sb(es, "ones", [128, 1024], BF16)
                b_ones = P.buf("ones")
                P.op("pool", "memset", ones[:], 1.0, writes=[b_ones])
            Ab = Ring([(sb(es, f"A{i}", [128, 1024], BF16), P.buf(f"A{i}")) for i in range(3)])
            ATb = Ring([(sb(es, f"AT{i}", [128, 8, 128], BF16), P.buf(f"AT{i}")) for i in range(3)])

            heads = {}

            def load_head(h):
                hb = {}
                hb["k"], hb["b_k"] = kT.next()
                hb["v"], hb["b_v"] = vv.next()
                hb["q"], hb["b_q"] = qT.next()
                hb["sgp"], hb["b_sgp"] = sgp.next()
                hb["sgs"], hb["b_sgs"] = sgs.next()
                P.dma("sp", hb["q"][:], qT_s[l][h, 0:128, :], writes=[hb["b_q"]])
                if is_mla:
                    hb["qr"], hb["b_qr"] = qr.next()
                    P.dma("sp", hb["qr"][:], qT_s[l][h, 128:192, :], writes=[hb["b_qr"]])
                P.dma_group("sp", [(hb["k"][:, a_:b_], kT_s[l][h, :, a_:b_]) for (a_, b_) in kv_valid], writes=[hb["b_k"]])
                P.dma_group("sp", [(hb["v"][:, 0:NT, :], v_s[l][h, :, 0:NT, :]),
                                   (hb["v"][0:64, NT:NT + 2, :], v_s[l][h, 0:64, NT:NT + 2, :]),
                                   (hb["v"][:, NT + 2:NBLK, :], v_s[l][h, :, NT + 2:NBLK, :])], writes=[hb["b_v"]])
                P.dma("sp", hb["sgp"][:], sg_s[l][h, 0:T, :].rearrange("(b p) d -> p b d", p=128), writes=[hb["b_sgp"]])
                P.dma("sp", hb["sgs"][:], sg_s[l][h, T:T + 128, :].rearrange("(s p) d -> p s d", p=64), writes=[hb["b_sgs"]])
                heads[h] = hb

            items = []
            for h in range(H):
                for qi, q in enumerate(QT):
                    chs = chunks_of(q)
                    order = chs if is_mla else list(reversed(chs))
                    qs = dict(nch=len(order))
                    for ci, (col0, n, isd) in enumerate(order):
                        items.append(dict(h=h, q=q, qs=qs, ci=ci, nch=len(order), col0=col0, n=n, isd=isd,
                                          first_of_head=(qi == 0 and ci == 0)))

            def dst_of(it):
                q, hb, h = it["q"], heads[it["h"]], it["h"]
                nq = q["nq"]
                if q["og"][0] == "p":
                    return (og[:nq, q["og"][1], h * 128:(h + 1) * 128], b_og[q["og"][1]], hb["sgp"][:nq, q["og"][1], :], hb["b_sgp"])
                return (ogs[:nq, q["og"][1], h * 128:(h + 1) * 128], b_ogs[q["og"][1]], hb["sgs"][:nq, q["og"][1], :], hb["b_sgs"])

            def st_qk(it):
                h = it["h"]
                hb = heads[h]
                q, n, col0 = it["q"], it["n"], it["col0"]
                nq = q["nq"]
                qc = slice(q["tok0"], q["tok0"] + nq)
                dn = q["diag"]
                masked = it["isd"] and (dn == 128 or not is_mla)
                zp, b_zp = Sps.next()
                it["zp"], it["b_zp"] = zp, b_zp
                def qk(c0, c1, with_mask):
                    P.op("pe", "matmul", zp[:nq, c0:c1], lhsT=hb["q"][:, qc], rhs=hb["k"][:, col0 + c0:col0 + c1],
                         start=True, stop=not (is_mla or with_mask), reads=[hb["b_q"], hb["b_k"]], writes=[b_zp])
                    if is_mla:
                        P.op("pe", "matmul", zp[:nq, c0:c1], lhsT=hb["qr"][:, qc], rhs=krT[:, col0 + c0:col0 + c1],
                             start=False, stop=not with_mask, reads=[hb["b_qr"], b_krT], writes=[b_zp])
                    if with_mask:
                        P.op("pe", "matmul", zp[:nq, c0:c1], lhsT=ident[:nq, :nq], rhs=nmask[:nq, :dn],
                             start=False, stop=True, reads=[b_const, b_nm], writes=[b_zp])

                for m in range(0, n, 512):
                    mm = min(512, n - m)
                    lastg = (m + mm == n)
                    if masked and lastg:
                        if n - dn > m:
                            qk(m, n - dn, False)
                        qk(n - dn, n, True)
                    else:
                        qk(m, m + mm, False)

            def st_T(it):
                q, n = it["q"], it["n"]
                nq = q["nq"]
                A, b_A = it["A"], it["b_A"]
                tp, b_tp = Tps.next()
                it["tp"], it["b_tp"] = tp, b_tp
                tpb = tp[:].bitcast(BF16)
                nb = (n + 127) // 128
                it["nks"] = []
                for bi in range(nb):
                    nk = min(128, n - bi * 128)
                    it["nks"].append(nk)
                    P.op("pe", "transpose", tpb[:nk, bi * 128:bi * 128 + nq], A[:nq, bi * 128:bi * 128 + nk],
                         ident[:nq, :nq], reads=[b_A, b_const], writes=[b_tp])

            cnt = [0]

            def st_copy(it):
                q = it["q"]
                nq = q["nq"]
                nb = len(it["nks"])
                nk0 = it["nks"][0]
                tpb = it["tp"][:].bitcast(BF16)
                at, b_at = ATb.next()
                it["at"], it["b_at"] = at, b_at
                cnt[0] += 1
                eng = "dve" if (is_mla and cnt[0] % 3 == 0) else "act"
                P.op(eng, "copy" if eng == "act" else "tensor_copy", at[:nk0, :nb, :nq],
                     tpb[:nk0, :nb * 128].rearrange("p (b t) -> p b t", t=128)[:, :, :nq], reads=[it["b_tp"]], writes=[b_at])

            def st_pv(it):
                q, n, col0 = it["q"], it["n"], it["col0"]
                nq = q["nq"]
                hb = heads[it["h"]]
                qs = it["qs"]
                if it["ci"] == 0:
                    qs["o"], qs["b_o"] = Ops.next()
                o_ps, b_o = qs["o"], qs["b_o"]
                oc = (it["ci"] * 128) if is_mla else 0
                nb = len(it["nks"])
                for bi in range(nb):
                    blk = (col0 + bi * 128) // 128
                    if is_mla:
                        st_, sp_ = (bi == 0), (bi == nb - 1)
                    else:
                        st_, sp_ = (it["ci"] == 0 and bi == 0), (it["ci"] == it["nch"] - 1 and bi == nb - 1)
                    P.op("pe", "matmul", o_ps[:nq, oc:oc + 128], lhsT=it["at"][:it["nks"][bi], bi, :nq],
                         rhs=hb["v"][:it["nks"][bi], blk, :], start=st_, stop=sp_, reads=[it["b_at"], hb["b_v"]], writes=[b_o])

            if not is_mla:
                def st_exp(it):
                    nq, n = it["q"]["nq"], it["n"]
                    E, b_E = Eb.next()
                    F_, b_F = Fb.next()
                    it["E"], it["b_E"], it["F"], it["b_F"] = E, b_E, F_, b_F
                    P.op("act", "activation", out=E[:nq, :n], in_=it["zp"][:nq, :n], func=AF.Exp, scale=scale,
                         reads=[it["b_zp"]], writes=[b_E])
                    P.op("act", "activation", out=F_[:nq, :n], in_=E[:nq, :n], func=AF.Ln, bias=1.0, scale=1.0,
                         reads=[b_E], writes=[b_F])

                def st_scan(it):
                    nq, n = it["q"]["nq"], it["n"]
                    qs = it["qs"]
                    Cs, b_C = Cb.next()
                    it["C"], it["b_C"] = Cs, b_C
                    carry = qs.get("carry")
                    rd = [it["b_F"], b_ones] + ([qs["b_carry"]] if carry is not None else [])
                    P.op("dve", "tensor_tensor_scan", Cs[:nq, 0:n][:, ::-1], ones[:nq, 0:n][:, ::-1], it["F"][:nq, 0:n][:, ::-1],
                         carry if carry is not None else 0.0, ALU.mult, ALU.add, reads=rd, writes=[b_C])
                    qs["carry"], qs["b_carry"] = Cs[:nq, 0:1], b_C

                def st_expf(it):
                    nq, n = it["q"]["nq"], it["n"]
                    P.op("act", "activation", out=it["F"][:nq, :n], in_=it["C"][:nq, :n], func=AF.Exp, scale=-1.0,
                         reads=[it["b_C"]], writes=[it["b_F"]])

                def st_mul(it):
                    nq, n = it["q"]["nq"], it["n"]
                    A, b_A = Ab.next()
                    it["A"], it["b_A"] = A, b_A
                    P.op("dve", "tensor_tensor", A[:nq, :n], it["E"][:nq, :n], it["F"][:nq, :n], ALU.mult,
                         reads=[it["b_E"], it["b_F"]], writes=[b_A])

                def st_fin(it):
                    if it["ci"] != it["nch"] - 1:
                        return
                    nq = it["q"]["nq"]
                    og_dst, b_dst, sg_src, b_sg = dst_of(it)
                    P.op("dve", "tensor_tensor", og_dst, it["qs"]["o"][:nq, :128], sg_src, ALU.mult,
                         reads=[it["qs"]["b_o"], b_sg], writes=[b_dst])

                stages = [st_qk, st_exp, st_scan, st_expf, st_mul, st_T, st_copy, st_pv, st_fin]
            else:
                def st_max(it):
                    nq, n, ci = it["q"]["nq"], it["n"], it["ci"]
                    qs = it["qs"]
                    if ci == 0:
                        qs["st"], qs["b_st"] = stat.next()
                    st, b_st = qs["st"], qs["b_st"]
                    S, b_S = Ssb.next()
                    it["S"], it["b_S"] = S, b_S
                    P.op("dve", "tensor_scalar", S[:nq, :n], it["zp"][:nq, :n], scale, None, ALU.mult, ALU.max,
                         st[:nq, ci:ci + 1], reads=[it["b_zp"]], writes=[b_S, b_st])
                    P.op("dve", "tensor_scalar", st[:nq, 2 + ci:3 + ci], st[:nq, ci:ci + 1], -1.0, None, ALU.mult,
                         reads=[b_st], writes=[b_st])

                def st_expp(it):
                    nq, n, ci = it["q"]["nq"], it["n"], it["ci"]
                    st, b_st = it["qs"]["st"], it["qs"]["b_st"]
                    A, b_A = Ab.next()
                    it["A"], it["b_A"] = A, b_A
                    P.op("act", "activation", out=A[:nq, :n], in_=it["S"][:nq, :n], func=AF.Exp,
                         bias=st[:nq, 2 + ci:3 + ci], scale=1.0, accum_out=st[:nq, 4 + ci:5 + ci],
                         reads=[it["b_S"], b_st], writes=[b_A, b_st])

                def st_fin(it):
                    if it["ci"] != it["nch"] - 1:
                        return
                    nq = it["q"]["nq"]
                    qs = it["qs"]
                    st, b_st, o_ps, b_o = qs["st"], qs["b_st"], qs["o"], qs["b_o"]
                    og_dst, b_dst, sg_src, b_sg = dst_of(it)
                    if it["nch"] == 1:
                        P.op("dve", "reciprocal", st[:nq, 6:7], st[:nq, 4:5], reads=[b_st], writes=[b_st])
                        P.op("dve", "scalar_tensor_tensor", out=og_dst, in0=o_ps[:nq, 0:128], scalar=st[:nq, 6:7], in1=sg_src,
                             op0=ALU.mult, op1=ALU.mult, reads=[b_o, b_st, b_sg], writes=[b_dst])
                        return
                    P.op("dve", "tensor_tensor", st[:nq, 6:7], st[:nq, 2:3], st[:nq, 3:4], ALU.min, reads=[b_st], writes=[b_st])
                    P.op("act", "activation", out=st[:nq, 8:10], in_=st[:nq, 0:2], func=AF.Exp, bias=st[:nq, 6:7], scale=1.0,
                         reads=[b_st], writes=[b_st])
                    P.op("dve", "tensor_tensor", st[:nq, 10:12], st[:nq, 8:10], st[:nq, 4:6], ALU.mult, reads=[b_st], writes=[b_st])
                    P.op("dve", "tensor_tensor", st[:nq, 12:13], st[:nq, 10:11], st[:nq, 11:12], ALU.add, reads=[b_st], writes=[b_st])
                    P.op("dve", "reciprocal", st[:nq, 12:13], st[:nq, 12:13], reads=[b_st], writes=[b_st])
                    P.op("dve", "tensor_scalar", st[:nq, 14:16], st[:nq, 8:10], st[:nq, 12:13], None, ALU.mult,
                         reads=[b_st], writes=[b_st])
                    t_, b_t = cmb.next()
                    P.op("dve", "tensor_scalar", t_[:nq, :], o_ps[:nq, 0:128], st[:nq, 14:15], None, ALU.mult,
                         reads=[b_o, b_st], writes=[b_t])
                    P.op("dve", "scalar_tensor_tensor", out=t_[:nq, :], in0=o_ps[:nq, 128:256], scalar=st[:nq, 15:16], in1=t_[:nq, :],
                         op0=ALU.mult, op1=ALU.add, reads=[b_o, b_st, b_t], writes=[b_t])
                    P.op("dve", "tensor_tensor", og_dst, t_[:nq, :], sg_src, ALU.mult, reads=[b_t, b_sg], writes=[b_dst])

                stages = [st_qk, st_max, st_expp, st_T, st_copy, st_pv, st_fin]

            ns = len(stages)
            assert ns == n_stages
            first_item = {}
            last_item = {}
            for k, it in enumerate(items):
                first_item.setdefault(it["h"], k)
                last_item[it["h"]] = k
            loaded = -1
            for step in range(len(items) + ns - 1):
                hcur = items[min(step, len(items) - 1)]["h"]
                while loaded < min(hcur + 1, H - 1):
                    g = loaded + 1
                    safe = (g - RH < 0) or (last_item[g - RH] + ns - 1 < step)
                    if not safe:
                        assert g > hcur, "head-buffer ring too shallow"
                        break
                    load_head(g)
                    loaded = g
                for si in range(ns):
                    k = step - si
                    if 0 <= k < len(items):
                        stages[si](items[k])

        def phase_out(es, l, j, is_mla, og, ogs, b_og, b_ogs):
            HC = HD // 128
            Wo = (mla_w_o if is_mla else sb_w_o)[j]
            x_src = x_in if l == 0 else x_s[l]
            last = (l == c.DEPTH - 1)
            pbank = Ring([(ps(es, f"pb{i}", [128, 512]), P.buf(f"pb{i}", True)) for i in range(8)])
            wo = sb(es, "wo", [128, HC, D], BF16)
            b_wo = [P.buf(f"wo{i}") for i in range(4)]
            kk = max(1, HC // 4)
            for qi, k0 in enumerate(range(0, HC, kk)):
                P.dma("pool", wo[:, k0:k0 + kk, :], Wo[k0 * 128:(k0 + kk) * 128, :].rearrange("(k p) n -> p k n", p=128),
                      writes=[b_wo[qi]])
            xt = Ring([(sb(es, f"xo{i}", [128, D], F32), P.buf(f"xo{i}")) for i in range(2)])
            xn = Ring([(sb(es, f"xn{i}", [128, D], F32), P.buf(f"xn{i}")) for i in range(2)])
            oT = Ring([(sb(es, f"oT{i}", [128, HC, 128], BF16), P.buf(f"oT{i}")) for i in range(2)])
            if last:
                fg = sb(es, "fg", [128, D], F32)
                b_fg = P.buf("fg")
                P.dma("sp", fg[:], final_gain.partition_broadcast(128), writes=[b_fg])
                yt = Ring([(sb(es, f"yt{i}", [128, D], F32), P.buf(f"yt{i}")) for i in range(2)])
                stt = Ring([(sb(es, f"fst{i}", [128, 2], F32), P.buf(f"fst{i}")) for i in range(2)])
                junk = sb(es, "junk3", [128, D], BF16)
                b_junk = P.buf("junk3")
            for ti in range(NTT):
                x_, b_x = xt.next()
                P.dma("sp", x_[:], x_src[ti * 128:(ti + 1) * 128, :], writes=[b_x])
                ot, b_ot = oT.next()
                for g in range(0, HC, 8):
                    ng = min(8, HC - g)
                    tp, b_tp = pbank.next()
                    tpb = tp[:].bitcast(BF16)
                    for k in range(ng):
                        kc = slice((g + k) * 128, (g + k + 1) * 128)
                        if ti < NT:
                            P.op("pe", "transpose", tpb[:, k * 128:(k + 1) * 128], og[:, ti, kc], ident[:],
                                 reads=[b_og[ti]], writes=[b_tp])
                        else:
                            for s_ in range(2):
                                P.op("pe", "transpose", tpb[:, k * 128 + s_ * 64:k * 128 + s_ * 64 + 64], ogs[:, s_, kc],
                                     ident[:64, :64], reads=[b_ogs[s_]], writes=[b_tp])
                    eng = "act" if (g // 8) % 2 == 0 else "dve"
                    P.op(eng, "copy" if eng == "act" else "tensor_copy", ot[:, g:g + ng, :],
                         tpb[:, :ng * 128].rearrange("p (k t) -> p k t", t=128), reads=[b_tp], writes=[b_ot])
                xn_, b_xn = xn.next()
                for nb in range(D // 512):
                    pt, b_pt = pbank.next()
                    for k in range(HC):
                        P.op("pe", "matmul", pt[:, :512], lhsT=ot[:, k, :], rhs=wo[:, k, nb * 512:(nb + 1) * 512],
                             start=(k == 0), stop=(k == HC - 1), reads=[b_ot] + b_wo, writes=[b_pt])
                    P.op("dve", "tensor_tensor", xn_[:, nb * 512:(nb + 1) * 512], pt[:, :512], x_[:, nb * 512:(nb + 1) * 512], ALU.add,
                         reads=[b_pt, b_x], writes=[b_xn])
                if not last:
                    P.dma("sp", x_s[l + 1][ti * 128:(ti + 1) * 128, :], xn_[:], reads=[b_xn])
                else:
                    st, b_st = stt.next()
                    P.op("act", "activation", out=junk[:], in_=xn_[:], func=AF.Square, accum_out=st[:, 0:1],
                         reads=[b_xn], writes=[b_junk, b_st])
                    rms_rstd(st[:, 0:1], st[:, 1:2], D, [b_st])
                    y_, b_y = yt.next()
                    P.op("dve", "scalar_tensor_tensor", out=y_[:], in0=xn_[:], scalar=st[:, 1:2], in1=fg[:],
                         op0=ALU.mult, op1=ALU.mult, reads=[b_xn, b_st, b_fg], writes=[b_y])
                    P.dma("sp", y_o[ti * 128:(ti + 1) * 128, :], y_[:], reads=[b_y])

        for l in range(DEPTH):
            is_mla = (l % 2 == 0)
            j = l // 2
            with ExitStack() as es:
                pbank = Ring([(ps(es, f"pb{i}", [128, 512]), P.buf(f"pb{i}", True)) for i in range(8)])
                hT = sb(es, "hT", [128, KC, NTOK], BF16)
                b_hT = [P.buf(f"hT{ti}") for ti in range(NTT)]
                with ExitStack() as es1:
                    phase_norm(es1, l, hT, b_hT, pbank)
                    P.flush()
                with ExitStack() as es2:
                    if is_mla:
                        phase_proj_mla_a(es2, l, j, hT, b_hT, pbank)
                    else:
                        phase_proj_sb(es2, l, j, hT, b_hT, pbank)
                    P.flush()
            if is_mla:
                with ExitStack() as es:
                    pbank = Ring([(ps(es, f"pb{i}", [128, 512]), P.buf(f"pb{i}", True)) for i in range(8)])
                    phase_proj_mla_b(es, l, j, pbank)
                    P.flush()
            with ExitStack() as es:
                og = sb(es, "og", [128, NT, HD], BF16)
                ogs = sb(es, "ogs", [64, 2, HD], BF16)
                b_og = [P.buf(f"og{i}") for i in range(NT)]
                b_ogs = [P.buf(f"ogs{i}") for i in range(2)]
                with ExitStack() as es3:
                    phase_attn(es3, l, j, is_mla, og, ogs, b_og, b_ogs)
                    P.flush()
                with ExitStack() as es4:
                    phase_out(es4, l, j, is_mla, og, ogs, b_og, b_ogs)
                    P.flush()
    return nc


def _consts(cfg):
    c = cfg
    ident = np.eye(128, dtype=np.float32).astype(ml_dtypes.bfloat16)
    t = np.arange(128)
    nm_sb = np.where(t[None, :] >= t[:, None], -30000.0, 0.0).astype(np.float32).astype(ml_dtypes.bfloat16)
    nm_mla = np.where((t[:, None] < 64) & (t[None, :] >= 64), -30000.0, 0.0).astype(np.float32).astype(ml_dtypes.bfloat16)
    half = 32
    inv = (1.0 / (np.float32(10000.0) ** (np.arange(half, dtype=np.float32) * np.float32(2.0 / 64)))).astype(np.float32)
    pos = np.concatenate([np.arange(c.T), c.PAST + np.arange(64), c.PAST + np.arange(64)]).astype(np.float32)
    ang = (pos[:, None] * inv[None, :]).astype(np.float32)
    cos, sin = np.cos(ang).astype(np.float32), np.sin(ang).astype(np.float32)
    cosT = np.ascontiguousarray(np.concatenate([cos, cos], axis=1).T)
    sinT = np.ascontiguousarray(np.concatenate([-sin, sin], axis=1).T)
    return dict(k_ident=ident, k_nm_sb=nm_sb, k_nm_mla=nm_mla, k_cos_tm=cos, k_sin_tm=sin, k_cosT=cosT, k_sinT=sinT)


def make_in_maps(cfg, n_cores, inp):
    c = cfg
    consts = _consts(c)
    f = lambda a: np.ascontiguousarray(np.asarray(a, dtype=np.float32))
    shared = dict(
        ln_gain=f(inp["ln_gain"]), final_gain=f(inp["final_gain"]).reshape(1, -1),
        mla_w_in=f(inp["mla_w_in"]), mla_q_norm=f(inp["mla_q_norm"]), mla_kv_norm=f(inp["mla_kv_norm"]),
        mla_w_q_up=f(inp["mla_w_q_up"]), mla_w_kv_up=f(inp["mla_w_kv_up"]), mla_w_o=f(inp["mla_w_o"]),
        sb_w_in=f(inp["sb_w_in"]), sb_w_o=f(inp["sb_w_o"]), **consts)
    xp, xs = np.asarray(inp["x_prompt"]), np.asarray(inp["x_sample"])
    ckv, kr = np.asarray(inp["cache_mla_ckv"]), np.asarray(inp["cache_mla_krope"])
    sk, sv = np.asarray(inp["cache_sb_k"]), np.asarray(inp["cache_sb_v"])
    maps = []
    for b in range(n_cores):
        m = dict(shared)
        m["x_in"] = f(np.concatenate([xp[b], xs[2 * b], xs[2 * b + 1]], axis=0))
        m["c_ckv"] = f(ckv[:, 2 * b:2 * b + 2])
        m["c_kr"] = f(kr[:, 2 * b:2 * b + 2])
        m["c_sbk"] = f(sk[:, 2 * b:2 * b + 2].reshape(c.N_SB, 2, c.PAST, c.HD))
        m["c_sbv"] = f(sv[:, 2 * b:2 * b + 2].reshape(c.N_SB, 2, c.PAST, c.HD))
        maps.append(m)
    return maps


def assemble(cfg, res):
    c = cfg
    T, H = c.T, c.H
    n = len(res)
    st = lambda k: np.stack([np.asarray(r[k]) for r in res], axis=0)
    y, ckv, kr, sbk, sbv = st("y"), st("ckv_o"), st("kr_o"), st("sbk_o"), st("sbv_o")

    def split_tok(a, tok_axis):
        p = np.take(a, np.arange(T), axis=tok_axis)
        s0 = np.take(a, np.arange(T, T + 64), axis=tok_axis)
        s1 = np.take(a, np.arange(T + 64, T + 128), axis=tok_axis)
        s = np.stack([s0, s1], axis=1)
        s = s.reshape((2 * n,) + s.shape[2:])
        return p, s

    yp, ys = split_tok(y, 1)
    cp, cs = split_tok(ckv, 2)
    kp, ks = split_tok(kr, 2)
    skp, sks = split_tok(sbk, 2)
    svp, svs = split_tok(sbv, 2)
    mv = lambda a: np.ascontiguousarray(np.moveaxis(a, 1, 0))
    hd = lambda a: a.reshape(a.shape[:-1] + (H, 128))
    return (np.ascontiguousarray(yp), np.ascontiguousarray(ys), mv(cp), mv(kp), hd(mv(skp)), hd(mv(svp)),
            mv(cs), mv(ks), hd(mv(sks)), hd(mv(svs)))


_NC_CACHE = {}


def kernel(**inputs):
    cfg = Cfg()
    n = 8
    if "nc" not in _NC_CACHE:
        _NC_CACHE["nc"] = build(cfg)
    nc = _NC_CACHE["nc"]
    in_maps = make_in_maps(cfg, n, inputs)
    res = run_bass_kernel_spmd(nc, in_maps, core_ids=list(range(n)))
    return assemble(cfg, res.results)
```

```python
import numpy as np
from contextlib import ExitStack
import concourse.bass as bass
import concourse.mybir as mybir
from concourse.bass_utils import run_bass_kernel_spmd
import ml_dtypes

F32 = mybir.dt.float32
BF16 = mybir.dt.bfloat16
AF = mybir.ActivationFunctionType
ALU = mybir.AluOpType
AX = mybir.AxisListType
NEG = -1e30
EPS = 1e-6


class Cfg:
    def __init__(self, D=2048, T=2048, TS=64, PAST=1024, H=16, DEPTH=4):
        self.D, self.T, self.TS, self.PAST, self.H, self.DEPTH = D, T, TS, PAST, H, DEPTH
        self.KC = D // 128
        self.NT = T // 128
        self.NTT = self.NT + 1
        self.NTOK = T + 128
        self.HD = H * 128
        self.QL = 512
        self.KVL = 512
        self.MLA_IN = 512 + 512 + 64 + self.HD
        self.NKV = T + 256 + 2 * PAST
        self.NBLK = self.NKV // 128
        self.N_MLA = (DEPTH + 1) // 2
        self.N_SB = DEPTH // 2
        self.kvA, self.kvB = T, T + 128
        self.pastA, self.pastB = T + 256, T + 256 + PAST
        assert TS == 64 and T % 128 == 0 and PAST % 128 == 0


class Buf:
    __slots__ = ("name", "wc", "wd", "rc", "rd", "excl")

    def __init__(self, name="", excl=False):
        self.name = name
        self.excl = excl
        self.wc = {}
        self.wd = []
        self.rc = {}
        self.rd = []

    def clear(self):
        self.wc = {}
        self.wd = []
        self.rc = {}
        self.rd = []


class Op:
    __slots__ = ("eng", "meth", "args", "kw", "deps", "needed", "sem", "val", "dma")

    def __init__(self, eng, meth, args, kw, dma):
        self.eng, self.meth, self.args, self.kw, self.dma = eng, meth, args, kw, dma
        self.deps = []
        self.needed = dma
        self.sem = None
        self.val = 0


class Prog:
    ENGS = ["pe", "act", "dve", "pool", "sp"]

    def __init__(self, nc, es, n_dma_sems=20):
        self.nc = nc
        self.h = {"pe": nc.tensor, "act": nc.scalar, "dve": nc.vector, "pool": nc.gpsimd, "sp": nc.sync}
        self.esem = {e: es.enter_context(nc.semaphore("c_" + e)) for e in self.ENGS}
        self.ecnt = {e: 0 for e in self.ENGS}
        self.dsem = {}
        for q in ("sp", "pool", "act"):
            self.dsem[q] = [[es.enter_context(nc.semaphore(f"d_{q}{i}")), 0, None] for i in range(n_dma_sems)]
        self.dptr = {q: 0 for q in self.dsem}
        self.ops = []
        self.bufs = []
        self.waited = {e: {} for e in self.ENGS}
        self.nblock = 0
        self.max_blocks = None

    def buf(self, name="", excl=False):
        b = Buf(name, excl)
        self.bufs.append(b)
        return b

    def _deps(self, eng, dma, reads, writes):
        deps = []
        for b in reads:
            for d in b.wc.values():
                if not (d.eng == eng == "pe"):
                    deps.append(d)
            deps.extend(b.wd)
            if b.excl:
                for d in b.rc.values():
                    if d.eng != eng:
                        deps.append(d)
        for b in writes:
            for d in b.wc.values():
                if dma or d.eng != eng or eng != "pe":
                    deps.append(d)
            deps.extend(b.wd)
            for d in b.rc.values():
                if dma or d.eng != eng or eng != "pe":
                    deps.append(d)
            deps.extend(b.rd)
        out, seen = [], set()
        for d in deps:
            if id(d) not in seen:
                seen.add(id(d))
                d.needed = True
                out.append(d)
        return out

    def _register(self, ops, eng, dma, reads, writes):
        for b in writes:
            b.clear()
            for o in ops:
                if dma:
                    b.wd.append(o)
                else:
                    b.wc[eng] = o
        for b in reads:
            if b in writes:
                continue
            for o in ops:
                if dma:
                    b.rd.append(o)
                else:
                    b.rc[eng] = o

    def _emit(self, eng, meth, args, kw, reads, writes, dma):
        o = Op(eng, meth, args, kw, dma)
        o.deps = self._deps(eng, dma, reads, writes)
        self._register([o], eng, dma, reads, writes)
        self.ops.append(o)
        return o

    def dma_group(self, q, pairs, reads=(), writes=()):
        deps = self._deps(q, True, reads, writes)
        ops = []
        for (out, in_) in pairs:
            o = Op(q, "dma_start", (), dict(out=out, in_=in_), True)
            o.deps = list(deps)
            ops.append(o)
            self.ops.append(o)
        self._register(ops, q, True, reads, writes)
        return ops

    def op(self, eng, meth, *args, reads=(), writes=(), **kw):
        return self._emit(eng, meth, args, kw, reads, writes, False)

    def dma(self, q, out, in_, reads=(), writes=(), **kw):
        kw = dict(kw)
        kw["out"] = out
        kw["in_"] = in_
        return self._emit(q, "dma_start", (), kw, reads, writes, True)

    def flush(self):
        nc = self.nc
        if self.max_blocks is not None and self.nblock >= self.max_blocks:
            self.ops = []
            for b in self.bufs:
                b.clear()
            return
        for o in self.ops:
            if o.dma:
                slot = self.dsem[o.eng][self.dptr[o.eng] % len(self.dsem[o.eng])]
                self.dptr[o.eng] += 1
                if slot[2] is not None:
                    o.deps.append(slot[2])
                slot[1] += 16
                o.sem, o.val = slot[0], slot[1]
                slot[2] = o
            elif o.needed:
                self.ecnt[o.eng] += 1
                o.sem, o.val = self.esem[o.eng], self.ecnt[o.eng]
        per = {e: [o for o in self.ops if o.eng == e] for e in self.ENGS}
        dmas = [o for o in self.ops if o.dma]
        self.nblock += 1
        with nc.Block() as block:
            for e in self.ENGS:
                ops_e = per[e]
                if not ops_e and not (e == "sp" and dmas):
                    continue

                def body(h, e=e, ops_e=ops_e):
                    waited = self.waited[e]
                    for o in ops_e:
                        for d in o.deps:
                            k = d.sem.num
                            if waited.get(k, 0) >= d.val:
                                continue
                            h.wait_ge(d.sem, d.val)
                            waited[k] = d.val
                        ins = getattr(h, o.meth)(*o.args, **o.kw)
                        if o.sem is not None:
                            ins.then_inc(o.sem, 16 if o.dma else 1)
                    if e == "sp":
                        for d in dmas:
                            k = d.sem.num
                            if waited.get(k, 0) >= d.val:
                                continue
                            h.wait_ge(d.sem, d.val)
                            waited[k] = d.val

                getattr(block, {"pe": "tensor", "act": "scalar", "dve": "vector", "pool": "gpsimd", "sp": "sync"}[e])(body)
        self.ops = []
        for b in self.bufs:
            b.clear()
        for q in self.dsem:
            for slot in self.dsem[q]:
                slot[2] = None


class Ring:
    def __init__(self, items):
        self.items = items
        self.i = 0

    def next(self):
        it = self.items[self.i % len(self.items)]
        self.i += 1
        return it


def build(cfg, dbg_layers=None, max_blocks=None):
    c = cfg
    D, T, H, KC, NT, NTT, NTOK, HD, NKV, NBLK, PAST = c.D, c.T, c.H, c.KC, c.NT, c.NTT, c.NTOK, c.HD, c.NKV, c.NBLK, c.PAST
    DEPTH = c.DEPTH if dbg_layers is None else dbg_layers
    nc = bass.Bass("TRN2", target_bir_lowering=False)

    def din(name, shape, dt=F32):
        return nc.dram_tensor(name, list(shape), dt, kind="ExternalInput").ap()

    def dout(name, shape, dt=F32):
        return nc.dram_tensor(name, list(shape), dt, kind="ExternalOutput").ap()

    def dscr(name, shape, dt):
        return nc.dram_tensor(name, list(shape), dt).ap()

    x_in = din("x_in", [NTOK, D])
    c_ckv = din("c_ckv", [c.N_MLA, 2, PAST, 512])
    c_kr = din("c_kr", [c.N_MLA, 2, PAST, 64])
    c_sbk = din("c_sbk", [c.N_SB, 2, PAST, HD])
    c_sbv = din("c_sbv", [c.N_SB, 2, PAST, HD])
    ln_gain = din("ln_gain", [c.DEPTH, D])
    final_gain = din("final_gain", [1, D])
    mla_w_in = din("mla_w_in", [c.N_MLA, D, c.MLA_IN])
    mla_q_norm = din("mla_q_norm", [c.N_MLA, 512])
    mla_kv_norm = din("mla_kv_norm", [c.N_MLA, 512])
    mla_w_q_up = din("mla_w_q_up", [c.N_MLA, 512, H * 192])
    mla_w_kv_up = din("mla_w_kv_up", [c.N_MLA, 512, H * 256])
    mla_w_o = din("mla_w_o", [c.N_MLA, HD, D])
    sb_w_in = din("sb_w_in", [c.N_SB, D, 4 * HD])
    sb_w_o = din("sb_w_o", [c.N_SB, HD, D])
    k_ident = din("k_ident", [128, 128], BF16)
    k_nm_sb = din("k_nm_sb", [128, 128], BF16)
    k_nm_mla = din("k_nm_mla", [128, 128], BF16)
    k_cos_tm = din("k_cos_tm", [NTOK, 32])
    k_sin_tm = din("k_sin_tm", [NTOK, 32])
    k_cosT = din("k_cosT", [64, NTOK])
    k_sinT = din("k_sinT", [64, NTOK])

    y_o = dout("y", [NTOK, D])
    ckv_o = dout("ckv_o", [c.N_MLA, NTOK, 512])
    kr_o = dout("kr_o", [c.N_MLA, NTOK, 64])
    sbk_o = dout("sbk_o", [c.N_SB, NTOK, HD])
    sbv_o = dout("sbv_o", [c.N_SB, NTOK, HD])

    x_s = [None] + [dscr(f"x_s{l}", [NTOK, D], F32) for l in range(1, c.DEPTH)]
    qT_s = [dscr(f"qT_s{l}", [H, 192, NTOK], BF16) for l in range(c.DEPTH)]
    kT_s = [dscr(f"kT_s{l}", [H, 128, NKV], BF16) for l in range(c.DEPTH)]
    krT_s = [dscr(f"krT_s{l}", [64, NKV], BF16) for l in range(c.DEPTH)]
    v_s = [dscr(f"v_s{l}", [H, 128, NBLK, 128], BF16) for l in range(c.DEPTH)]
    sg_s = [dscr(f"sg_s{l}", [H, NTOK, 128], BF16) for l in range(c.DEPTH)]
    cqnT_s = [dscr(f"cqnT_s{l}", [128, 4, NTOK], BF16) for l in range(c.DEPTH)]
    ckvT_s = [dscr(f"ckvT_s{l}", [128, 4, NKV], BF16) for l in range(c.DEPTH)]

    es_top = ExitStack()
    with es_top:
        P = Prog(nc, es_top)
        P.max_blocks = max_blocks

        uid = [0]

        def sb(es, name, shape, dt):
            uid[0] += 1
            return es.enter_context(nc.sbuf_tensor(f"{name}_{uid[0]}", list(shape), dt))

        def ps(es, name, shape, dt=F32):
            uid[0] += 1
            return es.enter_context(nc.psum_tensor(f"{name}_{uid[0]}", list(shape), dt))

        ident = sb(es_top, "ident", [128, 128], BF16)
        b_const = P.buf("const")
        P.dma("sp", ident[:], k_ident, writes=[b_const])
        P.flush()

        def kv_cols(ti):
            if ti < NT:
                return [(0, 128, ti * 128)]
            return [(0, 64, c.kvA), (64, 64, c.kvB)]

        def rms_rstd(ss, rstd, n, bufs, rows=128):
            P.op("act", "activation", out=rstd[:rows], in_=ss[:rows], func=AF.Ln, bias=EPS, scale=1.0 / n,
                 reads=bufs, writes=bufs)
            P.op("act", "activation", out=rstd[:rows], in_=rstd[:rows], func=AF.Exp, scale=-0.5,
                 reads=bufs, writes=bufs)

        def phase_norm(es, l, hT, b_hT, pbank):
            x_src = x_in if l == 0 else x_s[l]
            gbc = sb(es, "gbc", [128, D], F32)
            b_g = P.buf("gbc")
            P.dma("sp", gbc[:], ln_gain[l:l + 1, :].partition_broadcast(128), writes=[b_g])
            xt = Ring([(sb(es, f"xt{i}", [128, D], F32), P.buf(f"xt{i}")) for i in range(6)])
            xs = Ring([(sb(es, f"xs{i}", [128, D], BF16), P.buf(f"xs{i}")) for i in range(3)])
            stt = Ring([(sb(es, f"nst{i}", [128, 2], F32), P.buf(f"nst{i}")) for i in range(4)])
            junk = sb(es, "junk", [128, D], BF16)
            b_junk = P.buf("junk")
            tiles = [dict(ti=ti) for ti in range(NTT)]

            def s_load(it):
                it["x"], it["b_x"] = xt.next()
                P.dma("sp", it["x"][:], x_src[it["ti"] * 128:(it["ti"] + 1) * 128, :], writes=[it["b_x"]])

            def s_sq(it):
                it["st"], it["b_st"] = stt.next()
                P.op("act", "activation", out=junk[:], in_=it["x"][:], func=AF.Square, accum_out=it["st"][:, 0:1],
                     reads=[it["b_x"]], writes=[b_junk, it["b_st"]])

            def s_rstd(it):
                rms_rstd(it["st"][:, 0:1], it["st"][:, 1:2], D, [it["b_st"]])

            def s_scale(it):
                it["xs"], it["b_xs"] = xs.next()
                P.op("dve", "scalar_tensor_tensor", out=it["xs"][:], in0=it["x"][:], scalar=it["st"][:, 1:2], in1=gbc[:],
                     op0=ALU.mult, op1=ALU.mult, reads=[it["b_x"], it["b_st"], b_g], writes=[it["b_xs"]])

            def s_tr(it):
                it["pts"] = []
                for g in range(0, KC, 8):
                    pt, b_pt = pbank.next()
                    ptb = pt[:].bitcast(BF16)
                    ng = min(8, KC - g)
                    for k in range(ng):
                        P.op("pe", "transpose", ptb[:, k * 128:(k + 1) * 128], it["xs"][:, (g + k) * 128:(g + k + 1) * 128],
                             ident[:], reads=[it["b_xs"], b_const], writes=[b_pt])
                    it["pts"].append((g, ng, ptb, b_pt))

            def s_cp(it):
                ti = it["ti"]
                for gi, (g, ng, ptb, b_pt) in enumerate(it["pts"]):
                    src = ptb[:, :ng * 128].rearrange("p (k t) -> p k t", t=128)
                    if gi % 2 == 0:
                        P.op("dve", "tensor_copy", hT[:, g:g + ng, ti * 128:(ti + 1) * 128], src, reads=[b_pt], writes=[b_hT[ti]])
                    else:
                        P.op("act", "copy", hT[:, g:g + ng, ti * 128:(ti + 1) * 128], src, reads=[b_pt], writes=[b_hT[ti]])

            stages = [s_load, s_sq, s_rstd, s_scale, s_tr, s_cp]
            skew = [0, 2, 3, 4, 5, 6]
            nst = len(stages)
            for step in range(len(tiles) + skew[-1]):
                for si in range(nst):
                    k = step - skew[si]
                    if 0 <= k < len(tiles):
                        stages[si](tiles[k])

        def proj_tm(lhsT, b_l, nk, ti, wt, b_w, n, pbank):
            pt, b_pt = pbank.next()
            for k in range(nk):
                P.op("pe", "matmul", pt[:, :n], lhsT=lhsT[:, k, ti * 128:(ti + 1) * 128], rhs=wt[:, k, :n],
                     start=(k == 0), stop=(k == nk - 1), reads=[b_l] + list(b_w), writes=[b_pt])
            return pt, b_pt

        def load_w(wslots, src, nk, n):
            wt, b_w = wslots.next()
            kk = max(1, nk // 4)
            for k0 in range(0, nk, kk):
                P.dma("pool", wt[:, k0:k0 + kk, :n], src[k0 * 128:(k0 + kk) * 128, :].rearrange("(k p) n -> p k n", p=128),
                      writes=[b_w[k0 // kk]])
            return wt, b_w

        def phase_proj_sb(es, l, j, hT, b_hT, pbank):
            W = sb_w_in[j]
            wsl = []
            for i in range(2):
                wsl.append((sb(es, f"w{i}", [128, KC, 512], BF16), [P.buf(f"w{i}_{q}") for q in range(4)]))
            wslots = Ring(wsl)
            sf = Ring([(sb(es, f"sf{i}", [128, 512], F32), P.buf(f"sf{i}")) for i in range(3)])
            sh = Ring([(sb(es, f"sh{i}", [128, 512], BF16), P.buf(f"sh{i}")) for i in range(3)])
            skt = Ring([(sb(es, f"skt{i}", [128, 4, 128], BF16), P.buf(f"skt{i}")) for i in range(2)])
            sq = Ring([(sb(es, f"sq{i}", [128, 512], BF16), P.buf(f"sq{i}")) for i in range(3)])
            b_hall = b_hT
            for nb in range(HD // 512):
                wt, b_w = load_w(wslots, W[:, nb * 512:(nb + 1) * 512], KC, 512)
                for hh in range(4):
                    h = nb * 4 + hh
                    for t0 in range(0, NTOK, 512):
                        tn = min(512, NTOK - t0)
                        pt, b_pt = pbank.next()
                        for k in range(KC):
                            P.op("pe", "matmul", pt[:, :tn], lhsT=wt[:, k, hh * 128:(hh + 1) * 128], rhs=hT[:, k, t0:t0 + tn],
                                 start=(k == 0), stop=(k == KC - 1), reads=b_hall + b_w, writes=[b_pt])
                        s, b_s = sq.next()
                        P.op("act", "copy", s[:, :tn], pt[:, :tn], reads=[b_pt], writes=[b_s])
                        P.dma("sp", qT_s[l][h, 0:128, t0:t0 + tn], s[:, :tn], reads=[b_s])
            import os as _os
            _stop = int(_os.environ.get("SBSTOP", "9"))
            if _stop < 1:
                return
            pending = [None]
            for kind in ("k", "v", "g")[:max(0, _stop - 1)]:
                base = {"k": HD, "v": 2 * HD, "g": 3 * HD}[kind]
                for nb in range(HD // 512):
                    wt, b_w = load_w(wslots, W[:, base + nb * 512: base + (nb + 1) * 512], KC, 512)
                    h0 = nb * 4
                    for ti in range(NTT):
                        pt, b_pt = proj_tm(hT, b_hT[ti], KC, ti, wt, b_w, 512, pbank)
                        if pending[0] is not None:
                            pending[0]()
                            pending[0] = None
                        if kind == "g":
                            s, b_s = sh.next()
                            P.op("act", "activation", out=s[:], in_=pt[:], func=AF.Silu, reads=[b_pt], writes=[b_s])
                            P.dma("sp", sg_s[l][h0:h0 + 4, ti * 128:(ti + 1) * 128, :].rearrange("h t d -> t h d"),
                                  s[:].rearrange("t (h d) -> t h d", d=128), reads=[b_s])
                            continue
                        f, b_f = sf.next()
                        P.op("act", "copy", f[:], pt[:], reads=[b_pt], writes=[b_f])
                        dst = sbk_o if kind == "k" else sbv_o
                        P.dma("sp", dst[j, ti * 128:(ti + 1) * 128, nb * 512:(nb + 1) * 512], f[:], reads=[b_f])
                        s, b_s = sh.next()
                        P.op("dve", "tensor_copy", s[:], f[:], reads=[b_f], writes=[b_s])
                        if kind == "v":
                            for (r0, nr, kc0) in kv_cols(ti):
                                P.dma("sp", v_s[l][h0:h0 + 4, 0:nr, kc0 // 128, :].rearrange("h p d -> p h d"),
                                      s[r0:r0 + nr, :].rearrange("t (h d) -> t h d", d=128), reads=[b_s])
                        else:
                            def post(s=s, b_s=b_s, ti=ti, h0=h0):
                                tp, b_tp = pbank.next()
                                tpb = tp[:].bitcast(BF16)
                                for hh in range(4):
                                    P.op("pe", "transpose", tpb[:, hh * 128:(hh + 1) * 128], s[:, hh * 128:(hh + 1) * 128], ident[:],
                                         reads=[b_s], writes=[b_tp])
                                kt, b_kt = skt.next()
                                P.op("dve", "tensor_copy", kt[:], tpb[:, :512].rearrange("p (h t) -> p h t", t=128),
                                     reads=[b_tp], writes=[b_kt])
                                for (r0, nr, kc0) in kv_cols(ti):
                                    P.dma("sp", kT_s[l][h0:h0 + 4, :, kc0:kc0 + nr].rearrange("h p t -> p h t"),
                                          kt[:, :, r0:r0 + nr], reads=[b_kt])
                            pending[0] = post
            if pending[0] is not None:
                pending[0]()
                pending[0] = None
            if _stop < 5:
                return
            pk = Ring([(sb(es, f"pk{i}", [128, HD], BF16), P.buf(f"pk{i}")) for i in range(6)])
            pkt = Ring([(sb(es, f"pkt{i}", [128, H, 128], BF16), P.buf(f"pkt{i}")) for i in range(3)])
            for sq_i, kv0 in ((0, c.pastA), (1, c.pastB)):
                for tb in range(PAST // 128):
                    kvc = kv0 + tb * 128
                    vt, b_vt = pk.next()
                    P.dma("pool", vt[:], c_sbv[j, sq_i, tb * 128:(tb + 1) * 128, :], writes=[b_vt])
                    P.dma("sp", v_s[l][:, :, kvc // 128, :].rearrange("h p d -> p h d"),
                          vt[:].rearrange("t (h d) -> t h d", d=128), reads=[b_vt])
                    ktm, b_ktm = pk.next()
                    P.dma("pool", ktm[:], c_sbk[j, sq_i, tb * 128:(tb + 1) * 128, :], writes=[b_ktm])
                    kt, b_kt = pkt.next()
                    for g in range(0, H, 8):
                        ng = min(8, H - g)
                        tp, b_tp = pbank.next()
                        tpb = tp[:].bitcast(BF16)
                        for hh in range(ng):
                            P.op("pe", "transpose", tpb[:, hh * 128:(hh + 1) * 128], ktm[:, (g + hh) * 128:(g + hh + 1) * 128],
                                 ident[:], reads=[b_ktm], writes=[b_tp])
                        P.op("act" if (g // 8) % 2 == 0 else "dve", "copy" if (g // 8) % 2 == 0 else "tensor_copy",
                             kt[:, g:g + ng, :], tpb[:, :ng * 128].rearrange("p (h t) -> p h t", t=128),
                             reads=[b_tp], writes=[b_kt])
                    P.dma("sp", kT_s[l][:, :, kvc:kvc + 128].rearrange("h p t -> p h t"), kt[:], reads=[b_kt])

        def phase_proj_mla_a(es, l, j, hT, b_hT, pbank):
            W = mla_w_in[j]
            wsl = []
            for i in range(2):
                wsl.append((sb(es, f"w{i}", [128, KC, 512], BF16), [P.buf(f"w{i}_{q}") for q in range(4)]))
            wslots = Ring(wsl)
            sf = Ring([(sb(es, f"sf{i}", [128, 512], F32), P.buf(f"sf{i}")) for i in range(3)])
            sh = Ring([(sb(es, f"sh{i}", [128, 512], BF16), P.buf(f"sh{i}")) for i in range(3)])
            skt = Ring([(sb(es, f"skt{i}", [128, 4, 128], BF16), P.buf(f"skt{i}")) for i in range(2)])
            stt = Ring([(sb(es, f"st{i}", [128, 2], F32), P.buf(f"st{i}")) for i in range(3)])
            junk = sb(es, "junk2", [128, 512], BF16)
            b_junk = P.buf("junk2")
            gq = sb(es, "gq", [128, 512], F32)
            gkv = sb(es, "gkv", [128, 512], F32)
            b_gq = P.buf("gq")
            P.dma("sp", gq[:], mla_q_norm[j:j + 1, :].partition_broadcast(128), writes=[b_gq])
            P.dma("sp", gkv[:], mla_kv_norm[j:j + 1, :].partition_broadcast(128), writes=[b_gq])
            cosm = sb(es, "cosm", [128, NTT, 32], F32)
            sinm = sb(es, "sinm", [128, NTT, 32], F32)
            b_cs = P.buf("cossin")
            P.dma("sp", cosm[:], k_cos_tm.rearrange("(b p) i -> p b i", p=128), writes=[b_cs])
            P.dma("sp", sinm[:], k_sin_tm.rearrange("(b p) i -> p b i", p=128), writes=[b_cs])
            pend_a = [None]
            for kind in ("cq", "ckv"):
                c0 = 0 if kind == "cq" else 512
                gain = gq if kind == "cq" else gkv
                wt, b_w = load_w(wslots, W[:, c0:c0 + 512], KC, 512)
                for ti in range(NTT):
                    pt, b_pt = proj_tm(hT, b_hT[ti], KC, ti, wt, b_w, 512, pbank)
                    if pend_a[0] is not None:
                        pend_a[0]()
                        pend_a[0] = None
                    st, b_st = stt.next()
                    P.op("act", "activation", out=junk[:], in_=pt[:], func=AF.Square, accum_out=st[:, 0:1],
                         reads=[b_pt], writes=[b_junk, b_st])
                    rms_rstd(st[:, 0:1], st[:, 1:2], 512, [b_st])
                    s, b_s = sh.next()
                    if kind == "ckv":
                        f, b_f = sf.next()
                        P.op("dve", "scalar_tensor_tensor", out=f[:], in0=pt[:], scalar=st[:, 1:2], in1=gain[:],
                             op0=ALU.mult, op1=ALU.mult, reads=[b_pt, b_st, b_gq], writes=[b_f])
                        P.dma("sp", ckv_o[j, ti * 128:(ti + 1) * 128, :], f[:], reads=[b_f])
                        P.op("dve", "tensor_copy", s[:], f[:], reads=[b_f], writes=[b_s])
                    else:
                        P.op("dve", "scalar_tensor_tensor", out=s[:], in0=pt[:], scalar=st[:, 1:2], in1=gain[:],
                             op0=ALU.mult, op1=ALU.mult, reads=[b_pt, b_st, b_gq], writes=[b_s])
                    def post(s=s, b_s=b_s, ti=ti, kind=kind):
                        tp, b_tp = pbank.next()
                        tpb = tp[:].bitcast(BF16)
                        for k in range(4):
                            P.op("pe", "transpose", tpb[:, k * 128:(k + 1) * 128], s[:, k * 128:(k + 1) * 128], ident[:],
                                 reads=[b_s], writes=[b_tp])
                        kt, b_kt = skt.next()
                        P.op("act", "copy", kt[:], tpb[:, :512].rearrange("p (k t) -> p k t", t=128), reads=[b_tp], writes=[b_kt])
                        if kind == "cq":
                            P.dma("sp", cqnT_s[l][:, :, ti * 128:(ti + 1) * 128], kt[:], reads=[b_kt])
                        else:
                            for (r0, nr, kc0) in kv_cols(ti):
                                P.dma("sp", ckvT_s[l][:, :, kc0:kc0 + nr], kt[:, :, r0:r0 + nr], reads=[b_kt])
                    pend_a[0] = post
            if pend_a[0] is not None:
                pend_a[0]()
                pend_a[0] = None
            wkr = sb(es, "wkr", [128, KC, 64], BF16)
            b_wkr = [P.buf("wkr")]
            P.dma("pool", wkr[:], W[:, 1024:1088].rearrange("(k p) n -> p k n", p=128), writes=b_wkr)
            krf = Ring([(sb(es, f"krf{i}", [128, 64], F32), P.buf(f"krf{i}")) for i in range(2)])
            krt = Ring([(sb(es, f"krt{i}", [128, 64], F32), P.buf(f"krt{i}")) for i in range(2)])
            krb = Ring([(sb(es, f"krb{i}", [128, 64], BF16), P.buf(f"krb{i}")) for i in range(2)])
            krT = Ring([(sb(es, f"krT{i}", [64, 128], BF16), P.buf(f"krT{i}")) for i in range(2)])
            for ti in range(NTT):
                pt, b_pt = proj_tm(hT, b_hT[ti], KC, ti, wkr, b_wkr, 64, pbank)
                f, b_f = krf.next()
                t_, b_t = krt.next()
                cs, sn = cosm[:, ti, :], sinm[:, ti, :]
                P.op("dve", "tensor_tensor", t_[:, 0:32], pt[:, 32:64], sn, ALU.mult, reads=[b_pt, b_cs], writes=[b_t])
                P.op("dve", "tensor_tensor", t_[:, 32:64], pt[:, 0:32], sn, ALU.mult, reads=[b_pt, b_cs], writes=[b_t])
                P.op("dve", "tensor_tensor", f[:, 0:32], pt[:, 0:32], cs, ALU.mult, reads=[b_pt, b_cs], writes=[b_f])
                P.op("dve", "tensor_tensor", f[:, 32:64], pt[:, 32:64], cs, ALU.mult, reads=[b_pt, b_cs], writes=[b_f])
                P.op("dve", "tensor_tensor", f[:, 0:32], f[:, 0:32], t_[:, 0:32], ALU.subtract, reads=[b_f, b_t], writes=[b_f])
                P.op("dve", "tensor_tensor", f[:, 32:64], f[:, 32:64], t_[:, 32:64], ALU.add, reads=[b_f, b_t], writes=[b_f])
                P.dma("sp", kr_o[j, ti * 128:(ti + 1) * 128, :], f[:], reads=[b_f])
                s, b_s = krb.next()
                P.op("dve", "tensor_copy", s[:], f[:], reads=[b_f], writes=[b_s])
                tp, b_tp = pbank.next()
                tpb = tp[:].bitcast(BF16)
                P.op("pe", "transpose", tpb[:64, 0:128], s[:, :], ident[:], reads=[b_s], writes=[b_tp])
                kt, b_kt = krT.next()
                P.op("act", "copy", kt[:], tpb[:64, 0:128], reads=[b_tp], writes=[b_kt])
                for (r0, nr, kc0) in kv_cols(ti):
                    P.dma("sp", krT_s[l][:, kc0:kc0 + nr], kt[:, r0:r0 + nr], reads=[b_kt])
            for nb in range(HD // 512):
                wt, b_w = load_w(wslots, W[:, 1088 + nb * 512:1088 + (nb + 1) * 512], KC, 512)
                h0 = nb * 4
                for ti in range(NTT):
                    pt, b_pt = proj_tm(hT, b_hT[ti], KC, ti, wt, b_w, 512, pbank)
                    s, b_s = sh.next()
                    P.op("act", "activation", out=s[:], in_=pt[:], func=AF.Silu, reads=[b_pt], writes=[b_s])
                    P.dma("sp", sg_s[l][h0:h0 + 4, ti * 128:(ti + 1) * 128, :].rearrange("h t d -> t h d"),
                          s[:].rearrange("t (h d) -> t h d", d=128), reads=[b_s])

        def phase_proj_mla_b(es, l, j, pbank):
            cqnT = sb(es, "cqnT", [128, 4, NTOK], BF16)
            ckvT = sb(es, "ckvT", [128, 4, NKV], BF16)
            b_cq = P.buf("cqnT")
            b_ck = P.buf("ckvT_new")
            P.op("pool", "memset", ckvT[:, :, T:T + 256], 0.0, writes=[b_ck])
            P.dma("sp", cqnT[:], cqnT_s[l], writes=[b_cq])
            P.dma_group("sp", [(ckvT[:, :, 0:T], ckvT_s[l][:, :, 0:T]),
                               (ckvT[:, :, c.kvA:c.kvA + 64], ckvT_s[l][:, :, c.kvA:c.kvA + 64]),
                               (ckvT[:, :, c.kvB:c.kvB + 64], ckvT_s[l][:, :, c.kvB:c.kvB + 64])], writes=[b_ck])
            cosT = sb(es, "cosT", [64, NTOK], F32)
            sinT = sb(es, "sinT", [64, NTOK], F32)
            b_cs = P.buf("cossinT")
            P.dma("sp", cosT[:], k_cosT, writes=[b_cs])
            P.dma("sp", sinT[:], k_sinT, writes=[b_cs])
            pk = Ring([(sb(es, f"pc{i}", [128, 512], BF16), P.buf(f"pc{i}")) for i in range(2)])
            pkr = Ring([(sb(es, f"pr{i}", [128, 64], BF16), P.buf(f"pr{i}")) for i in range(2)])
            pkrT = Ring([(sb(es, f"prT{i}", [64, 128], BF16), P.buf(f"prT{i}")) for i in range(2)])
            b_past = []
            for sq_i, kv0 in ((0, c.pastA), (1, c.pastB)):
                for tb in range(PAST // 128):
                    kvc = kv0 + tb * 128
                    ct, b_ct = pk.next()
                    P.dma("pool", ct[:], c_ckv[j, sq_i, tb * 128:(tb + 1) * 128, :], writes=[b_ct])
                    tp, b_tp = pbank.next()
                    tpb = tp[:].bitcast(BF16)
                    for k in range(4):
                        P.op("pe", "transpose", tpb[:, k * 128:(k + 1) * 128], ct[:, k * 128:(k + 1) * 128], ident[:],
                             reads=[b_ct], writes=[b_tp])
                    bb = P.buf("ckvT_past")
                    b_past.append(bb)
                    P.op("act", "copy", ckvT[:, :, kvc:kvc + 128], tpb[:, :512].rearrange("p (k t) -> p k t", t=128),
                         reads=[b_tp], writes=[bb])
                    rt, b_rt = pkr.next()
                    P.dma("pool", rt[:], c_kr[j, sq_i, tb * 128:(tb + 1) * 128, :], writes=[b_rt])
                    tp, b_tp = pbank.next()
                    tpb = tp[:].bitcast(BF16)
                    P.op("pe", "transpose", tpb[:64, 0:128], rt[:, :], ident[:], reads=[b_rt], writes=[b_tp])
                    rT, b_rT = pkrT.next()
                    P.op("dve", "tensor_copy", rT[:], tpb[:64, 0:128], reads=[b_tp], writes=[b_rT])
                    P.dma("sp", krT_s[l][:, kvc:kvc + 128], rT[:], reads=[b_rT])
            b_ckall = [b_ck] + b_past
            Wq = mla_w_q_up[j].rearrange("(k p) (h e) -> p k h e", p=128, e=192)
            Wkv = mla_w_kv_up[j].rearrange("(k p) (h e) -> p k h e", p=128, e=256)
            HG = 4
            wqn = Ring([(sb(es, f"wqn{i}", [128, 4, HG, 128], BF16), [P.buf(f"wqn{i}")]) for i in range(2)])
            wqr = Ring([(sb(es, f"wqr{i}", [128, 4, HG, 64], BF16), [P.buf(f"wqr{i}")]) for i in range(2)])
            wqs = Ring([(sb(es, f"wqs{i}", [128, 4, HG, 64], BF16), [P.buf(f"wqs{i}")]) for i in range(2)])
            wkn = Ring([(sb(es, f"wkn{i}", [128, 4, HG, 128], BF16), [P.buf(f"wkn{i}")]) for i in range(2)])
            wv = Ring([(sb(es, f"wv{i}", [128, 4, HG, 128], BF16), [P.buf(f"wv{i}")]) for i in range(2)])
            sq = Ring([(sb(es, f"sq{i}", [128, 512], BF16), P.buf(f"sq{i}")) for i in range(3)])
            r1 = Ring([(sb(es, f"r1{i}", [64, 512], F32), P.buf(f"r1{i}")) for i in range(2)])
            r2 = Ring([(sb(es, f"r2{i}", [64, 512], F32), P.buf(f"r2{i}")) for i in range(2)])
            sr = Ring([(sb(es, f"sr{i}", [64, 512], BF16), P.buf(f"sr{i}")) for i in range(2)])
            for hg in range(H // HG):
                hs = slice(hg * HG, (hg + 1) * HG)
                a_n, b_n = wqn.next()
                a_r, b_r = wqr.next()
                a_s, b_s_ = wqs.next()
                a_k, b_k = wkn.next()
                a_v, b_v = wv.next()
                P.dma_group("pool", [(a_n[:, k], Wq[:, k, hs, 0:128]) for k in range(4)], writes=b_n)
                P.dma_group("pool", [(a_r[:, k], Wq[:, k, hs, 128:192]) for k in range(4)], writes=b_r)
                P.dma_group("pool", [(a_s[:, k, :, 0:32], Wq[:, k, hs, 160:192]) for k in range(4)]
                            + [(a_s[:, k, :, 32:64], Wq[:, k, hs, 128:160]) for k in range(4)], writes=b_s_)
                P.dma_group("pool", [(a_k[:, k], Wkv[:, k, hs, 0:128]) for k in range(4)], writes=b_k)
                P.dma_group("pool", [(a_v[:, k], Wkv[:, k, hs, 128:256]) for k in range(4)], writes=b_v)
                for hh in range(HG):
                    h = hg * HG + hh
                    for t0 in range(0, NTOK, 512):
                        tn = min(512, NTOK - t0)
                        pt, b_pt = pbank.next()
                        for k in range(4):
                            P.op("pe", "matmul", pt[:, :tn], lhsT=a_n[:, k, hh, :], rhs=cqnT[:, k, t0:t0 + tn],
                                 start=(k == 0), stop=(k == 3), reads=[b_cq] + b_n, writes=[b_pt])
                        s, b_s = sq.next()
                        P.op("act", "copy", s[:, :tn], pt[:, :tn], reads=[b_pt], writes=[b_s])
                        P.dma("sp", qT_s[l][h, 0:128, t0:t0 + tn], s[:, :tn], reads=[b_s])
                        p1, b_p1 = pbank.next()
                        p2, b_p2 = pbank.next()
                        for k in range(4):
                            P.op("pe", "matmul", p1[:64, :tn], lhsT=a_r[:, k, hh, :], rhs=cqnT[:, k, t0:t0 + tn],
                                 start=(k == 0), stop=(k == 3), reads=[b_cq] + b_r, writes=[b_p1])
                        for k in range(4):
                            P.op("pe", "matmul", p2[:64, :tn], lhsT=a_s[:, k, hh, :], rhs=cqnT[:, k, t0:t0 + tn],
                                 start=(k == 0), stop=(k == 3), reads=[b_cq] + b_s_, writes=[b_p2])
                        t1, b_t1 = r1.next()
                        t2, b_t2 = r2.next()
                        P.op("dve", "tensor_tensor", t1[:, :tn], p1[:64, :tn], cosT[:, t0:t0 + tn], ALU.mult,
                             reads=[b_p1, b_cs], writes=[b_t1])
                        P.op("dve", "tensor_tensor", t2[:, :tn], p2[:64, :tn], sinT[:, t0:t0 + tn], ALU.mult,
                             reads=[b_p2, b_cs], writes=[b_t2])
                        o_, b_o = sr.next()
                        P.op("pool", "tensor_tensor", o_[:, :tn], t1[:, :tn], t2[:, :tn], ALU.add,
                             reads=[b_t1, b_t2], writes=[b_o])
                        P.dma("sp", qT_s[l][h, 128:192, t0:t0 + tn], o_[:, :tn], reads=[b_o])
                    for t0 in range(0, NKV, 512):
                        tn = min(512, NKV - t0)
                        pt, b_pt = pbank.next()
                        for k in range(4):
                            P.op("pe", "matmul", pt[:, :tn], lhsT=a_k[:, k, hh, :], rhs=ckvT[:, k, t0:t0 + tn],
                                 start=(k == 0), stop=(k == 3), reads=b_ckall + b_k, writes=[b_pt])
                        s, b_s = sq.next()
                        P.op("dve" if (t0 // 512) % 2 else "act", "tensor_copy" if (t0 // 512) % 2 else "copy",
                             s[:, :tn], pt[:, :tn], reads=[b_pt], writes=[b_s])
                        P.dma("sp", kT_s[l][h, :, t0:t0 + tn], s[:, :tn], reads=[b_s])
                for blk in range(NBLK):
                    pt, b_pt = pbank.next()
                    for k in range(4):
                        P.op("pe", "matmul", pt[:, :HG * 128], lhsT=ckvT[:, k, blk * 128:(blk + 1) * 128],
                             rhs=a_v[:, k].rearrange("p h d -> p (h d)"), start=(k == 0), stop=(k == 3),
                             reads=b_ckall + b_v, writes=[b_pt])
                    s, b_s = sq.next()
                    P.op("dve" if blk % 2 else "act", "tensor_copy" if blk % 2 else "copy", s[:, :HG * 128], pt[:, :HG * 128],
                         reads=[b_pt], writes=[b_s])
                    P.dma("sp", v_s[l][hs, :, blk, :].rearrange("h p d -> p h d"),
                          s[:, :HG * 128].rearrange("t (h d) -> t h d", d=128), reads=[b_s])

        def qtiles():
            qt = []
            for i in range(NT):
                qt.append(dict(tok0=i * 128, nq=128, segs=[(0, (i + 1) * 128)], diag=128, og=("p", i)))
            qt.append(dict(tok0=T, nq=64, segs=[(c.pastA, PAST), (c.kvA, 64)], diag=64, og=("s", 0)))
            qt.append(dict(tok0=T + 64, nq=64, segs=[(c.pastB, PAST), (c.kvB, 64)], diag=64, og=("s", 1)))
            return qt

        def chunks_of(q):
            ch = []
            nseg = len(q["segs"])
            for si, (c0, n) in enumerate(q["segs"]):
                for o in range(0, n, 1024):
                    nn = min(1024, n - o)
                    last = (si == nseg - 1) and (o + nn == n)
                    ch.append((c0 + o, nn, last))
            return ch

        def phase_attn(es, l, j, is_mla, og, ogs, b_og, b_ogs):
            kv_valid = [(0, T + 64), (T + 128, T + 192), (T + 256, NKV)]
            scale = (192.0 if is_mla else 128.0) ** -0.5
            Sps = Ring([(ps(es, f"S{i}", [128, 1024]), P.buf(f"S{i}", True)) for i in range(2)])
            Tps = Ring([(ps(es, f"Tp{i}", [128, 512]), P.buf(f"Tp{i}", True)) for i in range(2)])
            Ops = Ring([(ps(es, f"O{i}", [128, 512]), P.buf(f"O{i}", True)) for i in range(2)])
            QT = qtiles()
            n_stages = 7 if is_mla else 8
            per_head = sum(len(chunks_of(q)) for q in QT)
            RH = 2 if per_head >= n_stages + 2 else 3
            kT = Ring([(sb(es, f"kT{i}", [128, NKV], BF16), P.buf(f"kT{i}")) for i in range(RH)])
            vv = Ring([(sb(es, f"vv{i}", [128, NBLK, 128], BF16), P.buf(f"vv{i}")) for i in range(RH)])
            qT = Ring([(sb(es, f"qT{i}", [128, NTOK], BF16), P.buf(f"qT{i}")) for i in range(RH)])
            sgp = Ring([(sb(es, f"sgp{i}", [128, NT, 128], BF16), P.buf(f"sgp{i}")) for i in range(RH)])
            sgs = Ring([(sb(es, f"sgs{i}", [64, 2, 128], BF16), P.buf(f"sgs{i}")) for i in range(RH)])
            nmask = sb(es, "nmask", [128, 128], BF16)
            b_nm = P.buf("nmask")
            P.dma("sp", nmask[:], k_nm_mla if is_mla else k_nm_sb, writes=[b_nm])
            if is_mla:
                qr = Ring([(sb(es, f"qr{i}", [128, NTOK], BF16), P.buf(f"qr{i}")) for i in range(RH)])
                krT = sb(es, "krTall", [128, NKV], BF16)
                b_krT = P.buf("krTall")
                P.dma_group("sp", [(krT[hf * 64:(hf + 1) * 64, a_:b_], krT_s[l][:, a_:b_]) for (a_, b_) in kv_valid for hf in range(2)],
                            writes=[b_krT])
                Ssb = Ring([(sb(es, f"Ssb{i}", [128, 1024], F32), P.buf(f"Ssb{i}")) for i in range(4)])
                stat = Ring([(sb(es, f"stat{i}", [128, 16], F32), P.buf(f"stat{i}")) for i in range(6)])
                cmb = Ring([(sb(es, f"cmb{i}", [128, 128], F32), P.buf(f"cmb{i}")) for i in range(2)])
            else:
                Gb = Ring([(sb(es, f"G{i}", [128, 1024], F32), P.buf(f"G{i}")) for i in range(4)])
                Pb = Ring([(sb(es, f"Pb{i}", [128, 1032], F32), P.buf(f"Pb{i}")) for i in range(4)])
                ones = sb(es, "ones", [128, 1024], BF16)
                b_ones = P.buf("ones")
                P.op("pool", "memset", ones[:], 1.0, writes=[b_ones])
            Ab = Ring([(sb(es, f"A{i}", [128, 1024], BF16), P.buf(f"A{i}")) for i in range(3)])
            ATb = Ring([(sb(es, f"AT{i}", [128, 8, 128], BF16), P.buf(f"AT{i}")) for i in range(3)])

            heads = {}

            def load_head(h):
                hb = {}
                hb["k"], hb["b_k"] = kT.next()
                hb["v"], hb["b_v"] = vv.next()
                hb["q"], hb["b_q"] = qT.next()
                hb["sgp"], hb["b_sgp"] = sgp.next()
                hb["sgs"], hb["b_sgs"] = sgs.next()
                P.dma("sp", hb["q"][:], qT_s[l][h, 0:128, :], writes=[hb["b_q"]])
                if is_mla:
                    hb["qr"], hb["b_qr"] = qr.next()
                    P.dma_group("sp", [(hb["qr"][hf * 64:(hf + 1) * 64, :], qT_s[l][h, 128:192, :]) for hf in range(2)],
                                writes=[hb["b_qr"]])
                P.dma_group("sp", [(hb["k"][:, a_:b_], kT_s[l][h, :, a_:b_]) for (a_, b_) in kv_valid], writes=[hb["b_k"]])
                P.dma_group("sp", [(hb["v"][:, 0:NT, :], v_s[l][h, :, 0:NT, :]),
                                   (hb["v"][0:64, NT:NT + 2, :], v_s[l][h, 0:64, NT:NT + 2, :]),
                                   (hb["v"][:, NT + 2:NBLK, :], v_s[l][h, :, NT + 2:NBLK, :])], writes=[hb["b_v"]])
                P.dma("sp", hb["sgp"][:], sg_s[l][h, 0:T, :].rearrange("(b p) d -> p b d", p=128), writes=[hb["b_sgp"]])
                P.dma("sp", hb["sgs"][:], sg_s[l][h, T:T + 128, :].rearrange("(s p) d -> p s d", p=64), writes=[hb["b_sgs"]])
                heads[h] = hb

            items = []
            for h in range(H):
                for qi, q in enumerate(QT):
                    chs = chunks_of(q)
                    order = chs if is_mla else list(reversed(chs))
                    qs = dict(nch=len(order))
                    for ci, (col0, n, isd) in enumerate(order):
                        items.append(dict(h=h, q=q, qs=qs, ci=ci, nch=len(order), col0=col0, n=n, isd=isd,
                                          first_of_head=(qi == 0 and ci == 0)))

            def dst_of(it):
                q, hb, h = it["q"], heads[it["h"]], it["h"]
                nq = q["nq"]
                if q["og"][0] == "p":
                    return (og[:nq, q["og"][1], h * 128:(h + 1) * 128], b_og[q["og"][1]], hb["sgp"][:nq, q["og"][1], :], hb["b_sgp"])
                return (ogs[:nq, q["og"][1], h * 128:(h + 1) * 128], b_ogs[q["og"][1]], hb["sgs"][:nq, q["og"][1], :], hb["b_sgs"])

            def st_qk(it):
                h = it["h"]
                hb = heads[h]
                q, n, col0 = it["q"], it["n"], it["col0"]
                nq = q["nq"]
                qc = slice(q["tok0"], q["tok0"] + nq)
                dn = q["diag"]
                masked = it["isd"] and (dn == 128 or not is_mla)
                zp, b_zp = Sps.next()
                it["zp"], it["b_zp"] = zp, b_zp
                def nope(c0, c1, with_mask):
                    P.op("pe", "matmul", zp[:nq, c0:c1], lhsT=hb["q"][:, qc], rhs=hb["k"][:, col0 + c0:col0 + c1],
                         start=True, stop=not (is_mla or with_mask), reads=[hb["b_q"], hb["b_k"]], writes=[b_zp])

                def rope(c0, c1, with_mask):
                    hf = (c0 // 512) % 2
                    ps_ = slice(hf * 64, (hf + 1) * 64)
                    P.op("pe", "matmul", zp[:nq, c0:c1], lhsT=hb["qr"][ps_, qc], rhs=krT[ps_, col0 + c0:col0 + c1],
                         start=False, stop=not with_mask, reads=[hb["b_qr"], b_krT], writes=[b_zp])

                def maskmm(c0, c1):
                    P.op("pe", "matmul", zp[:nq, c0:c1], lhsT=ident[:nq, :nq], rhs=nmask[:nq, :dn],
                         start=False, stop=True, reads=[b_const, b_nm], writes=[b_zp])

                ranges = []
                for m in range(0, n, 512):
                    mm = min(512, n - m)
                    lastg = (m + mm == n)
                    if masked and lastg:
                        if n - dn > m:
                            ranges.append((m, n - dn, False))
                        ranges.append((n - dn, n, True))
                    else:
                        ranges.append((m, m + mm, False))
                i = 0
                while i < len(ranges):
                    r0 = ranges[i]
                    r1 = ranges[i + 1] if i + 1 < len(ranges) else None
                    if is_mla and r1 is not None and (r0[0] // 512) != (r1[0] // 512) and not r0[2]:
                        nope(*r0)
                        nope(*r1)
                        rope(*r0)
                        rope(*r1)
                        if r1[2]:
                            maskmm(r1[0], r1[1])
                        i += 2
                    else:
                        nope(*r0)
                        if is_mla:
                            rope(*r0)
                        if r0[2]:
                            maskmm(r0[0], r0[1])
                        i += 1

            def st_T(it):
                q, n = it["q"], it["n"]
                nq = q["nq"]
                A, b_A = it["A"], it["b_A"]
                tp, b_tp = Tps.next()
                it["tp"], it["b_tp"] = tp, b_tp
                tpb = tp[:].bitcast(BF16)
                nb = (n + 127) // 128
                it["nks"] = []
                for bi in range(nb):
                    nk = min(128, n - bi * 128)
                    it["nks"].append(nk)
                    P.op("pe", "transpose", tpb[:nk, bi * 128:bi * 128 + nq], A[:nq, bi * 128:bi * 128 + nk],
                         ident[:nq, :nq], reads=[b_A, b_const], writes=[b_tp])

            cnt = [0]

            def st_copy(it):
                q = it["q"]
                nq = q["nq"]
                nb = len(it["nks"])
                nk0 = it["nks"][0]
                tpb = it["tp"][:].bitcast(BF16)
                at, b_at = ATb.next()
                it["at"], it["b_at"] = at, b_at
                cnt[0] += 1
                eng = "act"
                P.op(eng, "copy" if eng == "act" else "tensor_copy", at[:nk0, :nb, :nq],
                     tpb[:nk0, :nb * 128].rearrange("p (b t) -> p b t", t=128)[:, :, :nq], reads=[it["b_tp"]], writes=[b_at])

            def st_pv(it):
                q, n, col0 = it["q"], it["n"], it["col0"]
                nq = q["nq"]
                hb = heads[it["h"]]
                qs = it["qs"]
                if it["ci"] == 0:
                    qs["o"], qs["b_o"] = Ops.next()
                o_ps, b_o = qs["o"], qs["b_o"]
                oc = (it["ci"] * 128) if is_mla else 0
                nb = len(it["nks"])
                for bi in range(nb):
                    blk = (col0 + bi * 128) // 128
                    if is_mla:
                        st_, sp_ = (bi == 0), (bi == nb - 1)
                    else:
                        st_, sp_ = (it["ci"] == 0 and bi == 0), (it["ci"] == it["nch"] - 1 and bi == nb - 1)
                    P.op("pe", "matmul", o_ps[:nq, oc:oc + 128], lhsT=it["at"][:it["nks"][bi], bi, :nq],
                         rhs=hb["v"][:it["nks"][bi], blk, :], start=st_, stop=sp_, reads=[it["b_at"], hb["b_v"]], writes=[b_o])

            if not is_mla:
                def st_sig(it):
                    nq, n = it["q"]["nq"], it["n"]
                    G, b_G = Gb.next()
                    it["G"], it["b_G"] = G, b_G
                    P.op("act", "activation", out=G[:nq, :n], in_=it["zp"][:nq, :n], func=AF.Sigmoid, scale=-scale,
                         reads=[it["b_zp"]], writes=[b_G])

                def st_scan(it):
                    nq, n = it["q"]["nq"], it["n"]
                    qs = it["qs"]
                    Pc, b_P = Pb.next()
                    it["P"], it["b_P"] = Pc, b_P
                    carry = qs.get("carry")
                    if carry is None:
                        P.op("dve", "memset", Pc[:nq, n:n + 1], 1.0, writes=[b_P])
                    else:
                        P.op("dve", "tensor_copy", Pc[:nq, n:n + 1], carry, reads=[qs["b_carry"]], writes=[b_P])
                    P.op("dve", "tensor_tensor_scan", Pc[:nq, 0:n][:, ::-1], it["G"][:nq, 0:n][:, ::-1], ones[:nq, 0:n][:, ::-1],
                         Pc[:nq, n:n + 1], ALU.mult, ALU.mult, reads=[it["b_G"], b_ones, b_P], writes=[b_P])
                    qs["carry"], qs["b_carry"] = Pc[:nq, 0:1], b_P

                def st_sub(it):
                    nq, n = it["q"]["nq"], it["n"]
                    A, b_A = Ab.next()
                    it["A"], it["b_A"] = A, b_A
                    P.op("dve", "tensor_tensor", A[:nq, :n], it["P"][:nq, 1:n + 1], it["P"][:nq, 0:n], ALU.subtract,
                         reads=[it["b_P"]], writes=[b_A])

                def st_fin(it):
                    if it["ci"] != it["nch"] - 1:
                        return
                    nq = it["q"]["nq"]
                    og_dst, b_dst, sg_src, b_sg = dst_of(it)
                    P.op("dve", "tensor_tensor", og_dst, it["qs"]["o"][:nq, :128], sg_src, ALU.mult,
                         reads=[it["qs"]["b_o"], b_sg], writes=[b_dst])

                stages = [st_qk, st_sig, st_scan, st_sub, st_T, st_copy, st_pv, st_fin]
            else:
                def st_max(it):
                    nq, n, ci = it["q"]["nq"], it["n"], it["ci"]
                    qs = it["qs"]
                    if ci == 0:
                        qs["st"], qs["b_st"] = stat.next()
                    st, b_st = qs["st"], qs["b_st"]
                    S, b_S = Ssb.next()
                    it["S"], it["b_S"] = S, b_S
                    P.op("dve", "tensor_scalar", S[:nq, :n], it["zp"][:nq, :n], -scale, None, ALU.mult, ALU.min,
                         st[:nq, 2 + ci:3 + ci], reads=[it["b_zp"]], writes=[b_S, b_st])

                def st_expp(it):
                    nq, n, ci = it["q"]["nq"], it["n"], it["ci"]
                    st, b_st = it["qs"]["st"], it["qs"]["b_st"]
                    A, b_A = Ab.next()
                    it["A"], it["b_A"] = A, b_A
                    P.op("act", "activation", out=A[:nq, :n], in_=it["S"][:nq, :n], func=AF.Exp,
                         bias=st[:nq, 2 + ci:3 + ci], scale=-1.0, accum_out=st[:nq, 4 + ci:5 + ci],
                         reads=[it["b_S"], b_st], writes=[b_A, b_st])

                def st_fin(it):
                    if it["ci"] != it["nch"] - 1:
                        return
                    nq = it["q"]["nq"]
                    qs = it["qs"]
                    st, b_st, o_ps, b_o = qs["st"], qs["b_st"], qs["o"], qs["b_o"]
                    og_dst, b_dst, sg_src, b_sg = dst_of(it)
                    if it["nch"] == 1:
                        P.op("dve", "reciprocal", st[:nq, 6:7], st[:nq, 4:5], reads=[b_st], writes=[b_st])
                        P.op("dve", "scalar_tensor_tensor", out=og_dst, in0=o_ps[:nq, 0:128], scalar=st[:nq, 6:7], in1=sg_src,
                             op0=ALU.mult, op1=ALU.mult, reads=[b_o, b_st, b_sg], writes=[b_dst])
                        return
                    P.op("dve", "tensor_tensor", st[:nq, 6:7], st[:nq, 2:3], st[:nq, 3:4], ALU.min, reads=[b_st], writes=[b_st])
                    P.op("act", "activation", out=st[:nq, 8:10], in_=st[:nq, 2:4], func=AF.Exp, bias=st[:nq, 6:7], scale=-1.0,
                         reads=[b_st], writes=[b_st])
                    P.op("dve", "tensor_tensor", st[:nq, 10:12], st[:nq, 8:10], st[:nq, 4:6], ALU.mult, reads=[b_st], writes=[b_st])
                    P.op("dve", "tensor_tensor", st[:nq, 12:13], st[:nq, 10:11], st[:nq, 11:12], ALU.add, reads=[b_st], writes=[b_st])
                    P.op("dve", "reciprocal", st[:nq, 12:13], st[:nq, 12:13], reads=[b_st], writes=[b_st])
                    P.op("dve", "tensor_scalar", st[:nq, 14:16], st[:nq, 8:10], st[:nq, 12:13], None, ALU.mult,
                         reads=[b_st], writes=[b_st])
                    t_, b_t = cmb.next()
                    P.op("dve", "tensor_scalar", t_[:nq, :], o_ps[:nq, 0:128], st[:nq, 14:15], None, ALU.mult,
                         reads=[b_o, b_st], writes=[b_t])
                    P.op("dve", "scalar_tensor_tensor", out=t_[:nq, :], in0=o_ps[:nq, 128:256], scalar=st[:nq, 15:16], in1=t_[:nq, :],
                         op0=ALU.mult, op1=ALU.add, reads=[b_o, b_st, b_t], writes=[b_t])
                    P.op("dve", "tensor_tensor", og_dst, t_[:nq, :], sg_src, ALU.mult, reads=[b_t, b_sg], writes=[b_dst])

                stages = [st_qk, st_max, st_expp, st_T, st_copy, st_pv, st_fin]

            ns = len(stages)
            assert ns == n_stages
            first_item = {}
            last_item = {}
            for k, it in enumerate(items):
                first_item.setdefault(it["h"], k)
                last_item[it["h"]] = k
            loaded = -1
            for step in range(len(items) + ns - 1):
                hcur = items[min(step, len(items) - 1)]["h"]
                while loaded < min(hcur + 1, H - 1):
                    g = loaded + 1
                    safe = (g - RH < 0) or (last_item[g - RH] + ns - 1 < step)
                    if not safe:
                        assert g > hcur, "head-buffer ring too shallow"
                        break
                    load_head(g)
                    loaded = g
                for si in range(ns):
                    k = step - si
                    if 0 <= k < len(items):
                        stages[si](items[k])

        def phase_out(es, l, j, is_mla, og, ogs, b_og, b_ogs):
            HC = HD // 128
            Wo = (mla_w_o if is_mla else sb_w_o)[j]
            x_src = x_in if l == 0 else x_s[l]
            last = (l == c.DEPTH - 1)
            pbank = Ring([(ps(es, f"pb{i}", [128, 512]), P.buf(f"pb{i}", True)) for i in range(8)])
            wo = sb(es, "wo", [128, HC, D], BF16)
            b_wo = [P.buf(f"wo{i}") for i in range(4)]
            kk = max(1, HC // 4)
            for qi, k0 in enumerate(range(0, HC, kk)):
                P.dma("pool", wo[:, k0:k0 + kk, :], Wo[k0 * 128:(k0 + kk) * 128, :].rearrange("(k p) n -> p k n", p=128),
                      writes=[b_wo[qi]])
            xt = Ring([(sb(es, f"xo{i}", [128, D], F32), P.buf(f"xo{i}")) for i in range(2)])
            xn = Ring([(sb(es, f"xn{i}", [128, D], F32), P.buf(f"xn{i}")) for i in range(2)])
            oT = Ring([(sb(es, f"oT{i}", [128, HC, 128], BF16), P.buf(f"oT{i}")) for i in range(2)])
            if last:
                fg = sb(es, "fg", [128, D], F32)
                b_fg = P.buf("fg")
                P.dma("sp", fg[:], final_gain.partition_broadcast(128), writes=[b_fg])
                yt = Ring([(sb(es, f"yt{i}", [128, D], F32), P.buf(f"yt{i}")) for i in range(2)])
                stt = Ring([(sb(es, f"fst{i}", [128, 2], F32), P.buf(f"fst{i}")) for i in range(2)])
                junk = sb(es, "junk3", [128, D], BF16)
                b_junk = P.buf("junk3")
            def make_ot(ti):
                ot, b_ot = oT.next()
                for g in range(0, HC, 8):
                    ng = min(8, HC - g)
                    tp, b_tp = pbank.next()
                    tpb = tp[:].bitcast(BF16)
                    for k in range(ng):
                        kc = slice((g + k) * 128, (g + k + 1) * 128)
                        if ti < NT:
                            P.op("pe", "transpose", tpb[:, k * 128:(k + 1) * 128], og[:, ti, kc], ident[:],
                                 reads=[b_og[ti]], writes=[b_tp])
                        else:
                            for s_ in range(2):
                                P.op("pe", "transpose", tpb[:, k * 128 + s_ * 64:k * 128 + s_ * 64 + 64], ogs[:, s_, kc],
                                     ident[:64, :64], reads=[b_ogs[s_]], writes=[b_tp])
                    eng = "act" if (g // 8) % 2 == 0 else "dve"
                    P.op(eng, "copy" if eng == "act" else "tensor_copy", ot[:, g:g + ng, :],
                         tpb[:, :ng * 128].rearrange("p (k t) -> p k t", t=128), reads=[b_tp], writes=[b_ot])
                return ot, b_ot

            ots = {0: make_ot(0)}
            for ti in range(NTT):
                x_, b_x = xt.next()
                P.dma("sp", x_[:], x_src[ti * 128:(ti + 1) * 128, :], writes=[b_x])
                if ti + 1 < NTT:
                    ots[ti + 1] = make_ot(ti + 1)
                ot, b_ot = ots.pop(ti)
                xn_, b_xn = xn.next()
                for nb in range(D // 512):
                    pt, b_pt = pbank.next()
                    for k in range(HC):
                        P.op("pe", "matmul", pt[:, :512], lhsT=ot[:, k, :], rhs=wo[:, k, nb * 512:(nb + 1) * 512],
                             start=(k == 0), stop=(k == HC - 1), reads=[b_ot] + b_wo, writes=[b_pt])
                    P.op("dve", "tensor_tensor", xn_[:, nb * 512:(nb + 1) * 512], pt[:, :512], x_[:, nb * 512:(nb + 1) * 512], ALU.add,
                         reads=[b_pt, b_x], writes=[b_xn])
                if not last:
                    P.dma("sp", x_s[l + 1][ti * 128:(ti + 1) * 128, :], xn_[:], reads=[b_xn])
                else:
                    st, b_st = stt.next()
                    P.op("act", "activation", out=junk[:], in_=xn_[:], func=AF.Square, accum_out=st[:, 0:1],
                         reads=[b_xn], writes=[b_junk, b_st])
                    rms_rstd(st[:, 0:1], st[:, 1:2], D, [b_st])
                    y_, b_y = yt.next()
                    P.op("dve", "scalar_tensor_tensor", out=y_[:], in0=xn_[:], scalar=st[:, 1:2], in1=fg[:],
                         op0=ALU.mult, op1=ALU.mult, reads=[b_xn, b_st, b_fg], writes=[b_y])
                    P.dma("sp", y_o[ti * 128:(ti + 1) * 128, :], y_[:], reads=[b_y])

        for l in range(DEPTH):
            is_mla = (l % 2 == 0)
            j = l // 2
            with ExitStack() as es:
                pbank = Ring([(ps(es, f"pb{i}", [128, 512]), P.buf(f"pb{i}", True)) for i in range(8)])
                hT = sb(es, "hT", [128, KC, NTOK], BF16)
                b_hT = [P.buf(f"hT{ti}") for ti in range(NTT)]
                with ExitStack() as es1:
                    phase_norm(es1, l, hT, b_hT, pbank)
                    P.flush()
                with ExitStack() as es2:
                    if is_mla:
                        phase_proj_mla_a(es2, l, j, hT, b_hT, pbank)
                    else:
                        phase_proj_sb(es2, l, j, hT, b_hT, pbank)
                    P.flush()
            if is_mla:
                with ExitStack() as es:
                    pbank = Ring([(ps(es, f"pb{i}", [128, 512]), P.buf(f"pb{i}", True)) for i in range(8)])
                    phase_proj_mla_b(es, l, j, pbank)
                    P.flush()
            with ExitStack() as es:
                og = sb(es, "og", [128, NT, HD], BF16)
                ogs = sb(es, "ogs", [64, 2, HD], BF16)
                b_og = [P.buf(f"og{i}") for i in range(NT)]
                b_ogs = [P.buf(f"ogs{i}") for i in range(2)]
                with ExitStack() as es3:
                    phase_attn(es3, l, j, is_mla, og, ogs, b_og, b_ogs)
                    P.flush()
                with ExitStack() as es4:
                    phase_out(es4, l, j, is_mla, og, ogs, b_og, b_ogs)
                    P.flush()
    return nc


def _consts(cfg):
    c = cfg
    ident = np.eye(128, dtype=np.float32).astype(ml_dtypes.bfloat16)
    t = np.arange(128)
    nm_sb = np.where(t[None, :] >= t[:, None], -30000.0, 0.0).astype(np.float32).astype(ml_dtypes.bfloat16)
    nm_mla = np.where((t[:, None] < 64) & (t[None, :] >= 64), -30000.0, 0.0).astype(np.float32).astype(ml_dtypes.bfloat16)
    half = 32
    inv = (1.0 / (np.float32(10000.0) ** (np.arange(half, dtype=np.float32) * np.float32(2.0 / 64)))).astype(np.float32)
    pos = np.concatenate([np.arange(c.T), c.PAST + np.arange(64), c.PAST + np.arange(64)]).astype(np.float32)
    ang = (pos[:, None] * inv[None, :]).astype(np.float32)
    cos, sin = np.cos(ang).astype(np.float32), np.sin(ang).astype(np.float32)
    cosT = np.ascontiguousarray(np.concatenate([cos, cos], axis=1).T)
    sinT = np.ascontiguousarray(np.concatenate([-sin, sin], axis=1).T)
    return dict(k_ident=ident, k_nm_sb=nm_sb, k_nm_mla=nm_mla, k_cos_tm=cos, k_sin_tm=sin, k_cosT=cosT, k_sinT=sinT)


def make_in_maps(cfg, n_cores, inp):
    c = cfg
    consts = _consts(c)
    f = lambda a: np.ascontiguousarray(np.asarray(a, dtype=np.float32))
    shared = dict(
        ln_gain=f(inp["ln_gain"]), final_gain=f(inp["final_gain"]).reshape(1, -1),
        mla_w_in=f(inp["mla_w_in"]), mla_q_norm=f(inp["mla_q_norm"]), mla_kv_norm=f(inp["mla_kv_norm"]),
        mla_w_q_up=f(inp["mla_w_q_up"]), mla_w_kv_up=f(inp["mla_w_kv_up"]), mla_w_o=f(inp["mla_w_o"]),
        sb_w_in=f(inp["sb_w_in"]), sb_w_o=f(inp["sb_w_o"]), **consts)
    xp, xs = np.asarray(inp["x_prompt"]), np.asarray(inp["x_sample"])
    ckv, kr = np.asarray(inp["cache_mla_ckv"]), np.asarray(inp["cache_mla_krope"])
    sk, sv = np.asarray(inp["cache_sb_k"]), np.asarray(inp["cache_sb_v"])
    maps = []
    for b in range(n_cores):
        m = dict(shared)
        m["x_in"] = f(np.concatenate([xp[b], xs[2 * b], xs[2 * b + 1]], axis=0))
        m["c_ckv"] = f(ckv[:, 2 * b:2 * b + 2])
        m["c_kr"] = f(kr[:, 2 * b:2 * b + 2])
        m["c_sbk"] = f(sk[:, 2 * b:2 * b + 2].reshape(c.N_SB, 2, c.PAST, c.HD))
        m["c_sbv"] = f(sv[:, 2 * b:2 * b + 2].reshape(c.N_SB, 2, c.PAST, c.HD))
        maps.append(m)
    return maps


def assemble(cfg, res):
    c = cfg
    T, H = c.T, c.H
    n = len(res)
    st = lambda k: np.stack([np.asarray(r[k]) for r in res], axis=0)
    y, ckv, kr, sbk, sbv = st("y"), st("ckv_o"), st("kr_o"), st("sbk_o"), st("sbv_o")

    def split_tok(a, tok_axis):
        p = np.take(a, np.arange(T), axis=tok_axis)
        s0 = np.take(a, np.arange(T, T + 64), axis=tok_axis)
        s1 = np.take(a, np.arange(T + 64, T + 128), axis=tok_axis)
        s = np.stack([s0, s1], axis=1)
        s = s.reshape((2 * n,) + s.shape[2:])
        return p, s

    yp, ys = split_tok(y, 1)
    cp, cs = split_tok(ckv, 2)
    kp, ks = split_tok(kr, 2)
    skp, sks = split_tok(sbk, 2)
    svp, svs = split_tok(sbv, 2)
    mv = lambda a: np.ascontiguousarray(np.moveaxis(a, 1, 0))
    hd = lambda a: a.reshape(a.shape[:-1] + (H, 128))
    return (np.ascontiguousarray(yp), np.ascontiguousarray(ys), mv(cp), mv(kp), hd(mv(skp)), hd(mv(svp)),
            mv(cs), mv(ks), hd(mv(sks)), hd(mv(svs)))


_NC_CACHE = {}


def kernel(**inputs):
    cfg = Cfg()
    n = 8
    if "nc" not in _NC_CACHE:
        _NC_CACHE["nc"] = build(cfg)
    nc = _NC_CACHE["nc"]
    in_maps = make_in_maps(cfg, n, inputs)
    res = run_bass_kernel_spmd(nc, in_maps, core_ids=list(range(n)))
    return assemble(cfg, res.results)
```
